# Optimizing a Trainium2 kernel written in Bass

```python
import math
import jax
import jax.numpy as jnp
from jax import lax
import numpy as np

D_MODEL = 1024
BATCH = 8
SEQ = 2048
DEPTH = 4

CTX_LEN = 256
GRID_W = 64
EPS = 1e-6
GN_EPS = 1e-5
ROPE_BASE = 10000.0

RET_DK = 32
RET_DV = 64
RET_HEADS = (D_MODEL // 4) // RET_DV
RET_CHUNK = 128

GLA_DK = 32
GLA_DV = 64
GLA_HEADS = (D_MODEL // 4) // GLA_DV
GLA_CHUNK = 32
GLA_GATE_RANK = 16
GLA_GATE_TAU = 16.0

DIFF_HD = 64
DIFF_DV = 2 * DIFF_HD
DIFF_HEADS = (D_MODEL // 2) // DIFF_DV
Q_BLOCK = 128

MIX_WIDTH = RET_HEADS * RET_DV + GLA_HEADS * GLA_DV + DIFF_HEADS * DIFF_DV
D_FF = ((8 * D_MODEL + 3 * 256 - 1) // (3 * 256)) * 256

PROJ_SIZES = (
    RET_HEADS * RET_DK, RET_HEADS * RET_DK, RET_HEADS * RET_DV, RET_HEADS * RET_DV,
    GLA_HEADS * GLA_DK, GLA_HEADS * GLA_DK, GLA_HEADS * GLA_DV, GLA_HEADS * GLA_DV,
    2 * GLA_GATE_RANK,
    DIFF_HEADS * 2 * DIFF_HD, DIFF_HEADS * 2 * DIFF_HD, DIFF_HEADS * DIFF_DV,
)
PROJ_WIDTH = sum(PROJ_SIZES)

kernel_name = 'hybrid_ret_gla_diffattn_prefix_block'


def rms_norm(x, w):
    xf = x.astype(jnp.float32)
    y = xf * lax.rsqrt(jnp.mean(xf * xf, axis=-1, keepdims=True) + EPS)
    return (y * w.astype(jnp.float32)).astype(x.dtype)


def group_norm_heads(x):
    xf = x.astype(jnp.float32)
    mu = jnp.mean(xf, axis=-1, keepdims=True)
    var = jnp.mean(jnp.square(xf - mu), axis=-1, keepdims=True)
    return (xf - mu) * lax.rsqrt(var + GN_EPS)


def modulate(h, shift, scale):
    return h * (1.0 + scale) + shift


def angle_tables(pos, inv_freq):
    ang = pos.astype(jnp.float32)[:, None] * inv_freq[None, :]
    return jnp.cos(ang), jnp.sin(ang)


def axial_freqs(dim):
    half = dim // 2
    return 1.0 / (ROPE_BASE ** (jnp.arange(half, dtype=jnp.float32) / half))


def retnet_freqs(dim):
    return 1.0 / (ROPE_BASE ** jnp.linspace(0.0, 1.0, dim // 2, dtype=jnp.float32))


def apply_rope(x, cos, sin):
    x1, x2 = jnp.split(x, 2, axis=-1)
    return jnp.concatenate([x1 * cos - x2 * sin, x1 * sin + x2 * cos], axis=-1).astype(x.dtype)


def apply_axial_rope(x, tabs):
    cos_r, sin_r, cos_c, sin_c = tabs
    x_row, x_col = jnp.split(x, 2, axis=-1)
    return jnp.concatenate([apply_rope(x_row, cos_r, sin_r), apply_rope(x_col, cos_c, sin_c)], axis=-1)


def to_heads(t, n_heads):
    b, n, _ = t.shape
    return t.reshape(b, n, n_heads, -1).transpose(0, 2, 1, 3)


def from_heads(t):
    b, h, n, d = t.shape
    return t.transpose(0, 2, 1, 3).reshape(b, n, h * d)


def flip_seq(t):
    return jnp.flip(t, axis=2)


def split_proj(p):
    idx = [int(s) for s in np.cumsum(PROJ_SIZES)[:-1]]
    return jnp.split(p, idx, axis=-1)


def chunk_state_scan(decay, u, s0):
    def step(s, inp):
        d, du = inp
        return d[..., None] * s + du, s
    s_last, s_prev = lax.scan(step, s0, (jnp.moveaxis(decay, 2, 0), jnp.moveaxis(u, 2, 0)))
    return jnp.moveaxis(s_prev, 0, 2), s_last


def retention_chunked(q, k, v, log_gamma, s0):
    b, h, n_tok, dk = q.shape
    dv = v.shape[-1]
    n = n_tok // RET_CHUNK
    qc = q.astype(jnp.float32).reshape(b, h, n, RET_CHUNK, dk)
    kc = k.astype(jnp.float32).reshape(b, h, n, RET_CHUNK, dk)
    vc = v.astype(jnp.float32).reshape(b, h, n, RET_CHUNK, dv)
    i = jnp.arange(RET_CHUNK, dtype=jnp.float32)
    rel = i[:, None] - i[None, :]
    dmat = jnp.exp(jnp.where(rel >= 0, rel * log_gamma[:, None, None], -jnp.inf))
    scores = jnp.einsum('bhnid,bhnjd->bhnij', qc, kc) * dmat[:, None]
    intra = jnp.einsum('bhnij,bhnje->bhnie', scores, vc)
    q_dec = jnp.exp((i + 1.0)[None, :] * log_gamma[:, None])
    k_dec = jnp.exp((RET_CHUNK - 1.0 - i)[None, :] * log_gamma[:, None])
    u = jnp.einsum('bhnjd,bhnje->bhnde', kc * k_dec[None, :, None, :, None], vc)
    chunk_dec = jnp.broadcast_to(jnp.exp(RET_CHUNK * log_gamma)[None, :, None, None], (b, h, n, dk))
    s_prev, s_last = chunk_state_scan(chunk_dec, u, s0)
    inter = jnp.einsum('bhnid,bhnde->bhnie', qc * q_dec[None, :, None, :, None], s_prev)
    return (intra + inter).reshape(b, h, n_tok, dv), s_last


def gla_chunked(q, k, v, g, s0):
    b, h, n_tok, dk = q.shape
    dv = v.shape[-1]
    n = n_tok // GLA_CHUNK
    qc = q.astype(jnp.float32).reshape(b, h, n, GLA_CHUNK, dk)
    kc = k.astype(jnp.float32).reshape(b, h, n, GLA_CHUNK, dk)
    vc = v.astype(jnp.float32).reshape(b, h, n, GLA_CHUNK, dv)
    cum = lax.cumsum(g.astype(jnp.float32).reshape(b, h, n, GLA_CHUNK, dk), axis=3)
    i = jnp.arange(GLA_CHUNK)
    lower = (i[:, None] >= i[None, :])[..., None]
    pair = jnp.exp(jnp.where(lower, cum[:, :, :, :, None, :] - cum[:, :, :, None, :, :], -jnp.inf))
    attn = jnp.einsum('bhnid,bhnjd,bhnijd->bhnij', qc, kc, pair)
    intra = jnp.einsum('bhnij,bhnje->bhnie', attn, vc)
    cum_last = cum[:, :, :, -1:, :]
    u = jnp.einsum('bhnjd,bhnje->bhnde', kc * jnp.exp(cum_last - cum), vc)
    s_prev, s_last = chunk_state_scan(jnp.exp(cum_last[:, :, :, 0]), u, s0)
    inter = jnp.einsum('bhnid,bhnde->bhnie', qc * jnp.exp(cum), s_prev)
    return (intra + inter).reshape(b, h, n_tok, dv), s_last


def retention_bidir(q, k, v, log_gamma, s0_f, s0_b):
    o_f, s_f = retention_chunked(q, k, v, log_gamma[0], s0_f)
    o_b, s_b = retention_chunked(flip_seq(q), flip_seq(k), flip_seq(v), log_gamma[1], s0_b)
    return o_f + flip_seq(o_b), s_f, s_b


def gla_bidir(q, k, v, g_f, g_b, s0_f, s0_b):
    o_f, s_f = gla_chunked(q, k, v, g_f, s0_f)
    o_b, s_b = gla_chunked(flip_seq(q), flip_seq(k), flip_seq(v), flip_seq(g_b), s0_b)
    return o_f + flip_seq(o_b), s_f, s_b


def gla_gates(lr, w_gate, b_gate):
    lr_f, lr_b = jnp.split(lr, 2, axis=-1)
    z_f = (lr_f @ w_gate[0] + b_gate[0]).astype(jnp.float32)
    z_b = (lr_b @ w_gate[1] + b_gate[1]).astype(jnp.float32)
    g_f = to_heads(jax.nn.log_sigmoid(z_f) / GLA_GATE_TAU, GLA_HEADS)
    g_b = to_heads(jax.nn.log_sigmoid(z_b) / GLA_GATE_TAU, GLA_HEADS)
    return g_f, g_b


def diff_qk_heads(t):
    b, n, _ = t.shape
    return t.reshape(b, n, DIFF_HEADS, 2, DIFF_HD).transpose(0, 2, 3, 1, 4)


def diff_attend(q, k, v, lam):
    s = jnp.einsum('bhmqd,bhmkd->bhmqk', q.astype(jnp.float32), k.astype(jnp.float32)) * (DIFF_HD ** -0.5)
    p = jax.nn.softmax(s, axis=-1)
    a = p[:, :, 0] - lam * p[:, :, 1]
    return jnp.einsum('bhqk,bhkd->bhqd', a, v.astype(jnp.float32))


def token_mixers(p_lat, p_ctx, ret_rot, axial_rot, ret_decay_logit, gla_w_gate, gla_b_gate,
                 gla_norm_w, diff_lambda, diff_norm_w, lam_init, ctx_out):
    rq, rk, rv, rg, gq, gk, gv, gr, glr, dq, dk, dv = split_proj(p_lat)
    rq_c, rk_c, rv_c, rg_c, gq_c, gk_c, gv_c, gr_c, glr_c, dq_c, dk_c, dv_c = split_proj(p_ctx)
    bsz = p_lat.shape[0]

    log_gamma = jax.nn.log_sigmoid(ret_decay_logit.astype(jnp.float32))
    k_scale = RET_DK ** -0.5
    zero_r = jnp.zeros((bsz, RET_HEADS, RET_DK, RET_DV), jnp.float32)
    o_rc, s_rf, s_rb = retention_bidir(to_heads(rq_c, RET_HEADS), to_heads(rk_c, RET_HEADS) * k_scale,
                                       to_heads(rv_c, RET_HEADS), log_gamma, zero_r, zero_r)
    cos_t, sin_t = ret_rot
    o_r, _, _ = retention_bidir(apply_rope(to_heads(rq, RET_HEADS), cos_t, sin_t),
                                apply_rope(to_heads(rk, RET_HEADS), cos_t, sin_t) * k_scale,
                                to_heads(rv, RET_HEADS), log_gamma, s_rf, s_rb)
    ret_out = jax.nn.silu(rg) * from_heads(group_norm_heads(o_r))

    q_scale = GLA_DK ** -0.5
    zero_g = jnp.zeros((bsz, GLA_HEADS, GLA_DK, GLA_DV), jnp.float32)
    gf_c, gb_c = gla_gates(glr_c, gla_w_gate, gla_b_gate)
    o_gc, s_gf, s_gb = gla_bidir(to_heads(gq_c, GLA_HEADS) * q_scale, to_heads(gk_c, GLA_HEADS),
                                 to_heads(gv_c, GLA_HEADS), gf_c, gb_c, zero_g, zero_g)
    g_f, g_b = gla_gates(glr, gla_w_gate, gla_b_gate)
    o_g, _, _ = gla_bidir(to_heads(gq, GLA_HEADS) * q_scale, to_heads(gk, GLA_HEADS),
                          to_heads(gv, GLA_HEADS), g_f, g_b, s_gf, s_gb)
    gla_out = jax.nn.silu(gr) * from_heads(rms_norm(o_g, gla_norm_w))

    lam = (jnp.exp(jnp.sum(diff_lambda[0] * diff_lambda[1])) - jnp.exp(jnp.sum(diff_lambda[2] * diff_lambda[3]))
           + lam_init).astype(jnp.float32)
    d_q = apply_axial_rope(diff_qk_heads(dq), axial_rot)
    d_k = apply_axial_rope(diff_qk_heads(dk), axial_rot)
    d_k_c = diff_qk_heads(dk_c)
    d_v_c = to_heads(dv_c, DIFF_HEADS)
    k_all = jnp.concatenate([d_k, d_k_c], axis=3)
    v_all = jnp.concatenate([to_heads(dv, DIFF_HEADS), d_v_c], axis=2)
    n_tok = d_q.shape[3]
    q_blocks = jnp.moveaxis(d_q.reshape(bsz, DIFF_HEADS, 2, n_tok // Q_BLOCK, Q_BLOCK, DIFF_HD), 3, 0)
    o_blocks = lax.map(lambda qb: diff_attend(qb, k_all, v_all, lam), q_blocks)
    o_d = jnp.moveaxis(o_blocks, 0, 2).reshape(bsz, DIFF_HEADS, n_tok, DIFF_DV)
    diff_out = from_heads(rms_norm(o_d, diff_norm_w) * (1.0 - lam_init))

    o_lat = jnp.concatenate([ret_out, gla_out, diff_out], axis=-1)
    if not ctx_out:
        return o_lat, None
    o_dc = diff_attend(diff_qk_heads(dq_c), d_k_c, d_v_c, lam)
    o_ctx = jnp.concatenate([
        jax.nn.silu(rg_c) * from_heads(group_norm_heads(o_rc)),
        jax.nn.silu(gr_c) * from_heads(rms_norm(o_gc, gla_norm_w)),
        from_heads(rms_norm(o_dc, diff_norm_w) * (1.0 - lam_init)),
    ], axis=-1)
    return o_lat, o_ctx


def swiglu(h, w_in, w_out):
    gate, up = jnp.split(h @ w_in, 2, axis=-1)
    return (jax.nn.silu(gate) * up) @ w_out


def setup_inputs(seed: int = 0) -> dict:
    key = jax.random.key(seed)
    ks = jax.random.split(key, 20)

    def nrm(k, shape, scale):
        return jax.random.normal(k, shape, jnp.float32) * scale

    ret_logit0 = np.log(2.0 ** (5.0 + np.arange(RET_HEADS)) - 1.0).astype(np.float32)
    return {
        'x': nrm(ks[0], (BATCH, SEQ, D_MODEL), 1.0),
        'c': nrm(ks[1], (BATCH, D_MODEL), 1.0),
        'ctx': nrm(ks[2], (BATCH, CTX_LEN, D_MODEL), 1.0),
        'c_ctx': nrm(ks[3], (D_MODEL,), 1.0),
        'w_ada': nrm(ks[4], (DEPTH, D_MODEL, 6 * D_MODEL), 0.5 * D_MODEL ** -0.5),
        'b_ada': nrm(ks[5], (DEPTH, 6 * D_MODEL), 0.02),
        'norm1_w': 1.0 + nrm(ks[6], (DEPTH, D_MODEL), 0.02),
        'w_in': nrm(ks[7], (DEPTH, D_MODEL, PROJ_WIDTH), D_MODEL ** -0.5),
        'ret_decay_logit': jnp.asarray(ret_logit0) + nrm(ks[8], (DEPTH, 2, RET_HEADS), 0.05),
        'gla_w_gate': nrm(ks[9], (DEPTH, 2, GLA_GATE_RANK, GLA_HEADS * GLA_DK), GLA_GATE_RANK ** -0.5),
        'gla_b_gate': nrm(ks[10], (DEPTH, 2, GLA_HEADS * GLA_DK), 0.1),
        'gla_norm_w': 1.0 + nrm(ks[11], (DEPTH, GLA_DV), 0.02),
        'diff_lambda': nrm(ks[12], (DEPTH, 4, DIFF_HD), 0.1),
        'diff_norm_w': 1.0 + nrm(ks[13], (DEPTH, DIFF_DV), 0.02),
        'w_out': nrm(ks[14], (DEPTH, MIX_WIDTH, D_MODEL), MIX_WIDTH ** -0.5),
        'norm2_w': 1.0 + nrm(ks[15], (DEPTH, D_MODEL), 0.02),
        'w_ffn_in': nrm(ks[16], (DEPTH, D_MODEL, 2 * D_FF), D_MODEL ** -0.5),
        'w_ffn_out': nrm(ks[17], (DEPTH, D_FF, D_MODEL), D_FF ** -0.5),
        'final_norm_w': 1.0 + nrm(ks[18], (D_MODEL,), 0.02),
    }


def reference(x, c, ctx, c_ctx, w_ada, b_ada, norm1_w, w_in, ret_decay_logit, gla_w_gate,
              gla_b_gate, gla_norm_w, diff_lambda, diff_norm_w, w_out, norm2_w, w_ffn_in,
              w_ffn_out, final_norm_w):
    n_lat = x.shape[1]
    rows = n_lat // GRID_W
    row_idx, col_idx = jnp.meshgrid(jnp.arange(rows), jnp.arange(GRID_W), indexing='ij')
    axial_rot = (*angle_tables(row_idx.reshape(-1), axial_freqs(DIFF_HD // 2)),
                 *angle_tables(col_idx.reshape(-1), axial_freqs(DIFF_HD // 2)))
    ret_rot = angle_tables(jnp.arange(n_lat), retnet_freqs(RET_DK))
    cond_lat = jax.nn.silu(c)[:, None, :]
    cond_ctx = jax.nn.silu(c_ctx)[None, None, :]

    for layer in range(DEPTH):
        last = layer == DEPTH - 1
        lam_init = 0.8 - 0.6 * math.exp(-0.3 * layer)
        m = jnp.split(cond_lat @ w_ada[layer] + b_ada[layer], 6, axis=-1)
        mc = jnp.split(cond_ctx @ w_ada[layer] + b_ada[layer], 6, axis=-1)

        h = modulate(rms_norm(x, norm1_w[layer]), m[0], m[1])
        h_c = modulate(rms_norm(ctx, norm1_w[layer]), mc[0], mc[1])
        o, o_c = token_mixers(h @ w_in[layer], h_c @ w_in[layer], ret_rot, axial_rot,
                              ret_decay_logit[layer], gla_w_gate[layer], gla_b_gate[layer],
                              gla_norm_w[layer], diff_lambda[layer], diff_norm_w[layer],
                              lam_init, not last)
        x = x + m[2] * (o @ w_out[layer])
        x = x + m[5] * swiglu(modulate(rms_norm(x, norm2_w[layer]), m[3], m[4]),
                              w_ffn_in[layer], w_ffn_out[layer])
        if not last:
            ctx = ctx + mc[2] * (o_c @ w_out[layer])
            ctx = ctx + mc[5] * swiglu(modulate(rms_norm(ctx, norm2_w[layer]), mc[3], mc[4]),
                                       w_ffn_in[layer], w_ffn_out[layer])

    return rms_norm(x, final_norm_w)
```

```python
import math
import numpy as np
from contextlib import ExitStack
import concourse.bass as bass
import concourse.mybir as mybir
from concourse.bass_utils import run_bass_kernel_spmd

F32 = mybir.dt.float32
BF16 = mybir.dt.bfloat16
AF = mybir.ActivationFunctionType
ALU = mybir.AluOpType
AX = mybir.AxisListType

D = 1024
NLAT = 2048
NCTX = 256
T = NLAT + NCTX
NT = T // 128
DEPTH = 4
DFF = 2816
NF = DFF // 128
PROJW = 3104
EPS = 1e-6
GN_EPS = 1e-5
BTS = [(0, 512), (512, 512), (1024, 512), (1536, 512), (2048, 256)]
UNIT = 2048
FGROUPS = [(0, 12), (12, 10)]

ENGS = ("pe", "act", "dve", "pool", "sp")


class Op:
    __slots__ = ("eng", "fn", "deps", "sig", "sem", "sigval", "idx", "dma", "dmak")

    def __init__(self, eng, fn, dma, idx):
        self.eng = eng
        self.fn = fn
        self.dma = dma
        self.idx = idx
        self.deps = {}
        self.sig = False
        self.sem = None
        self.sigval = 0
        self.dmak = 0


class Sched:
    def __init__(self):
        self.ops = []
        self.st = {}
        self.ndma = 0

    def newbuf(self, name, nslots=1, inherit=()):
        assert name not in self.st, name
        inh = frozenset(inherit)
        self.st[name] = [[None, {}, inh] for _ in range(nslots)]

    def final_ops(self, name):
        out = set()
        for w, rd, inh in self.st[name]:
            if w is not None:
                out.add(w)
            out.update(rd.values())
            out.update(inh)
        return out

    def delbuf(self, name):
        del self.st[name]

    def add(self, eng, fn, reads=(), writes=(), dma=False):
        op = Op(eng, fn, dma, len(self.ops))
        deps = op.deps
        for (name, lo, hi) in reads:
            st = self.st[name]
            assert 0 <= lo < hi <= len(st), (name, lo, hi, len(st))
            for s in range(lo, hi):
                w = st[s][0]
                if w is not None:
                    deps[w] = "raw"
                for o in st[s][2]:
                    deps.setdefault(o, "raw")
        for (name, lo, hi) in writes:
            st = self.st[name]
            assert 0 <= lo < hi <= len(st), (name, lo, hi, len(st))
            for s in range(lo, hi):
                w = st[s][0]
                if w is not None and w is not op:
                    deps.setdefault(w, "waw")
                for r in st[s][1].values():
                    if r is not op:
                        deps.setdefault(r, "war")
                for o in st[s][2]:
                    deps.setdefault(o, "war")
        rk = ("d", op.idx) if dma else eng
        for (name, lo, hi) in reads:
            st = self.st[name]
            for s in range(lo, hi):
                st[s][1][rk] = op
        for (name, lo, hi) in writes:
            st = self.st[name]
            for s in range(lo, hi):
                st[s][0] = op
                st[s][1] = {}
                st[s][2] = frozenset()
        for o in list(deps):
            if (not o.dma) and (not dma) and o.eng == eng and deps[o] != "raw":
                del deps[o]
        for o in deps:
            o.sig = True
        if dma:
            op.sig = True
        self.ops.append(op)
        return op

    def emit(self, nc, es):
        NDS = 16
        sems = {e: es.enter_context(nc.semaphore("s_" + e)) for e in ENGS}
        dsems = [es.enter_context(nc.semaphore("d%d" % i)) for i in range(NDS)]
        cnt = {e: 0 for e in ENGS}
        dcnt = [0] * NDS
        k = 0
        for op in self.ops:
            if not op.sig:
                continue
            if op.dma:
                j = k % NDS
                k += 1
                dcnt[j] += 16
                op.sem = dsems[j]
                op.sigval = dcnt[j]
                op.dmak = dcnt[j] - 16
            else:
                cnt[op.eng] += 1
                op.sem = sems[op.eng]
                op.sigval = cnt[op.eng]
        per = {e: [] for e in ENGS}
        for op in self.ops:
            per[op.eng].append(op)
        block = es.enter_context(nc.Block())

        def run(e, ops):
            waited = {}
            for op in ops:
                need = {}
                for o in op.deps:
                    key = id(o.sem)
                    if key not in need or need[key][1] < o.sigval:
                        need[key] = (o.sem, o.sigval)
                for key, (sem, val) in need.items():
                    if waited.get(key, 0) < val:
                        e.wait_ge(sem, val)
                        waited[key] = val
                if op.fn is None:
                    continue
                if op.dma and op.dmak > 0 and waited.get(id(op.sem), 0) < op.dmak:
                    e.wait_ge(op.sem, op.dmak)
                    waited[id(op.sem)] = op.dmak
                inst = op.fn(e)
                if op.sig:
                    inst.then_inc(op.sem, 16 if op.dma else 1)

        block.tensor(lambda e: run(e, per["pe"]))
        block.scalar(lambda e: run(e, per["act"]))
        block.vector(lambda e: run(e, per["dve"]))
        block.gpsimd(lambda e: run(e, per["pool"]))
        def run_sp(e):
            run(e, per["sp"])
            for j in range(NDS):
                if dcnt[j]:
                    e.wait_ge(dsems[j], dcnt[j])
        block.sync(run_sp)


class Arena:
    def __init__(self, sb, sched, nwords):
        self.sb = sb
        self.S = sched
        self.nwords = nwords
        self.top = 0
        self.stack = []
        self.retired = []

    def alloc(self, name, dtype, fshape, nslots=1):
        nel = 1
        for d in fshape:
            nel *= d
        nbytes = nel * (2 if dtype == BF16 else 4)
        nw = (nbytes + 3) // 4
        nw = (nw + 15) // 16 * 16
        off = self.top
        assert off + nw <= self.nwords, ("SBUF arena overflow", name, off, nw, self.nwords)
        self.top += nw
        inherit = set()
        for (lo, hi, ops) in self.retired:
            if lo < off + nw and hi > off:
                inherit |= ops
        self.S.newbuf(name, nslots, inherit)
        self.stack.append((name, off, nw))
        ap = self.sb[:, off:off + nw]
        if dtype == BF16:
            ap = ap.bitcast(BF16)
        ap = ap[:, 0:nel]
        if len(fshape) == 2:
            ap = ap.rearrange("p (a b) -> p a b", b=fshape[1])
        elif len(fshape) == 3:
            ap = ap.rearrange("p (a b c) -> p a b c", b=fshape[1], c=fshape[2])
        elif len(fshape) == 4:
            ap = ap.rearrange("p (a b c d) -> p a b c d", b=fshape[1], c=fshape[2], d=fshape[3])
        return ap

    def mark(self):
        return len(self.stack)

    def release(self, mark):
        while len(self.stack) > mark:
            name, off, nw = self.stack.pop()
            self.retired.append((off, off + nw, self.S.final_ops(name)))
            self.S.delbuf(name)
            self.top = off


def _swap_cols(base, nheads_blocks, blk):
    idx = np.arange(nheads_blocks * blk)
    b = idx // blk
    d = idx % blk
    return base + b * blk + (d + blk // 2) % blk


def _proj_units():
    u = []
    rq = np.arange(0, 128)
    rk = np.arange(128, 256)
    u.append(np.concatenate([rq, rk]))
    u.append(np.concatenate([_swap_cols(0, 4, 32), _swap_cols(128, 4, 32)]))
    u.append(np.arange(256, 512))
    u.append(np.arange(512, 768))
    u.append(np.arange(768, 1024))
    g1 = -np.ones(256, np.int64)
    g1[:32] = np.arange(1536, 1568)
    u.append(g1)
    u.append(np.arange(1024, 1280))
    u.append(np.arange(1280, 1536))
    u.append(np.arange(2080, 2336))
    u.append(np.arange(2336, 2592))
    sw = _swap_cols(2080, 16, 32)
    u.append(sw[:256])
    u.append(sw[256:])
    u.append(np.arange(1568, 1824))
    u.append(np.arange(1824, 2080))
    sw = _swap_cols(1568, 16, 32)
    u.append(sw[:256])
    u.append(sw[256:])
    u.append(np.arange(2592, 2848))
    u.append(np.arange(2848, 3104))
    return u


NPU = 18
NUL = NPU + 4 + 22 + 11


def _pack_weights(w_in, w_out, w_ffn_in, w_ffn_out):
    L = w_in.shape[0]
    out = np.zeros((L, 128, NUL, UNIT), np.float32)
    units = _proj_units()
    for l in range(L):
        wi = w_in[l].reshape(8, 128, PROJW)
        wo = w_out[l].reshape(8, 128, D)
        fi = w_ffn_in[l].reshape(8, 128, 2 * DFF)
        fo = w_ffn_out[l].reshape(NF, 128, D)
        k = 0
        def wout_units(part, k):
            for j in range(2):
                blk = wo[4 * part:4 * part + 4, :, 512 * j:512 * j + 512].transpose(1, 0, 2)
                out[l, :, k, :] = blk.reshape(128, UNIT)
                k += 1
            return k
        for ui, cols in enumerate(units):
            if ui == 8:
                k = wout_units(0, k)
            blk = np.zeros((128, 8, 256), np.float32)
            ok = cols >= 0
            blk[:, :, ok] = wi[:, :, cols[ok]].transpose(1, 0, 2)
            out[l, :, k, :] = blk.reshape(128, UNIT)
            k += 1
        k = wout_units(1, k)
        for (f0, nf) in FGROUPS:
            for f in range(f0, f0 + nf):
                blk = np.concatenate([fi[:, :, 128 * f:128 * f + 128],
                                      fi[:, :, DFF + 128 * f:DFF + 128 * f + 128]], axis=2)
                out[l, :, k, :] = blk.transpose(1, 0, 2).reshape(128, UNIT)
                k += 1
            for f in range(f0, f0 + nf, 2):
                blk = fo[f:f + 2].transpose(1, 0, 2)
                out[l, :, k, :] = blk.reshape(128, UNIT)
                k += 1
        assert k == NUL
    return out


C_ID = 0
C_SDM = 128
C_LT = 384
C_UT = 512
C_RPOS = 640
C_RNEG = 768
C_IP1 = 896
C_IREV = 1024
C_JC = 1152
C_ONE = 1168
NCS = 1296
C_BM4 = 1296
NCONST = 1808


def _const_pack():
    c = np.zeros((128, NCONST), np.float32)
    p = np.arange(128)
    c[:, C_ID:C_ID + 128] = np.eye(128)
    bm = np.zeros((128, 4, 128), np.float32)
    for h in range(4):
        bm[32 * h:32 * h + 32, h, :] = 1.0
    c[:, C_BM4:C_BM4 + 512] = bm.reshape(128, 512)
    sd = np.zeros((128, 4, 64), np.float32)
    for h in range(4):
        sd[32 * h:32 * h + 32, h, :] = 1.0
    c[:, C_SDM:C_SDM + 256] = sd.reshape(128, 256)
    j = p[:, None]
    i = p[None, :]
    c[:, C_LT:C_LT + 128] = (j <= i)
    c[:, C_UT:C_UT + 128] = (j >= i)
    c[:, C_RPOS:C_RPOS + 128] = np.maximum(i - j, 0)
    c[:, C_RNEG:C_RNEG + 128] = np.maximum(j - i, 0)
    c[:, C_IP1:C_IP1 + 128] = np.broadcast_to(i + 1, (128, 128))
    c[:, C_IREV:C_IREV + 128] = np.broadcast_to(128 - i, (128, 128))
    c[:, C_JC] = 127 - p
    c[:, C_JC + 1] = p
    c[:, C_ONE:C_ONE + 128] = 1.0
    return c


def _rope_tables():
    t = np.arange(NLAT, dtype=np.float32)
    p = np.arange(128)
    tab = np.zeros((2, 128, 2, NLAT), np.float32)
    invf = (1.0 / (np.float32(10000.0) ** np.linspace(0.0, 1.0, 16, dtype=np.float32))).astype(np.float32)
    d = p % 32
    ang = t[None, :] * invf[d % 16][:, None]
    tab[0, :, 0] = np.cos(ang)
    tab[0, :, 1] = np.sin(ang) * np.where(d < 16, -1.0, 1.0)[:, None]
    half = 16
    invf2 = (1.0 / (np.float32(10000.0) ** (np.arange(half, dtype=np.float32) / half))).astype(np.float32)
    e = p % 64
    part = e // 32
    d = e % 32
    row = (np.arange(NLAT) // 64).astype(np.float32)
    col = (np.arange(NLAT) % 64).astype(np.float32)
    pos = np.where(part[:, None] == 0, row[None, :], col[None, :]).astype(np.float32)
    ang = pos * invf2[d % 16][:, None]
    tab[1, :, 0] = np.cos(ang)
    tab[1, :, 1] = np.sin(ang) * np.where(d < 16, -1.0, 1.0)[:, None]
    tab = tab.reshape(2, 128, 2, 4, 512).transpose(0, 1, 3, 2, 4)
    return np.ascontiguousarray(tab)


P_N1 = 0
P_N2 = 8
P_BADA = 16
P_RDLP = 64
P_RDLB = 66
P_BG = 74
P_WG = 76
P_GNW = 332
P_DLAM = 588
P_DNW = 844
P_DNWC = 1360
NLP = 1364


def _layer_params(inp):
    L = DEPTH
    o = np.zeros((L, 128, NLP), np.float32)
    p = np.arange(128)
    for l in range(L):
        o[l, :, P_N1:P_N1 + 8] = inp["norm1_w"][l].reshape(8, 128).T
        o[l, :, P_N2:P_N2 + 8] = inp["norm2_w"][l].reshape(8, 128).T
        o[l, :, P_BADA:P_BADA + 48] = inp["b_ada"][l].reshape(48, 128).T
        o[l, :, P_RDLP:P_RDLP + 2] = inp["ret_decay_logit"][l][:, p // 32].T
        o[l, :, P_RDLB:P_RDLB + 8] = inp["ret_decay_logit"][l].reshape(1, 8)
        o[l, :, P_BG:P_BG + 2] = inp["gla_b_gate"][l].T
        o[l, 0:16, P_WG:P_WG + 128] = inp["gla_w_gate"][l, 0]
        o[l, 16:32, P_WG + 128:P_WG + 256] = inp["gla_w_gate"][l, 1]
        o[l, :, P_GNW:P_GNW + 256] = np.tile(inp["gla_norm_w"][l], 4)[None, :]
        o[l, :, P_DLAM:P_DLAM + 256] = inp["diff_lambda"][l].reshape(1, 256)
        o[l, :, P_DNW:P_DNW + 512] = np.tile(inp["diff_norm_w"][l], 4)[None, :]
        o[l, :, P_DNWC] = inp["diff_norm_w"][l]
    return o


def build(depth=DEPTH, taps=()):
    nc = bass.Bass("TRN2", target_bir_lowering=False)
    dt = lambda n, s, kind="ExternalInput": nc.dram_tensor(n, list(s), F32, kind=kind).ap()
    d_x = dt("xin", [D, T])
    d_c = dt("cin", [128, 8, 2])
    d_wada = dt("wada", [DEPTH, D, 6 * D])
    d_wpk = dt("wpk", [DEPTH, 128, NUL, UNIT])
    d_lp = dt("lp", [DEPTH, 128, NLP])
    d_fn = dt("fnw", [128, 8])
    d_const = dt("cst", [128, NCONST])
    d_rope = dt("rope", [2, 128, 4, 2, 512])
    d_out = dt("out", [D, NLAT], kind="ExternalOutput")
    d_xd = dt("xd", [D, T], kind="Internal")
    d_ot = nc.dram_tensor("otrg", [128, 4, T], BF16, kind="Internal").ap()
    d_taps = {}
    for (name, shape, tdt) in taps:
        d_taps[name] = nc.dram_tensor("tap_" + name, list(shape), tdt, kind="ExternalOutput").ap()
    xd_v = d_xd.rearrange("(c p) t -> p c t", p=128)

    S = Sched()
    es = ExitStack()
    NW = 52800
    sb = es.enter_context(nc.sbuf_tensor("sb", [128, NW], F32))
    A = Arena(sb, S, NW)
    psb = []
    PSLOTS = {}
    for i in range(8):
        psb.append(es.enter_context(nc.psum_tensor("ps%d" % i, [128, 512], F32)))
        S.newbuf("ps%d" % i, PSLOTS.get("ps%d" % i, 1))
    S.newbuf("dram_out", 1)
    S.newbuf("dram_in", 1)
    S.newbuf("xd", 8 * NT)
    S.newbuf("otd", 1)

    class Pool_:
        def __init__(self, banks):
            self.banks = banks
            self.i = 0

        def next(self):
            b = self.banks[self.i % len(self.banks)]
            self.i += 1
            return psb[b], "ps%d" % b

    def W1(name, s=0, n=1):
        if name in PSLOTS and s == 0 and n == 1:
            return (name, 0, PSLOTS[name])
        return (name, s, s + n)

    hT = A.alloc("hT", BF16, [8, T], nslots=8 * NT)
    cst = A.alloc("cst", F32, [NCS])
    ident = A.alloc("ident", BF16, [128])
    ones_bf = A.alloc("ones", BF16, [128])
    bm4 = A.alloc("bm4", BF16, [4, 128])
    modv = A.alloc("modv", F32, [DEPTH, 48, 2])
    lpar = A.alloc("lpar", F32, [NLP])
    drv = A.alloc("drv", F32, [2, 2, 8])
    fnw = A.alloc("fnw", F32, [8])
    NST, NBF = 2, 5
    wst = A.alloc("wst", F32, [NST, UNIT], nslots=NST)
    wbf = A.alloc("wbf", BF16, [NBF, UNIT], nslots=NBF)

    def xs(c, t0, w):
        return (c * NT + t0 // 128, c * NT + (t0 + w + 127) // 128)

    def xsl(name, c, t0, w):
        lo, hi = xs(c, t0, w)
        return (name, lo, hi)

    def xall(name, t0, w):
        return [xsl(name, c, t0, w) for c in range(8)]

    class WStream:
        def __init__(self, total):
            self.total = total
            self.nd = 0
            self.ncst = 0
            self.ng = 0

        def _dma(self, j):
            l, u = divmod(j, NUL)
            s = j % NST
            S.add("sp", lambda e, l=l, u=u, s=s: e.dma_start(out=wst[:, s, :], in_=d_wpk[l, :, u, :]),
                  reads=[W1("dram_in")], writes=[W1("wst", s)], dma=True)

        def _cast(self, j):
            s = j % NST
            b = j % NBF
            S.add("pool", lambda e, s=s, b=b: e.tensor_copy(out=wbf[:, b, :], in_=wst[:, s, :]),
                  reads=[W1("wst", s)], writes=[W1("wbf", b)])

        def get(self, n=1):
            i = self.ng
            self.ng += n
            assert n <= NBF and self.ng <= self.total
            last_cast = min(i + NBF - 1, self.total - 1)
            while self.ncst <= last_cast:
                while self.nd <= min(self.ncst + NST - 1, self.total - 1):
                    self._dma(self.nd)
                    self.nd += 1
                self._cast(self.ncst)
                self.ncst += 1
            return [(wbf[:, (i + k) % NBF, :], (i + k) % NBF) for k in range(n)]

    WS = WStream(depth * NUL)

    S.add("sp", lambda e: e.dma_start(out=cst[:, :], in_=d_const[:, 0:NCS]), reads=[W1("dram_in")], writes=[W1("cst")], dma=True)
    S.add("sp", lambda e: e.dma_start(out=fnw[:, :], in_=d_fn[:, :]), reads=[W1("dram_in")], writes=[W1("fnw")], dma=True)
    for c in range(8):
        S.add("sp", lambda e, c=c: e.dma_start(out=d_xd[128 * c:128 * c + 128, :], in_=d_x[128 * c:128 * c + 128, :]),
              reads=[W1("dram_in")], writes=[("xd", c * NT, (c + 1) * NT)], dma=True)
    S.add("dve", lambda e: e.tensor_copy(out=ident[:, :], in_=cst[:, C_ID:C_ID + 128]), reads=[W1("cst")], writes=[W1("ident")])
    S.add("dve", lambda e: e.memset(ones_bf[:, :], 1.0), writes=[W1("ones")])
    mkb = A.mark()
    bmf = A.alloc("bmf", F32, [512])
    S.add("sp", lambda e: e.dma_start(out=bmf[:, :], in_=d_const[:, C_BM4:C_BM4 + 512]), reads=[W1("dram_in")], writes=[W1("bmf")], dma=True)
    S.add("dve", lambda e: e.tensor_copy(out=bm4[:, :, :], in_=bmf[:, :].rearrange("p (a b) -> p a b", b=128)),
          reads=[W1("bmf")], writes=[W1("bm4")])
    A.release(mkb)

    mk = A.mark()
    condT = A.alloc("condT", F32, [8, 2])
    ast = A.alloc("ast", F32, [2, 8, 512], nslots=2)
    mrow = A.alloc("mrow", F32, [6 * D])
    S.add("sp", lambda e: e.dma_start(out=condT[:, :, :], in_=d_c[:, :, :]), reads=[W1("dram_in")], writes=[W1("condT")], dma=True)
    S.add("act", lambda e: e.activation(out=condT[:, :, :], in_=condT[:, :, :], func=AF.Silu), reads=[W1("condT")], writes=[W1("condT")])
    mps, mpsn = psb[7], "ps7"
    PR = Pool_([0, 1, 2, 3])
    nslab = 0
    for l in range(depth):
        wv = d_wada[l].rearrange("(kc p) n -> p kc n", p=128)
        for s in range(12):
            r = nslab % 2
            nslab += 1
            S.add("sp", lambda e, r=r, s=s, wv=wv: e.dma_start(out=ast[:, r, :, :], in_=wv[:, :, 512 * s:512 * s + 512]),
                  reads=[W1("dram_in")], writes=[W1("ast", r)], dma=True)
            ps, psn = PR.next()

            def f(e, r=r, ps=ps):
                for kc in range(8):
                    ins = e.matmul(ps[0:2, :], lhsT=condT[:, kc, :], rhs=ast[:, r, kc, :], start=(kc == 0), stop=(kc == 7))
                return ins
            S.add("pe", f, reads=[W1("ast", r), W1("condT")], writes=[W1(psn)])
            S.add("act", lambda e, ps=ps, s=s: e.activation(out=mrow[0:2, 512 * s:512 * s + 512], in_=ps[0:2, :], func=AF.Copy),
                  reads=[W1(psn)], writes=[W1("mrow")])

        def g(e, l=l):
            for j in range(48):
                col = (l * 48 + j) * 2
                ins = e.transpose(mps[:, col:col + 2], mrow[0:2, 128 * j:128 * j + 128], cst[0:2, C_ID:C_ID + 2])
            return ins
        S.add("pe", g, reads=[W1("mrow"), W1("cst")], writes=[W1(mpsn)])
    S.add("dve", lambda e: e.tensor_copy(out=modv[:, 0:depth, :, :].rearrange("p l j w -> p (l j w)"), in_=mps[:, 0:depth * 96]),
          reads=[W1(mpsn)], writes=[W1("modv")])
    A.release(mk)

    PA = Pool_([0, 1, 2, 3])
    PB = Pool_([4, 5])
    PC = Pool_([6, 7])

    def tap(name, ap, reads):
        if name in d_taps:
            S.add("sp", lambda e: e.dma_start(out=d_taps[name], in_=ap), reads=reads, writes=[W1("dram_out")], dma=True)

    def norm_stats(xt, xtn_reads, sq, sqn, lnv, lnn, rstd, rsn, w):
        S.add("act", lambda e: e.activation(out=sq[:, :, 0:w], in_=xt[:, :, 0:w], func=AF.Square), reads=xtn_reads, writes=[W1(sqn)])
        ps, psn = PA.next()

        def f(e):
            for c in range(8):
                ins = e.matmul(ps[:, 0:w], lhsT=ones_bf[:, :], rhs=sq[:, c, 0:w], start=(c == 0), stop=(c == 7))
            return ins
        S.add("pe", f, reads=[W1(sqn), W1("ones")], writes=[W1(psn)])
        S.add("act", lambda e: e.activation(out=lnv[:, 0:w], in_=ps[:, 0:w], func=AF.Ln, scale=1.0 / D, bias=EPS), reads=[W1(psn)], writes=[W1(lnn)])
        S.add("act", lambda e: e.activation(out=rstd[:, 0:w], in_=lnv[:, 0:w], func=AF.Exp, scale=-0.5), reads=[W1(lnn)], writes=[W1(rsn)])

    def rmsnorm_mod(l, which):
        mk = A.mark()
        xt = A.alloc("n_xt", F32, [2, 8, 512], nslots=2)
        sq = A.alloc("n_sq", BF16, [8, 512])
        lnv = A.alloc("n_ln", F32, [512])
        rstd = A.alloc("n_rstd", F32, [512])
        tmp = A.alloc("n_tmp", F32, [2, 512], nslots=2)
        k = 0
        for bi, (t0, w) in enumerate(BTS):
            wi = 1 if bi == 4 else 0
            xr_ = bi % 2
            S.add("sp", lambda e, xr_=xr_, t0=t0, w=w: e.dma_start(out=xt[:, xr_, :, 0:w], in_=xd_v[:, :, t0:t0 + w]),
                  reads=xall("xd", t0, w), writes=[W1("n_xt", xr_)], dma=True)
            norm_stats(xt[:, xr_, :, :], [W1("n_xt", xr_)], sq, "n_sq", lnv, "n_ln", rstd, "n_rstd", w)
            for c in range(8):
                r = k % 2
                k += 1
                S.add("dve", lambda e, c=c, r=r, xr_=xr_, w=w, wi=wi: e.scalar_tensor_tensor(
                    out=tmp[:, r, 0:w], in0=xt[:, xr_, c, 0:w], scalar=drv[:, which, wi, c:c + 1], in1=rstd[:, 0:w],
                    op0=ALU.mult, op1=ALU.mult),
                    reads=[W1("n_xt", xr_), W1("drv"), W1("n_rstd")], writes=[W1("n_tmp", r)])
                shj = (0 if which == 0 else 3) * 8 + c
                S.add("act", lambda e, c=c, r=r, t0=t0, w=w, wi=wi, shj=shj: e.activation(
                    out=hT[:, c, t0:t0 + w], in_=tmp[:, r, 0:w], func=AF.Identity, bias=modv[:, l, shj, wi:wi + 1]),
                    reads=[W1("n_tmp", r), W1("modv")], writes=[xsl("hT", c, t0, w)])
        A.release(mk)

    def layer_params(l):
        S.add("sp", lambda e: e.dma_start(out=lpar[:, :], in_=d_lp[l, :, :]), reads=[W1("dram_in")], writes=[W1("lpar")], dma=True)
        for wi in range(2):
            S.add("dve", lambda e, wi=wi: e.tensor_tensor(out=modv[:, l, :, wi], in0=modv[:, l, :, wi], in1=lpar[:, P_BADA:P_BADA + 48], op=ALU.add),
                  reads=[W1("modv"), W1("lpar")], writes=[W1("modv")])
        for which in range(2):
            sj = (1 if which == 0 else 4) * 8
            nw0 = P_N1 if which == 0 else P_N2
            for wi in range(2):
                S.add("dve", lambda e, which=which, wi=wi, sj=sj, nw0=nw0: e.scalar_tensor_tensor(
                    out=drv[:, which, wi, :], in0=modv[:, l, sj:sj + 8, wi], scalar=1.0, in1=lpar[:, nw0:nw0 + 8],
                    op0=ALU.add, op1=ALU.mult),
                    reads=[W1("modv"), W1("lpar")], writes=[W1("drv")])

    class ResidPass:
        def __init__(self, l, gate_which, tiles, ring):
            self.l = l
            self.g = gate_which
            self.tiles = tiles
            self.i = 0
            self.il = 0
            self.xr, self.xrn, self.n = ring
            self.LA = self.n - 1

        def _load(self, k):
            dm, t0, w = self.tiles[k]
            r = k % self.n
            xr, xrn = self.xr, self.xrn
            S.add("sp", lambda e: e.dma_start(out=xr[:, r, 0:w], in_=xd_v[:, dm, t0:t0 + w]),
                  reads=[xsl("xd", dm, t0, w)], writes=[W1(xrn, r)], dma=True)

        def prefetch(self):
            while self.il <= min(self.i + self.LA, len(self.tiles) - 1):
                self._load(self.il)
                self.il += 1

        def evac(self, ps, psn):
            self.prefetch()
            dm, t0, w = self.tiles[self.i]
            r = self.i % self.n
            self.i += 1
            gj = self.g * 8 + dm
            wi = 1 if t0 >= NLAT else 0
            l = self.l
            xr, xrn = self.xr, self.xrn
            S.add("dve", lambda e: e.scalar_tensor_tensor(
                out=xr[:, r, 0:w], in0=ps[:, 0:w], scalar=modv[:, l, gj, wi:wi + 1], in1=xr[:, r, 0:w],
                op0=ALU.mult, op1=ALU.add),
                reads=[W1(psn), W1("modv"), W1(xrn, r)], writes=[W1(xrn, r)])
            S.add("sp", lambda e: e.dma_start(out=xd_v[:, dm, t0:t0 + w], in_=xr[:, r, 0:w]),
                  reads=[W1(xrn, r)], writes=[xsl("xd", dm, t0, w)], dma=True)

    def proj_fm(unit, ucol, M, evac, tiles=BTS):
        wap, ws = unit
        for (t0, w) in tiles:
            ps, psn = PA.next()

            def f(e, ps=ps, t0=t0, w=w):
                for kc in range(8):
                    ins = e.matmul(ps[0:M, 0:w], lhsT=wap[:, 256 * kc + ucol:256 * kc + ucol + M], rhs=hT[:, kc, t0:t0 + w],
                                   start=(kc == 0), stop=(kc == 7))
                return ins
            S.add("pe", f, reads=[W1("wbf", ws)] + xall("hT", t0, w), writes=[W1(psn)])
            evac(ps, psn, t0, w)

    def proj_tm(ulist, evac):
        (w0, s0), (w1, s1) = ulist
        for j in range(NT):
            ps, psn = PA.next()

            def f(e, ps=ps, j=j):
                for half, wap in ((0, w0), (1, w1)):
                    for kc in range(8):
                        ins = e.matmul(ps[:, 256 * half:256 * half + 256], lhsT=hT[:, kc, 128 * j:128 * j + 128],
                                       rhs=wap[:, 256 * kc:256 * kc + 256], start=(kc == 0), stop=(kc == 7))
                return ins
            S.add("pe", f, reads=[W1("wbf", s0), W1("wbf", s1)] + xall("hT", 128 * j, 128), writes=[W1(psn)])
            evac(ps, psn, j)

    def rope_proj(kind, umain, uswap, ucol, dsl, slotf):
        mk = A.mark()
        tab = A.alloc("rp_tab", F32, [2, 2, 512], nslots=2)
        t1 = A.alloc("rp_t1", F32, [2, 512], nslots=2)
        t2 = A.alloc("rp_t2", F32, [2, 512], nslots=2)
        for bi, (t0, w) in enumerate(BTS):
            if bi == 4:
                def ev(ps, psn, t0, w):
                    S.add("act", lambda e: e.activation(out=dsl(t0, w), in_=ps[:, 0:w], func=AF.Copy),
                          reads=[W1(psn)], writes=[slotf(t0, w)])
                proj_fm(umain, ucol, 128, ev, tiles=[(t0, w)])
                continue
            r = bi % 2
            S.add("sp", lambda e, r=r, bi=bi: e.dma_start(out=tab[:, r, :, :], in_=d_rope[kind, :, bi, :, :]),
                  reads=[W1("dram_in")], writes=[W1("rp_tab", r)], dma=True)

            def ev1(ps, psn, t0, w, r=r):
                S.add("dve", lambda e: e.tensor_tensor(out=t1[:, r, 0:w], in0=ps[:, 0:w], in1=tab[:, r, 0, 0:w], op=ALU.mult),
                      reads=[W1(psn), W1("rp_tab", r)], writes=[W1("rp_t1", r)])

            def ev2(ps, psn, t0, w, r=r):
                S.add("dve", lambda e: e.tensor_tensor(out=t2[:, r, 0:w], in0=ps[:, 0:w], in1=tab[:, r, 1, 0:w], op=ALU.mult),
                      reads=[W1(psn), W1("rp_tab", r)], writes=[W1("rp_t2", r)])
                S.add("pool", lambda e: e.tensor_tensor(out=dsl(t0, w), in0=t1[:, r, 0:w], in1=t2[:, r, 0:w], op=ALU.add),
                      reads=[W1("rp_t1", r), W1("rp_t2", r)], writes=[slotf(t0, w)])
            proj_fm(umain, ucol, 128, ev1, tiles=[(t0, w)])
            proj_fm(uswap, ucol, 128, ev2, tiles=[(t0, w)])
        A.release(mk)

    def linattn_phase(l, oT):
        mk0 = A.mark()
        qT = A.alloc("la_qT", BF16, [T], nslots=NT)
        kT = A.alloc("la_kT", BF16, [T], nslots=NT)
        FWD = [16, 17] + list(range(16))
        BWD = [17, 16] + list(range(15, -1, -1))
        lat = lambda name, j, n=1: (name, j, j + n)

        def tok_slot(name):
            return lambda t0, w: (name, t0 // 128, (t0 + w + 127) // 128)

        u = WS.get(2)
        rope_proj(0, u[0], u[1], 0, lambda t0, w: qT[:, t0:t0 + w], tok_slot("la_qT"))
        rope_proj(0, u[0], u[1], 128, lambda t0, w: kT[:, t0:t0 + w], tok_slot("la_kT"))

        vg = A.alloc("la_vg", BF16, [NT, 512], nslots=NT)
        ktm = A.alloc("la_ktm", BF16, [8, 128], nslots=8)
        sprev = A.alloc("la_sprev", BF16, [2, NT, 256], nslots=2 * NT)
        sst = A.alloc("la_S", F32, [4, 256], nslots=4)
        um = A.alloc("la_um", F32, [6, 256], nslots=6)
        qx = A.alloc("la_qx", BF16, [2, 4, 128], nslots=2)
        at = A.alloc("la_at", BF16, [4, 4, 128], nslots=4)
        qd = A.alloc("la_qd", BF16, [2, 2, 128], nslots=2)
        osb = A.alloc("la_osb", F32, [2, 256], nslots=2)
        otm = A.alloc("la_otm", BF16, [2, 256], nslots=2)
        sm = A.alloc("la_sm", F32, [2, 32], nslots=2)
        par = A.alloc("la_par", F32, [1408])
        tmpb = A.alloc("la_tmpb", F32, [2, 256], nslots=2)

        def common_chunks(scoreK, scoreQ, masks, interQ, decs, udecs, kdec, normf, ocol):
            nd = len(scoreK)
            nk = 0
            for d in range(2):
                S.add("dve", lambda e, d=d: e.memset(sst[:, 2 * d, :], 0.0), writes=[W1("la_S", 2 * d)])
            ORD = (FWD, BWD)

            def emit_T(step, d):
                j = ORD[d][step]
                src, srcslot, tb = kdec[d]
                tb_ = 2 * d + step % 2
                psn = "ps%d" % tb_
                pst = psb[tb_][:, 0:64].bitcast(BF16)
                psl = W1(psn)
                S.add("pe", lambda e: e.transpose(pst[:, 0:128], src(j), ident[:, :]), reads=[srcslot(j), W1("ident")], writes=[psl])
                kr = 4 * d + step % 4
                if tb is None:
                    S.add("act", lambda e: e.activation(out=ktm[:, kr, :], in_=pst[:, 0:128], func=AF.Copy), reads=[psl], writes=[W1("la_ktm", kr)])
                else:
                    S.add("dve", lambda e: e.tensor_tensor(
                        out=ktm[:, kr, :].rearrange("p (h x) -> p h x", x=32), in0=pst[:, 0:128].rearrange("p (h x) -> p h x", x=32),
                        in1=tb.unsqueeze(2).to_broadcast([128, 4, 32]), op=ALU.mult),
                        reads=[psl, W1("la_par")], writes=[W1("la_ktm", kr)])

            def emit_U(step, d):
                j = ORD[d][step]
                kr = 4 * d + step % 4
                us = 3 * d + step % 3
                ub_ = 4 + 2 * d + step % 2
                ps2 = psb[ub_][:, 0:256]
                psl2 = W1("ps%d" % ub_)
                S.add("pe", lambda e: e.matmul(ps2, lhsT=ktm[:, kr, :], rhs=vg[:, j, 0:256], start=True, stop=True),
                      reads=[W1("la_ktm", kr), lat("la_vg", j)], writes=[psl2])
                if udecs[d] is None:
                    S.add("dve", lambda e: e.tensor_tensor(out=um[:, us, :], in0=ps2, in1=cst[:, C_SDM:C_SDM + 256], op=ALU.mult),
                          reads=[psl2, W1("cst")], writes=[W1("la_um", us)])
                else:
                    uap, urd = udecs[d](j)
                    S.add("dve", lambda e: e.scalar_tensor_tensor(
                        out=um[:, us, :], in0=ps2, scalar=uap, in1=cst[:, C_SDM:C_SDM + 256], op0=ALU.mult, op1=ALU.mult),
                        reads=[psl2, W1("cst"), urd], writes=[W1("la_um", us)])

            def emit_S(step, d):
                j = ORD[d][step]
                us = 3 * d + step % 3
                cur = 2 * d + step % 2
                nxt = 2 * d + (step + 1) % 2
                S.add("act", lambda e: e.activation(out=sprev[:, d, j, :], in_=sst[:, cur, :], func=AF.Copy),
                      reads=[W1("la_S", cur)], writes=[("la_sprev", d * NT + j, d * NT + j + 1)])
                if step + 1 < NT:
                    dap, drd = decs[d](j)
                    S.add("dve", lambda e: e.scalar_tensor_tensor(
                        out=sst[:, nxt, :], in0=sst[:, cur, :], scalar=dap, in1=um[:, us, :], op0=ALU.mult, op1=ALU.add),
                        reads=[W1("la_S", cur), W1("la_um", us), drd], writes=[W1("la_S", nxt)])
            for st0 in range(3):
                for d in range(2):
                    emit_T(st0, d)
            for st0 in range(2):
                for d in range(2):
                    emit_U(st0, d)
            for step in range(NT):
                for d in range(2):
                    if step + 3 < NT:
                        emit_T(step + 3, d)
                for d in range(2):
                    if step + 2 < NT:
                        emit_U(step + 2, d)
                for d in range(2):
                    emit_S(step, d)
            stA = {}

            def stage_A(j):
                r = j % 2
                ats = []
                for d in range(nd):
                    qr = (j * nd + d) % 2
                    S.add("pool", lambda e, d=d, qr=qr: e.tensor_tensor(
                        out=qx[:, qr, :, :], in0=scoreQ[d][0](j).unsqueeze(1).to_broadcast([128, 4, 128]), in1=bm4[:, :, :], op=ALU.mult),
                        reads=[scoreQ[d][1](j), W1("bm4")], writes=[W1("la_qx", qr)])
                    ps, psn = PA.next()
                    S.add("pe", lambda e, ps=ps, d=d, qr=qr: e.matmul(ps[:, :], lhsT=scoreK[d][0](j), rhs=qx[:, qr, :, :].rearrange("p a b -> p (a b)"),
                                                                 start=True, stop=True),
                          reads=[scoreK[d][1](j), W1("la_qx", qr)], writes=[W1(psn)])
                    ar = (j % 2) * 2 + d
                    S.add("dve", lambda e, ps=ps, d=d, ar=ar: e.tensor_tensor(
                        out=at[:, ar, :, :], in0=ps[:, :].rearrange("p (a b) -> p a b", b=128), in1=masks[d], op=ALU.mult),
                        reads=[W1(psn), W1("la_par"), W1("cst")], writes=[W1("la_at", ar)])
                    ats.append(ar)
                iq = []
                for d in range(2):
                    if interQ[d][2] is None:
                        iq.append((interQ[d][0](j), interQ[d][1](j)))
                    else:
                        tb = interQ[d][2]
                        S.add("pool", lambda e, d=d, tb=tb: e.tensor_tensor(out=qd[:, r, d, :], in0=interQ[d][0](j), in1=tb, op=ALU.mult),
                              reads=[interQ[d][1](j), W1("la_par")], writes=[W1("la_qd", r)])
                        iq.append((qd[:, r, d, :], W1("la_qd", r)))
                stA[j] = (ats, iq)

            def stage_B(j):
                r = j % 2
                ats, iq = stA.pop(j)
                ps, psn = PB.next()

                def f(e):
                    e.matmul(ps[:, 0:256], lhsT=iq[0][0], rhs=sprev[:, 0, j, :], start=True, stop=False)
                    ins = e.matmul(ps[:, 0:256], lhsT=iq[1][0], rhs=sprev[:, 1, j, :], start=False, stop=False)
                    n = len(ats) * 4
                    k = 0
                    for ar in ats:
                        for h in range(4):
                            k += 1
                            ins = e.matmul(ps[:, 64 * h:64 * h + 64], lhsT=at[:, ar, h, :], rhs=vg[:, j, 64 * h:64 * h + 64],
                                           start=False, stop=(k == n))
                    return ins
                S.add("pe", f, reads=[iq[0][1], iq[1][1], ("la_sprev", j, j + 1), ("la_sprev", NT + j, NT + j + 1), lat("la_vg", j)]
                      + [W1("la_at", a) for a in ats], writes=[W1(psn)])
                normf(ps, psn, j, r)

            stage_A(0)
            for j in range(NT):
                if j + 1 < NT:
                    stage_A(j + 1)
                stage_B(j)
                if j >= 1:
                    finish_chunk(j - 1, (j - 1) % 2, ocol)
            finish_chunk(NT - 1, (NT - 1) % 2, ocol)

        def finish_chunk(j, r, ocol):
            for hh in range(2):
                ps, psn = PC.next()
                pst = ps[:, 0:64].bitcast(BF16)
                S.add("pe", lambda e, pst=pst, hh=hh: e.transpose(pst[:, 0:128], otm[:, r, 128 * hh:128 * hh + 128], ident[:, :]),
                      reads=[W1("la_otm", r), W1("ident")], writes=[W1(psn)])
                S.add("act", lambda e, pst=pst, hh=hh: e.activation(out=oT[:, ocol + hh, 128 * j:128 * j + 128], in_=pst[:, 0:128], func=AF.Copy),
                      reads=[W1(psn)], writes=[xsl("oT", ocol + hh, 128 * j, 128)])

        def vg_evac(ps, psn, j):
            S.add("dve", lambda e: e.tensor_copy(out=vg[:, j, 0:256], in_=ps[:, 0:256]), reads=[W1(psn)], writes=[lat("la_vg", j)])
            S.add("act", lambda e: e.activation(out=vg[:, j, 256:512], in_=ps[:, 256:512], func=AF.Silu), reads=[W1(psn), lat("la_vg", j)], writes=[lat("la_vg", j)])

        u = WS.get(2)
        proj_tm(u, vg_evac)
        KS = 32 ** -0.5

        def logsig(out_ap, in_ap):
            S.add("act", lambda e: e.activation(out=out_ap, in_=in_ap, func=AF.Exp, scale=-1.0), reads=[W1("lpar"), W1("la_par")], writes=[W1("la_par")])
            S.add("act", lambda e: e.activation(out=out_ap, in_=out_ap, func=AF.Ln, bias=1.0), reads=[W1("la_par")], writes=[W1("la_par")])
            S.add("dve", lambda e: e.tensor_scalar(out=out_ap, in0=out_ap, scalar1=-1.0, scalar2=None, op0=ALU.mult), reads=[W1("la_par")], writes=[W1("la_par")])
        logsig(par[:, 0:2], lpar[:, P_RDLP:P_RDLP + 2])
        logsig(par[:, 2:10], lpar[:, P_RDLB:P_RDLB + 8])
        S.add("act", lambda e: e.activation(out=par[:, 10:12], in_=par[:, 0:2], func=AF.Exp, scale=128.0), reads=[W1("la_par")], writes=[W1("la_par")])
        for d in range(2):
            S.add("dve", lambda e, d=d: e.tensor_scalar(out=par[:, 16 + 4 * d:20 + 4 * d], in0=par[:, 2 + 4 * d:6 + 4 * d],
                                                       scalar1=cst[:, C_JC + d:C_JC + d + 1], scalar2=None, op0=ALU.mult),
                  reads=[W1("la_par"), W1("cst")], writes=[W1("la_par")])
            S.add("act", lambda e, d=d: e.activation(out=par[:, 16 + 4 * d:20 + 4 * d], in_=par[:, 16 + 4 * d:20 + 4 * d], func=AF.Exp),
                  reads=[W1("la_par")], writes=[W1("la_par")])
            S.add("dve", lambda e, d=d: e.tensor_scalar(out=par[:, 16 + 4 * d:20 + 4 * d], in0=par[:, 16 + 4 * d:20 + 4 * d], scalar1=KS, scalar2=None, op0=ALU.mult),
                  reads=[W1("la_par")], writes=[W1("la_par")])
            io = C_IP1 if d == 0 else C_IREV
            S.add("act", lambda e, d=d, io=io: e.activation(out=par[:, 128 + 128 * d:256 + 128 * d], in_=cst[:, io:io + 128], func=AF.Exp, scale=par[:, d:d + 1]),
                  reads=[W1("la_par"), W1("cst")], writes=[W1("la_par")])
        for h in range(4):
            mo = 512 + 128 * h
            S.add("dve", lambda e, h=h, mo=mo: e.tensor_scalar(out=par[:, mo:mo + 128], in0=cst[:, C_RPOS:C_RPOS + 128], scalar1=par[:, 2 + h:3 + h], scalar2=None, op0=ALU.mult),
                  reads=[W1("la_par"), W1("cst")], writes=[W1("la_par")])
            S.add("dve", lambda e, h=h, mo=mo: e.scalar_tensor_tensor(out=par[:, mo:mo + 128], in0=cst[:, C_RNEG:C_RNEG + 128], scalar=par[:, 6 + h:7 + h], in1=par[:, mo:mo + 128],
                                                                 op0=ALU.mult, op1=ALU.add),
                  reads=[W1("la_par"), W1("cst")], writes=[W1("la_par")])
            S.add("act", lambda e, mo=mo: e.activation(out=par[:, mo:mo + 128], in_=par[:, mo:mo + 128], func=AF.Exp), reads=[W1("la_par")], writes=[W1("la_par")])
            S.add("dve", lambda e, mo=mo: e.tensor_tensor(out=par[:, mo:mo + 128], in0=par[:, mo:mo + 128], in1=cst[:, C_ID:C_ID + 128], op=ALU.add),
                  reads=[W1("la_par"), W1("cst")], writes=[W1("la_par")])
            S.add("dve", lambda e, mo=mo: e.tensor_scalar(out=par[:, mo:mo + 128], in0=par[:, mo:mo + 128], scalar1=KS, scalar2=None, op0=ALU.mult),
                  reads=[W1("la_par")], writes=[W1("la_par")])

        kslice = lambda j: kT[:, 128 * j:128 * j + 128]
        qslice = lambda j: qT[:, 128 * j:128 * j + 128]
        ksl = lambda j: lat("la_kT", j)
        qsl = lambda j: lat("la_qT", j)

        def ret_norm(ps, psn, j, r):
            S.add("act", lambda e: e.activation(out=osb[:, r, :], in_=ps[:, 0:256], func=AF.Copy), reads=[W1(psn)], writes=[W1("la_osb", r)])
            o3 = osb[:, r, :].rearrange("p (h x) -> p h x", x=64)
            S.add("dve", lambda e: e.tensor_reduce(out=sm[:, r, 0:4], in_=o3, axis=AX.X, op=ALU.add), reads=[W1("la_osb", r)], writes=[W1("la_sm", r)])
            S.add("pool", lambda e: e.tensor_tensor(out=tmpb[:, r, :], in0=osb[:, r, :], in1=osb[:, r, :], op=ALU.mult), reads=[W1("la_osb", r)], writes=[W1("la_tmpb", r)])
            S.add("dve", lambda e: e.tensor_reduce(out=sm[:, r, 4:8], in_=tmpb[:, r, :].rearrange("p (h x) -> p h x", x=64), axis=AX.X, op=ALU.add),
                  reads=[W1("la_tmpb", r), W1("la_sm", r)], writes=[W1("la_sm", r)])
            S.add("dve", lambda e: e.tensor_scalar(out=sm[:, r, 8:12], in0=sm[:, r, 0:4], scalar1=1.0 / 64, scalar2=None, op0=ALU.mult), reads=[W1("la_sm", r)], writes=[W1("la_sm", r)])
            S.add("dve", lambda e: e.tensor_tensor(out=sm[:, r, 12:16], in0=sm[:, r, 8:12], in1=sm[:, r, 8:12], op=ALU.mult), reads=[W1("la_sm", r)], writes=[W1("la_sm", r)])
            S.add("dve", lambda e: e.scalar_tensor_tensor(out=sm[:, r, 16:20], in0=sm[:, r, 4:8], scalar=1.0 / 64, in1=sm[:, r, 12:16], op0=ALU.mult, op1=ALU.subtract),
                  reads=[W1("la_sm", r)], writes=[W1("la_sm", r)])
            S.add("act", lambda e: e.activation(out=sm[:, r, 20:24], in_=sm[:, r, 16:20], func=AF.Ln, bias=GN_EPS), reads=[W1("la_sm", r)], writes=[W1("la_sm", r)])
            S.add("act", lambda e: e.activation(out=sm[:, r, 24:28], in_=sm[:, r, 20:24], func=AF.Exp, scale=-0.5), reads=[W1("la_sm", r)], writes=[W1("la_sm", r)])
            S.add("dve", lambda e: e.tensor_tensor(out=o3, in0=o3, in1=sm[:, r, 8:12].unsqueeze(2).to_broadcast([128, 4, 64]), op=ALU.subtract),
                  reads=[W1("la_osb", r), W1("la_sm", r)], writes=[W1("la_osb", r)])
            S.add("dve", lambda e: e.tensor_tensor(out=o3, in0=o3, in1=sm[:, r, 24:28].unsqueeze(2).to_broadcast([128, 4, 64]), op=ALU.mult),
                  reads=[W1("la_osb", r), W1("la_sm", r)], writes=[W1("la_osb", r)])
            S.add("pool", lambda e: e.tensor_tensor(out=otm[:, r, :], in0=osb[:, r, :], in1=vg[:, j, 256:512], op=ALU.mult),
                  reads=[W1("la_osb", r), lat("la_vg", j)], writes=[W1("la_otm", r)])

        MT = par[:, 512:1024].rearrange("p (a b) -> p a b", b=128)
        PR = W1("la_par")
        common_chunks(
            scoreK=[(kslice, ksl)], scoreQ=[(qslice, qsl)], masks=[MT],
            interQ=[(qslice, qsl, par[:, 128:256]), (qslice, qsl, par[:, 256:384])],
            decs=[lambda j: (par[:, 10:11], PR), lambda j: (par[:, 11:12], PR)], udecs=[None, None],
            kdec=[(kslice, ksl, par[:, 16:20]), (kslice, ksl, par[:, 20:24])],
            normf=ret_norm, ocol=0)
        tap("oret", oT[:, 0:2, :], [("oT", 0, 2 * NT)])

        mk1 = A.mark()
        lr = A.alloc("gl_lr", BF16, [T], nslots=NT)
        gq = A.alloc("gl_q", BF16, [2, T], nslots=2 * NT)
        gk = A.alloc("gl_k", BF16, [2, T], nslots=2 * NT)
        g1 = A.alloc("gl_g1", F32, [2, 128], nslots=2)
        g2 = A.alloc("gl_g2", F32, [2, 128], nslots=2)
        g3 = A.alloc("gl_g3", F32, [2, 128], nslots=2)
        gdec = A.alloc("gl_dec", F32, [2, NT])
        wgb = A.alloc("gl_wg", BF16, [256])
        u = WS.get(2)

        def ev_q(ps, psn, t0, w):
            S.add("act", lambda e: e.activation(out=qT[:, t0:t0 + w], in_=ps[:, 0:w], func=AF.Copy), reads=[W1(psn)], writes=[tok_slot("la_qT")(t0, w)])

        def ev_k(ps, psn, t0, w):
            S.add("act", lambda e: e.activation(out=kT[:, t0:t0 + w], in_=ps[:, 0:w], func=AF.Copy), reads=[W1(psn)], writes=[tok_slot("la_kT")(t0, w)])

        def ev_lr(ps, psn, t0, w):
            S.add("act", lambda e: e.activation(out=lr[0:32, t0:t0 + w], in_=ps[0:32, 0:w], func=AF.Copy), reads=[W1(psn)], writes=[tok_slot("gl_lr")(t0, w)])
        proj_fm(u[0], 0, 128, ev_q)
        proj_fm(u[0], 128, 128, ev_k)
        proj_fm(u[1], 0, 32, ev_lr)
        u = WS.get(2)
        proj_tm(u, vg_evac)
        S.add("dve", lambda e: e.tensor_copy(out=wgb[0:32, :], in_=lpar[0:32, P_WG:P_WG + 256]), reads=[W1("lpar")], writes=[W1("gl_wg")])
        S.add("dve", lambda e: e.tensor_scalar(out=par[:, 1024:1026], in0=lpar[:, P_BG:P_BG + 2], scalar1=-1.0, scalar2=None, op0=ALU.mult),
              reads=[W1("lpar"), W1("la_par")], writes=[W1("la_par")])
        S.add("dve", lambda e: e.tensor_copy(out=par[:, 1100:1356], in_=lpar[:, P_GNW:P_GNW + 256]), reads=[W1("lpar"), W1("la_par")], writes=[W1("la_par")])
        QS = 32 ** -0.5
        ng = 0
        ONE = cst[:, C_ONE:C_ONE + 128]
        for d in range(2):
            for j in range(NT):
                r = ng % 2
                ng += 1
                sl = slice(128 * j, 128 * j + 128)
                ps, psn = PA.next()
                S.add("pe", lambda e, ps=ps, d=d, sl=sl: e.matmul(ps[:, 0:128], lhsT=wgb[0:32, 128 * d:128 * d + 128], rhs=lr[0:32, sl], start=True, stop=True),
                      reads=[W1("gl_wg"), lat("gl_lr", j)], writes=[W1(psn)])
                S.add("act", lambda e, ps=ps, d=d, r=r: e.activation(out=g1[:, r, :], in_=ps[:, 0:128], func=AF.Exp, scale=-1.0, bias=par[:, 1024 + d:1025 + d]),
                      reads=[W1(psn), W1("la_par")], writes=[W1("gl_g1", r)])
                S.add("act", lambda e, r=r: e.activation(out=g1[:, r, :], in_=g1[:, r, :], func=AF.Ln, bias=1.0), reads=[W1("gl_g1", r)], writes=[W1("gl_g1", r)])
                if d == 0:
                    S.add("dve", lambda e, r=r: e.tensor_tensor_scan(out=g2[:, r, :], data0=ONE, data1=g1[:, r, :], initial=0.0, op0=ALU.mult, op1=ALU.add),
                          reads=[W1("cst"), W1("gl_g1", r)], writes=[W1("gl_g2", r)])
                else:
                    S.add("dve", lambda e, r=r: e.tensor_tensor_scan(out=g2[:, r, ::-1], data0=ONE, data1=g1[:, r, ::-1], initial=0.0, op0=ALU.mult, op1=ALU.add),
                          reads=[W1("cst"), W1("gl_g1", r)], writes=[W1("gl_g2", r)])
                S.add("act", lambda e, r=r: e.activation(out=g3[:, r, :], in_=g2[:, r, :], func=AF.Exp, scale=-1.0 / 16), reads=[W1("gl_g2", r)], writes=[W1("gl_g3", r)])
                S.add("act", lambda e, r=r: e.activation(out=g1[:, r, :], in_=g2[:, r, :], func=AF.Exp, scale=1.0 / 16), reads=[W1("gl_g2", r), W1("gl_g1", r)], writes=[W1("gl_g1", r)])
                S.add("dve", lambda e, d=d, r=r, sl=sl: e.scalar_tensor_tensor(out=gq[:, d, sl], in0=qT[:, sl], scalar=QS, in1=g3[:, r, :], op0=ALU.mult, op1=ALU.mult),
                      reads=[lat("la_qT", j), W1("gl_g3", r)], writes=[("gl_q", d * NT + j, d * NT + j + 1)])
                S.add("pool", lambda e, d=d, r=r, sl=sl: e.tensor_tensor(out=gk[:, d, sl], in0=kT[:, sl], in1=g1[:, r, :], op=ALU.mult),
                      reads=[lat("la_kT", j), W1("gl_g1", r)], writes=[("gl_k", d * NT + j, d * NT + j + 1)])
                col = 127 if d == 0 else 0
                S.add("pool", lambda e, d=d, j=j, r=r, col=col: e.tensor_copy(out=gdec[:, d, j:j + 1], in_=g3[:, r, col:col + 1]),
                      reads=[W1("gl_g3", r)], writes=[W1("gl_dec")])

        def gla_norm(ps, psn, j, r):
            S.add("act", lambda e: e.activation(out=osb[:, r, :], in_=ps[:, 0:256], func=AF.Copy), reads=[W1(psn)], writes=[W1("la_osb", r)])
            o3 = osb[:, r, :].rearrange("p (h x) -> p h x", x=64)
            S.add("pool", lambda e: e.tensor_tensor(out=tmpb[:, r, :], in0=osb[:, r, :], in1=osb[:, r, :], op=ALU.mult), reads=[W1("la_osb", r)], writes=[W1("la_tmpb", r)])
            S.add("dve", lambda e: e.tensor_reduce(out=sm[:, r, 4:8], in_=tmpb[:, r, :].rearrange("p (h x) -> p h x", x=64), axis=AX.X, op=ALU.add),
                  reads=[W1("la_tmpb", r), W1("la_sm", r)], writes=[W1("la_sm", r)])
            S.add("act", lambda e: e.activation(out=sm[:, r, 20:24], in_=sm[:, r, 4:8], func=AF.Ln, scale=1.0 / 64, bias=EPS), reads=[W1("la_sm", r)], writes=[W1("la_sm", r)])
            S.add("act", lambda e: e.activation(out=sm[:, r, 24:28], in_=sm[:, r, 20:24], func=AF.Exp, scale=-0.5), reads=[W1("la_sm", r)], writes=[W1("la_sm", r)])
            S.add("dve", lambda e: e.tensor_tensor(out=o3, in0=o3, in1=sm[:, r, 24:28].unsqueeze(2).to_broadcast([128, 4, 64]), op=ALU.mult),
                  reads=[W1("la_osb", r), W1("la_sm", r)], writes=[W1("la_osb", r)])
            S.add("pool", lambda e: e.tensor_tensor(out=osb[:, r, :], in0=osb[:, r, :], in1=par[:, 1100:1356], op=ALU.mult),
                  reads=[W1("la_osb", r), W1("la_par")], writes=[W1("la_osb", r)])
            S.add("pool", lambda e: e.tensor_tensor(out=otm[:, r, :], in0=osb[:, r, :], in1=vg[:, j, 256:512], op=ALU.mult),
                  reads=[W1("la_osb", r), lat("la_vg", j)], writes=[W1("la_otm", r)])

        gsl = lambda nm, d: (lambda j: (nm, d * NT + j, d * NT + j + 1))
        LTm = cst[:, C_LT:C_LT + 128].unsqueeze(1).to_broadcast([128, 4, 128])
        UTm = cst[:, C_UT:C_UT + 128].unsqueeze(1).to_broadcast([128, 4, 128])
        gkf = lambda j: gk[:, 0, 128 * j:128 * j + 128]
        gkb = lambda j: gk[:, 1, 128 * j:128 * j + 128]
        gqf = lambda j: gq[:, 0, 128 * j:128 * j + 128]
        gqb = lambda j: gq[:, 1, 128 * j:128 * j + 128]
        GD = W1("gl_dec")
        decf = lambda j: (gdec[:, 0, j:j + 1], GD)
        decb = lambda j: (gdec[:, 1, j:j + 1], GD)
        common_chunks(
            scoreK=[(gkf, gsl("gl_k", 0)), (gkb, gsl("gl_k", 1))],
            scoreQ=[(gqf, gsl("gl_q", 0)), (gqb, gsl("gl_q", 1))],
            masks=[LTm, UTm],
            interQ=[(gqf, gsl("gl_q", 0), None), (gqb, gsl("gl_q", 1), None)],
            decs=[decf, decb], udecs=[decf, decb],
            kdec=[(gkf, gsl("gl_k", 0), None), (gkb, gsl("gl_k", 1), None)],
            normf=gla_norm, ocol=2)
        tap("ogla", oT[:, 2:4, :], [("oT", 2 * NT, 4 * NT)])
        A.release(mk1)
        A.release(mk0)

    def wout_rg(l, oT, units):
        tiles = [(dm, t0, w) for (t0, w) in BTS for dm in range(8)]
        mkx = A.mark()
        xr = A.alloc("xr_rg", F32, [12, 512], nslots=12)
        RP = ResidPass(l, 2, tiles, (xr, "xr_rg", 12))
        for (dm, t0, w) in tiles:
            wap, ws = units[dm // 4]
            ps, psn = PA.next()

            def f(e, ps=ps, wap=wap, dm=dm, t0=t0, w=w):
                for k in range(4):
                    off = 512 * k + 128 * (dm % 4)
                    ins = e.matmul(ps[:, 0:w], lhsT=wap[:, off:off + 128], rhs=oT[:, k, t0:t0 + w], start=(k == 0), stop=(k == 3))
                return ins
            S.add("pe", f, reads=[W1("wbf", ws)] + [xsl("oT", k, t0, w) for k in range(4)], writes=[W1(psn)])
            RP.evac(ps, psn)
        A.release(mkx)

    def diff_phase(l, lam_init):
        wrg = A.alloc("df_wrg", BF16, [2, UNIT], nslots=2)
        urg = WS.get(2)
        for i in range(2):
            S.add("act", lambda e, i=i: e.activation(out=wrg[:, i, :], in_=urg[i][0], func=AF.Copy), reads=[W1("wbf", urg[i][1])], writes=[W1("df_wrg", i)])
        kT = A.alloc("df_kT", BF16, [4, T], nslots=4 * NT)
        qT = A.alloc("df_qT", BF16, [4, T], nslots=4 * NT)

        def slotc(name, c):
            return lambda t0, w: xsl(name, c, t0, w)
        u = WS.get(4)
        for c in range(4):
            rope_proj(1, u[c // 2], u[2 + c // 2], 128 * (c % 2), (lambda c: (lambda t0, w: kT[:, c, t0:t0 + w]))(c), slotc("df_kT", c))
        u = WS.get(4)
        for c in range(4):
            rope_proj(1, u[c // 2], u[2 + c // 2], 128 * (c % 2), (lambda c: (lambda t0, w: qT[:, c, t0:t0 + w]))(c), slotc("df_qT", c))
        va = A.alloc("df_va", BF16, [NT, 512], nslots=NT)
        qx = A.alloc("df_qx", BF16, [4, 2, 512])
        pt = A.alloc("df_pt", BF16, [4, 512], nslots=4)
        pacc = A.alloc("df_pacc", F32, [2, 512], nslots=2)
        tt = A.alloc("df_t", F32, [2, 512], nslots=2)
        oo = A.alloc("df_o", F32, [2, 512], nslots=2)
        lns = A.alloc("df_lns", F32, [2, 512], nslots=2)
        rs = A.alloc("df_rs", F32, [2, 512], nslots=2)
        sqb = A.alloc("df_sqb", BF16, [2, 512], nslots=2)
        odT = A.alloc("df_odT", BF16, [4, 512], nslots=4)
        tmp = A.alloc("df_tmp", F32, [128])
        lamv = A.alloc("df_lam", F32, [8])
        wod = A.alloc("df_wod", BF16, [2, UNIT], nslots=2)
        xrd = A.alloc("xr_d", F32, [3, 512], nslots=3)
        otl = A.alloc("df_otl", BF16, [4, 512])

        S.add("dve", lambda e: e.tensor_tensor(out=tmp[:, 0:64], in0=lpar[:, P_DLAM:P_DLAM + 64], in1=lpar[:, P_DLAM + 64:P_DLAM + 128], op=ALU.mult),
              reads=[W1("lpar")], writes=[W1("df_tmp")])
        S.add("dve", lambda e: e.tensor_tensor(out=tmp[:, 64:128], in0=lpar[:, P_DLAM + 128:P_DLAM + 192], in1=lpar[:, P_DLAM + 192:P_DLAM + 256], op=ALU.mult),
              reads=[W1("lpar"), W1("df_tmp")], writes=[W1("df_tmp")])
        S.add("dve", lambda e: e.tensor_reduce(out=lamv[:, 0:2], in_=tmp[:, 0:128].rearrange("p (a b) -> p a b", b=64), axis=AX.X, op=ALU.add),
              reads=[W1("df_tmp")], writes=[W1("df_lam")])
        S.add("act", lambda e: e.activation(out=lamv[:, 2:4], in_=lamv[:, 0:2], func=AF.Exp), reads=[W1("df_lam")], writes=[W1("df_lam")])
        S.add("dve", lambda e: e.tensor_tensor(out=lamv[:, 4:5], in0=lamv[:, 3:4], in1=lamv[:, 2:3], op=ALU.subtract), reads=[W1("df_lam")], writes=[W1("df_lam")])
        S.add("dve", lambda e: e.tensor_scalar(out=lamv[:, 4:5], in0=lamv[:, 4:5], scalar1=-float(lam_init), scalar2=None, op0=ALU.add), reads=[W1("df_lam")], writes=[W1("df_lam")])
        S.add("dve", lambda e: e.tensor_scalar(out=lamv[:, 5:6], in0=lpar[:, P_DNWC:P_DNWC + 1], scalar1=float(1.0 - lam_init), scalar2=None, op0=ALU.mult),
              reads=[W1("lpar"), W1("df_lam")], writes=[W1("df_lam")])
        S.add("pool", lambda e: e.memset(qx[:, :, :, :], 0.0), writes=[W1("df_qx")])
        u = WS.get(2)

        def va_evac(ps, psn, j):
            S.add("act", lambda e: e.activation(out=va[:, j, :], in_=ps[:, :], func=AF.Copy), reads=[W1(psn)], writes=[("df_va", j, j + 1)])
        proj_tm(u, va_evac)
        tap("dk", kT[:, :, :], [("df_kT", 0, 4 * NT)])
        tap("dq", qT[:, :, :], [("df_qT", 0, 4 * NT)])
        ud = WS.get(2)
        for i in range(2):
            S.add("act", lambda e, i=i: e.activation(out=wod[:, i, :], in_=ud[i][0], func=AF.Copy), reads=[W1("wbf", ud[i][1])], writes=[W1("df_wod", i)])

        PS_S = Pool_([0, 1, 2])
        PS_ACC = Pool_([4, 5])
        PS_N = Pool_([6, 7])
        ONEF = cst[:, C_ONE:C_ONE + 128]
        npt = 0
        ng = 0
        for bi, (t0, w) in enumerate(BTS):
            keys = list(range(NT)) if bi < 4 else [16, 17]
            nk = len(keys)
            S.add("sp", lambda e, t0=t0, w=w: e.dma_start(out=otl[:, :, 0:w], in_=d_ot[:, :, t0:t0 + w]), reads=[W1("otd")], writes=[W1("df_otl")], dma=True)
            S.add("pool", lambda e, t0=t0, w=w: e.tensor_copy(out=qx[0:64, :, 0, 0:w], in_=qT[0:64, :, t0:t0 + w]),
                  reads=[xsl("df_qT", h, t0, w) for h in range(4)] + [W1("df_qx")], writes=[W1("df_qx")])
            S.add("pool", lambda e, t0=t0, w=w: e.tensor_copy(out=qx[64:128, :, 1, 0:w], in_=qT[64:128, :, t0:t0 + w]),
                  reads=[xsl("df_qT", h, t0, w) for h in range(4)] + [W1("df_qx")], writes=[W1("df_qx")])
            LA = 2
            stream = []
            for h in range(4):
                for m in range(2):
                    g = ng % 2
                    ng += 1
                    acc, accn = PS_ACC.next()
                    for ki, kt in enumerate(keys):
                        stream.append((h, m, g, acc, accn, ki, kt))
            sps = {}

            def emit_S(idx, w=w):
                h, m, g, acc, accn, ki, kt = stream[idx]
                ps, psn = PS_S.next()
                sps[idx] = (ps, psn)
                S.add("pe", lambda e: e.matmul(ps[:, 0:w], lhsT=kT[:, h, 128 * kt:128 * kt + 128], rhs=qx[:, h, m, 0:w], start=True, stop=True),
                      reads=[xsl("df_kT", h, 128 * kt, 128), W1("df_qx")], writes=[W1(psn)])

            def epilogue(h, m, g, acc, accn, w=w):
                pn, pnn = PS_N.next()
                S.add("pe", lambda e: e.matmul(pn[:, 0:w], lhsT=ONEF, rhs=pacc[:, g, 0:w], start=True, stop=True),
                      reads=[W1("cst"), W1("df_pacc", g)], writes=[W1(pnn)])
                S.add("act", lambda e: e.activation(out=lns[:, g, 0:w], in_=pn[:, 0:w], func=AF.Ln), reads=[W1(pnn)], writes=[W1("df_lns", g)])
                S.add("act", lambda e: e.activation(out=rs[:, g, 0:w], in_=lns[:, g, 0:w], func=AF.Exp, scale=-1.0), reads=[W1("df_lns", g)], writes=[W1("df_rs", g)])
                hr = h % 2
                if m == 0:
                    S.add("dve", lambda e: e.tensor_tensor(out=tt[:, hr, 0:w], in0=acc[:, 0:w], in1=rs[:, g, 0:w], op=ALU.mult),
                          reads=[W1(accn), W1("df_rs", g)], writes=[W1("df_t", hr)])
                    return
                S.add("dve", lambda e: e.scalar_tensor_tensor(
                    out=oo[:, hr, 0:w], in0=rs[:, g, 0:w], scalar=lamv[:, 4:5], in1=acc[:, 0:w], op0=ALU.mult, op1=ALU.mult),
                    reads=[W1(accn), W1("df_rs", g), W1("df_lam")], writes=[W1("df_o", hr)])
                S.add("pool", lambda e: e.tensor_tensor(out=oo[:, hr, 0:w], in0=oo[:, hr, 0:w], in1=tt[:, hr, 0:w], op=ALU.add),
                      reads=[W1("df_o", hr), W1("df_t", hr)], writes=[W1("df_o", hr)])
                S.add("pool", lambda e: e.tensor_tensor(out=sqb[:, hr, 0:w], in0=oo[:, hr, 0:w], in1=oo[:, hr, 0:w], op=ALU.mult),
                      reads=[W1("df_o", hr)], writes=[W1("df_sqb", hr)])
                pn2, pnn2 = PS_N.next()
                S.add("pe", lambda e: e.matmul(pn2[:, 0:w], lhsT=ones_bf[:, :], rhs=sqb[:, hr, 0:w], start=True, stop=True),
                      reads=[W1("ones"), W1("df_sqb", hr)], writes=[W1(pnn2)])
                S.add("act", lambda e: e.activation(out=lns[:, g, 0:w], in_=pn2[:, 0:w], func=AF.Ln, scale=1.0 / 128, bias=EPS),
                      reads=[W1(pnn2), W1("df_lns", g)], writes=[W1("df_lns", g)])
                S.add("act", lambda e: e.activation(out=rs[:, g, 0:w], in_=lns[:, g, 0:w], func=AF.Exp, scale=-0.5),
                      reads=[W1("df_lns", g), W1("df_rs", g)], writes=[W1("df_rs", g)])
                S.add("dve", lambda e: e.scalar_tensor_tensor(
                    out=odT[:, h, 0:w], in0=oo[:, hr, 0:w], scalar=lamv[:, 5:6], in1=rs[:, g, 0:w], op0=ALU.mult, op1=ALU.mult),
                    reads=[W1("df_o", hr), W1("df_rs", g), W1("df_lam")], writes=[W1("df_odT", h)])

            pending = []
            for idx in range(min(LA, len(stream))):
                emit_S(idx)
            for idx, (h, m, g, acc, accn, ki, kt) in enumerate(stream):
                if idx + LA < len(stream):
                    emit_S(idx + LA)
                ps, psn = sps.pop(idx)
                pr = npt % 4
                npt += 1
                S.add("act", lambda e, ps=ps, pr=pr, w=w: e.activation(out=pt[:, pr, 0:w], in_=ps[:, 0:w], func=AF.Exp, scale=0.125),
                      reads=[W1(psn)], writes=[W1("df_pt", pr)])
                S.add("pe", lambda e, acc=acc, pr=pr, kt=kt, h=h, ki=ki, w=w, nk=nk: e.matmul(
                    acc[:, 0:w], lhsT=va[:, kt, 128 * h:128 * h + 128], rhs=pt[:, pr, 0:w], start=(ki == 0), stop=(ki == nk - 1)),
                    reads=[W1("df_pt", pr), ("df_va", kt, kt + 1)], writes=[W1(accn)])
                if ki == 0:
                    S.add("dve", lambda e, g=g, pr=pr, w=w: e.tensor_copy(out=pacc[:, g, 0:w], in_=pt[:, pr, 0:w]),
                          reads=[W1("df_pt", pr)], writes=[W1("df_pacc", g)])
                else:
                    S.add("dve", lambda e, g=g, pr=pr, w=w: e.tensor_tensor(out=pacc[:, g, 0:w], in0=pacc[:, g, 0:w], in1=pt[:, pr, 0:w], op=ALU.add),
                          reads=[W1("df_pt", pr), W1("df_pacc", g)], writes=[W1("df_pacc", g)])
                if pending and ki == min(3, nk - 1):
                    epilogue(*pending.pop(0))
                if ki == nk - 1:
                    pending.append((h, m, g, acc, accn))
            while pending:
                epilogue(*pending.pop(0))
            if "odT" in d_taps:
                S.add("sp", lambda e, t0=t0, w=w: e.dma_start(out=d_taps["odT"][:, :, t0:t0 + w], in_=odT[:, :, 0:w]), reads=[("df_odT", 0, 4)], writes=[W1("dram_out")], dma=True)
            tiles = [(dm, t0, w) for dm in range(8)]
            RP = ResidPass(l, 2, tiles, (xrd, "xr_d", 3))
            for dm in range(8):
                ps, psn = PS_S.next()

                def f(e, ps=ps, dm=dm, w=w):
                    for k in range(4):
                        off = 512 * k + 128 * (dm % 4)
                        e.matmul(ps[:, 0:w], lhsT=wrg[:, dm // 4, off:off + 128], rhs=otl[:, k, 0:w], start=(k == 0), stop=False)
                    for k in range(4):
                        off = 512 * k + 128 * (dm % 4)
                        ins = e.matmul(ps[:, 0:w], lhsT=wod[:, dm // 4, off:off + 128], rhs=odT[:, k, 0:w], start=False, stop=(k == 3))
                    return ins
                S.add("pe", f, reads=[("df_wod", dm // 4, dm // 4 + 1), ("df_wrg", dm // 4, dm // 4 + 1), ("df_odT", 0, 4), W1("df_otl")], writes=[W1(psn)])
                RP.evac(ps, psn)

    def ffn_phase(l):
        mk = A.mark()
        actT = A.alloc("ff_act", BF16, [12, T], nslots=12 * NT)
        wdn = A.alloc("ff_wdn", BF16, [6, UNIT], nslots=6)
        sg = A.alloc("ff_sg", F32, [2, 512], nslots=2)
        xrf = A.alloc("xr_f", F32, [8, 512], nslots=8)
        k = 0
        for (f0, nf) in FGROUPS:
            for fi in range(nf):
                (wap, ws), = WS.get(1)
                for (t0, w) in BTS:
                    pg, pgn = PA.next()
                    pu, pun = PA.next()

                    def f(e, pg=pg, pu=pu, wap=wap, t0=t0, w=w):
                        for kc in range(8):
                            e.matmul(pg[:, 0:w], lhsT=wap[:, 256 * kc:256 * kc + 128], rhs=hT[:, kc, t0:t0 + w], start=(kc == 0), stop=(kc == 7))
                        for kc in range(8):
                            ins = e.matmul(pu[:, 0:w], lhsT=wap[:, 256 * kc + 128:256 * kc + 256], rhs=hT[:, kc, t0:t0 + w], start=(kc == 0), stop=(kc == 7))
                        return ins
                    S.add("pe", f, reads=[W1("wbf", ws)] + xall("hT", t0, w), writes=[W1(pgn), W1(pun)])
                    r = k % 2
                    k += 1
                    S.add("act", lambda e, pg=pg, r=r, w=w: e.activation(out=sg[:, r, 0:w], in_=pg[:, 0:w], func=AF.Silu), reads=[W1(pgn)], writes=[W1("ff_sg", r)])
                    S.add("dve", lambda e, pu=pu, r=r, fi=fi, t0=t0, w=w: e.tensor_tensor(out=actT[:, fi, t0:t0 + w], in0=pu[:, 0:w], in1=sg[:, r, 0:w], op=ALU.mult),
                          reads=[W1(pun), W1("ff_sg", r)], writes=[xsl("ff_act", fi, t0, w)])
            nu = nf // 2
            for i in range(nu):
                (wap, ws), = WS.get(1)
                S.add("act", lambda e, i=i, wap=wap: e.activation(out=wdn[:, i, :], in_=wap, func=AF.Copy), reads=[W1("wbf", ws)], writes=[W1("ff_wdn", i)])
            tiles = [(dm, t0, w) for (t0, w) in BTS for dm in range(8)]
            RP = ResidPass(l, 5, tiles, (xrf, "xr_f", 8))
            for (dm, t0, w) in tiles:
                ps, psn = PA.next()

                def f(e, ps=ps, dm=dm, t0=t0, w=w, nf=nf):
                    for fi in range(nf):
                        off = 1024 * (fi % 2) + 128 * dm
                        ins = e.matmul(ps[:, 0:w], lhsT=wdn[:, fi // 2, off:off + 128], rhs=actT[:, fi, t0:t0 + w], start=(fi == 0), stop=(fi == nf - 1))
                    return ins
                S.add("pe", f, reads=[("ff_wdn", 0, nu)] + [xsl("ff_act", fi, t0, w) for fi in range(nf)], writes=[W1(psn)])
                RP.evac(ps, psn)
        A.release(mk)

    for l in range(depth):
        lam_init = 0.8 - 0.6 * math.exp(-0.3 * l)
        layer_params(l)
        rmsnorm_mod(l, 0)
        tap("h%d" % l, hT[:, :, :], [("hT", 0, 8 * NT)])
        mkL = A.mark()
        oT = A.alloc("oT", BF16, [4, T], nslots=4 * NT)
        linattn_phase(l, oT)
        S.add("sp", lambda e: e.dma_start(out=d_ot[:, :, :], in_=oT[:, :, :]), reads=[("oT", 0, 4 * NT)], writes=[W1("otd")], dma=True)
        A.release(mkL)
        mkD = A.mark()
        diff_phase(l, lam_init)
        A.release(mkD)
        tap("xattn%d" % l, d_xd, [("xd", 0, 8 * NT)])
        rmsnorm_mod(l, 1)
        ffn_phase(l)
        tap("x%d" % l, d_xd, [("xd", 0, 8 * NT)])

    mk = A.mark()
    xt = A.alloc("f_xt", F32, [2, 8, 512], nslots=2)
    sq = A.alloc("f_sq", BF16, [8, 512])
    lnv = A.alloc("f_ln", F32, [512])
    rstd = A.alloc("f_rstd", F32, [512])
    out_v = d_out.rearrange("(c p) t -> p c t", p=128)
    for bi, (t0, w) in enumerate(BTS[:4]):
        r = bi % 2
        S.add("sp", lambda e, r=r, t0=t0, w=w: e.dma_start(out=xt[:, r, :, 0:w], in_=xd_v[:, :, t0:t0 + w]),
              reads=xall("xd", t0, w), writes=[W1("f_xt", r)], dma=True)
        norm_stats(xt[:, r, :, :], [W1("f_xt", r)], sq, "f_sq", lnv, "f_ln", rstd, "f_rstd", w)
        for c in range(8):
            S.add("dve", lambda e, c=c, r=r, w=w: e.scalar_tensor_tensor(out=xt[:, r, c, 0:w], in0=xt[:, r, c, 0:w], scalar=fnw[:, c:c + 1], in1=rstd[:, 0:w],
                                                                  op0=ALU.mult, op1=ALU.mult),
                  reads=[W1("f_xt", r), W1("fnw"), W1("f_rstd")], writes=[W1("f_xt", r)])
        S.add("sp", lambda e, r=r, t0=t0, w=w: e.dma_start(out=out_v[:, :, t0:t0 + w], in_=xt[:, r, :, 0:w]), reads=[W1("f_xt", r)], writes=[W1("dram_out")], dma=True)
    A.release(mk)
    S.add("sp", None, reads=[W1("dram_out")])
    S.emit(nc, es)
    es.close()
    return nc


_CACHE = {}


def _host_inputs(inp):
    wpk = _pack_weights(np.asarray(inp["w_in"], np.float32), np.asarray(inp["w_out"], np.float32),
                        np.asarray(inp["w_ffn_in"], np.float32), np.asarray(inp["w_ffn_out"], np.float32))
    npi = {k: np.asarray(v, np.float32) for k, v in inp.items()}
    lp = _layer_params(npi)
    cst = _const_pack()
    rope = _rope_tables()
    fnw = np.ascontiguousarray(npi["final_norm_w"].reshape(8, 128).T)
    wada = np.ascontiguousarray(npi["w_ada"])
    maps = []
    for b in range(8):
        xin = np.ascontiguousarray(np.concatenate([npi["x"][b].T, npi["ctx"][b].T], axis=1))
        cin = np.stack([npi["c"][b].reshape(8, 128).T, npi["c_ctx"].reshape(8, 128).T], axis=2)
        maps.append({"xin": xin, "cin": np.ascontiguousarray(cin), "wada": wada, "wpk": wpk, "lp": lp,
                     "fnw": fnw, "cst": cst, "rope": rope})
    return maps


def kernel(**inputs):
    maps = _host_inputs(inputs)
    if "nc" not in _CACHE:
        _CACHE["nc"] = build()
    res = run_bass_kernel_spmd(_CACHE["nc"], maps, core_ids=list(range(8)))
    out = np.stack([np.ascontiguousarray(r["out"].T) for r in res.results], axis=0)
    return out.astype(np.float32)
```

```python
import math
import numpy as np
from contextlib import ExitStack
import concourse.bass as bass
import concourse.mybir as mybir
from concourse.bass_utils import run_bass_kernel_spmd

F32 = mybir.dt.float32
BF16 = mybir.dt.bfloat16
AF = mybir.ActivationFunctionType
ALU = mybir.AluOpType
AX = mybir.AxisListType

D = 1024
NLAT = 2048
NCTX = 256
T = NLAT + NCTX
NT = T // 128
DEPTH = 4
DFF = 2816
NF = DFF // 128
PROJW = 3104
EPS = 1e-6
GN_EPS = 1e-5
BTS = [(0, 512), (512, 512), (1024, 512), (1536, 512), (2048, 256)]
UNIT = 2048
FGROUPS = [(0, 12), (12, 10)]

ENGS = ("pe", "act", "dve", "pool", "sp")


class Op:
    __slots__ = ("eng", "fn", "deps", "sig", "sem", "sigval", "idx", "dma", "dmak")

    def __init__(self, eng, fn, dma, idx):
        self.eng = eng
        self.fn = fn
        self.dma = dma
        self.idx = idx
        self.deps = {}
        self.sig = False
        self.sem = None
        self.sigval = 0
        self.dmak = 0


class Sched:
    def __init__(self):
        self.ops = []
        self.st = {}
        self.ndma = 0

    def newbuf(self, name, nslots=1, inherit=()):
        assert name not in self.st, name
        inh = frozenset(inherit)
        self.st[name] = [[None, {}, inh] for _ in range(nslots)]

    def final_ops(self, name):
        out = set()
        for w, rd, inh in self.st[name]:
            if w is not None:
                out.add(w)
            out.update(rd.values())
            out.update(inh)
        return out

    def delbuf(self, name):
        del self.st[name]

    def add(self, eng, fn, reads=(), writes=(), dma=False):
        op = Op(eng, fn, dma, len(self.ops))
        deps = op.deps
        for (name, lo, hi) in reads:
            st = self.st[name]
            assert 0 <= lo < hi <= len(st), (name, lo, hi, len(st))
            for s in range(lo, hi):
                w = st[s][0]
                if w is not None:
                    deps[w] = "raw"
                for o in st[s][2]:
                    deps.setdefault(o, "raw")
        for (name, lo, hi) in writes:
            st = self.st[name]
            assert 0 <= lo < hi <= len(st), (name, lo, hi, len(st))
            for s in range(lo, hi):
                w = st[s][0]
                if w is not None and w is not op:
                    deps.setdefault(w, "waw")
                for r in st[s][1].values():
                    if r is not op:
                        deps.setdefault(r, "war")
                for o in st[s][2]:
                    deps.setdefault(o, "war")
        rk = ("d", op.idx) if dma else eng
        for (name, lo, hi) in reads:
            st = self.st[name]
            for s in range(lo, hi):
                st[s][1][rk] = op
        for (name, lo, hi) in writes:
            st = self.st[name]
            for s in range(lo, hi):
                st[s][0] = op
                st[s][1] = {}
                st[s][2] = frozenset()
        for o in list(deps):
            if (not o.dma) and (not dma) and o.eng == eng and deps[o] != "raw":
                del deps[o]
        for o in deps:
            o.sig = True
        if dma:
            op.sig = True
        self.ops.append(op)
        return op

    def emit(self, nc, es):
        NDS = 16
        sems = {e: es.enter_context(nc.semaphore("s_" + e)) for e in ENGS}
        dsems = [es.enter_context(nc.semaphore("d%d" % i)) for i in range(NDS)]
        cnt = {e: 0 for e in ENGS}
        dcnt = [0] * NDS
        k = 0
        for op in self.ops:
            if not op.sig:
                continue
            if op.dma:
                j = k % NDS
                k += 1
                dcnt[j] += 16
                op.sem = dsems[j]
                op.sigval = dcnt[j]
                op.dmak = dcnt[j] - 16
            else:
                cnt[op.eng] += 1
                op.sem = sems[op.eng]
                op.sigval = cnt[op.eng]
        per = {e: [] for e in ENGS}
        for op in self.ops:
            per[op.eng].append(op)
        block = es.enter_context(nc.Block())

        def run(e, ops):
            waited = {}
            for op in ops:
                need = {}
                for o in op.deps:
                    key = id(o.sem)
                    if key not in need or need[key][1] < o.sigval:
                        need[key] = (o.sem, o.sigval)
                for key, (sem, val) in need.items():
                    if waited.get(key, 0) < val:
                        e.wait_ge(sem, val)
                        waited[key] = val
                if op.fn is None:
                    continue
                if op.dma and op.dmak > 0 and waited.get(id(op.sem), 0) < op.dmak:
                    e.wait_ge(op.sem, op.dmak)
                    waited[id(op.sem)] = op.dmak
                inst = op.fn(e)
                if op.sig:
                    inst.then_inc(op.sem, 16 if op.dma else 1)

        block.tensor(lambda e: run(e, per["pe"]))
        block.scalar(lambda e: run(e, per["act"]))
        block.vector(lambda e: run(e, per["dve"]))
        block.gpsimd(lambda e: run(e, per["pool"]))
        def run_sp(e):
            run(e, per["sp"])
            for j in range(NDS):
                if dcnt[j]:
                    e.wait_ge(dsems[j], dcnt[j])
        block.sync(run_sp)


class Arena:
    def __init__(self, sb, sched, nwords):
        self.sb = sb
        self.S = sched
        self.nwords = nwords
        self.top = 0
        self.stack = []
        self.retired = []

    def alloc(self, name, dtype, fshape, nslots=1):
        nel = 1
        for d in fshape:
            nel *= d
        nbytes = nel * (2 if dtype == BF16 else 4)
        nw = (nbytes + 3) // 4
        nw = (nw + 15) // 16 * 16
        off = self.top
        assert off + nw <= self.nwords, ("SBUF arena overflow", name, off, nw, self.nwords)
        self.top += nw
        inherit = set()
        for (lo, hi, ops) in self.retired:
            if lo < off + nw and hi > off:
                inherit |= ops
        self.S.newbuf(name, nslots, inherit)
        self.stack.append((name, off, nw))
        ap = self.sb[:, off:off + nw]
        if dtype == BF16:
            ap = ap.bitcast(BF16)
        ap = ap[:, 0:nel]
        if len(fshape) == 2:
            ap = ap.rearrange("p (a b) -> p a b", b=fshape[1])
        elif len(fshape) == 3:
            ap = ap.rearrange("p (a b c) -> p a b c", b=fshape[1], c=fshape[2])
        elif len(fshape) == 4:
            ap = ap.rearrange("p (a b c d) -> p a b c d", b=fshape[1], c=fshape[2], d=fshape[3])
        return ap

    def mark(self):
        return len(self.stack)

    def release(self, mark):
        while len(self.stack) > mark:
            name, off, nw = self.stack.pop()
            self.retired.append((off, off + nw, self.S.final_ops(name)))
            self.S.delbuf(name)
            self.top = off


def _swap_cols(base, nheads_blocks, blk):
    idx = np.arange(nheads_blocks * blk)
    b = idx // blk
    d = idx % blk
    return base + b * blk + (d + blk // 2) % blk


def _proj_units():
    u = []
    rq = np.arange(0, 128)
    rk = np.arange(128, 256)
    u.append(np.concatenate([rq, rk]))
    u.append(np.concatenate([_swap_cols(0, 4, 32), _swap_cols(128, 4, 32)]))
    u.append(np.arange(256, 512))
    u.append(np.arange(512, 768))
    u.append(np.arange(768, 1024))
    g1 = -np.ones(256, np.int64)
    g1[:32] = np.arange(1536, 1568)
    u.append(g1)
    u.append(np.arange(1024, 1280))
    u.append(np.arange(1280, 1536))
    u.append(np.arange(2080, 2336))
    u.append(np.arange(2336, 2592))
    sw = _swap_cols(2080, 16, 32)
    u.append(sw[:256])
    u.append(sw[256:])
    u.append(np.arange(1568, 1824))
    u.append(np.arange(1824, 2080))
    sw = _swap_cols(1568, 16, 32)
    u.append(sw[:256])
    u.append(sw[256:])
    u.append(np.arange(2592, 2848))
    u.append(np.arange(2848, 3104))
    return u


NPU = 18
NUL = NPU + 4 + 22 + 11


def _pack_weights(w_in, w_out, w_ffn_in, w_ffn_out):
    L = w_in.shape[0]
    out = np.zeros((L, 128, NUL, UNIT), np.float32)
    units = _proj_units()
    for l in range(L):
        wi = w_in[l].reshape(8, 128, PROJW)
        wo = w_out[l].reshape(8, 128, D)
        fi = w_ffn_in[l].reshape(8, 128, 2 * DFF)
        fo = w_ffn_out[l].reshape(NF, 128, D)
        k = 0
        def wout_units(part, k):
            for j in range(2):
                blk = wo[4 * part:4 * part + 4, :, 512 * j:512 * j + 512].transpose(1, 0, 2)
                out[l, :, k, :] = blk.reshape(128, UNIT)
                k += 1
            return k
        for ui, cols in enumerate(units):
            if ui == 8:
                k = wout_units(0, k)
            blk = np.zeros((128, 8, 256), np.float32)
            ok = cols >= 0
            blk[:, :, ok] = wi[:, :, cols[ok]].transpose(1, 0, 2)
            out[l, :, k, :] = blk.reshape(128, UNIT)
            k += 1
        k = wout_units(1, k)
        for (f0, nf) in FGROUPS:
            for f in range(f0, f0 + nf):
                blk = np.concatenate([fi[:, :, 128 * f:128 * f + 128],
                                      fi[:, :, DFF + 128 * f:DFF + 128 * f + 128]], axis=2)
                out[l, :, k, :] = blk.transpose(1, 0, 2).reshape(128, UNIT)
                k += 1
            for f in range(f0, f0 + nf, 2):
                blk = fo[f:f + 2].transpose(1, 0, 2)
                out[l, :, k, :] = blk.reshape(128, UNIT)
                k += 1
        assert k == NUL
    return out


C_ID = 0
C_SDM = 128
C_LT = 384
C_UT = 512
C_RPOS = 640
C_RNEG = 768
C_IP1 = 896
C_IREV = 1024
C_JC = 1152
C_ONE = 1168
NCS = 1296
C_BM4 = 1296
NCONST = 1808


def _const_pack():
    c = np.zeros((128, NCONST), np.float32)
    p = np.arange(128)
    c[:, C_ID:C_ID + 128] = np.eye(128)
    bm = np.zeros((128, 4, 128), np.float32)
    for h in range(4):
        bm[32 * h:32 * h + 32, h, :] = 1.0
    c[:, C_BM4:C_BM4 + 512] = bm.reshape(128, 512)
    sd = np.zeros((128, 4, 64), np.float32)
    for h in range(4):
        sd[32 * h:32 * h + 32, h, :] = 1.0
    c[:, C_SDM:C_SDM + 256] = sd.reshape(128, 256)
    j = p[:, None]
    i = p[None, :]
    c[:, C_LT:C_LT + 128] = (j <= i)
    c[:, C_UT:C_UT + 128] = (j >= i)
    c[:, C_RPOS:C_RPOS + 128] = np.maximum(i - j, 0)
    c[:, C_RNEG:C_RNEG + 128] = np.maximum(j - i, 0)
    c[:, C_IP1:C_IP1 + 128] = np.broadcast_to(i + 1, (128, 128))
    c[:, C_IREV:C_IREV + 128] = np.broadcast_to(128 - i, (128, 128))
    c[:, C_JC] = 127 - p
    c[:, C_JC + 1] = p
    c[:, C_ONE:C_ONE + 128] = 1.0
    return c


def _rope_tables():
    t = np.arange(NLAT, dtype=np.float32)
    p = np.arange(128)
    tab = np.zeros((2, 128, 2, NLAT), np.float32)
    invf = (1.0 / (np.float32(10000.0) ** np.linspace(0.0, 1.0, 16, dtype=np.float32))).astype(np.float32)
    d = p % 32
    ang = t[None, :] * invf[d % 16][:, None]
    tab[0, :, 0] = np.cos(ang)
    tab[0, :, 1] = np.sin(ang) * np.where(d < 16, -1.0, 1.0)[:, None]
    half = 16
    invf2 = (1.0 / (np.float32(10000.0) ** (np.arange(half, dtype=np.float32) / half))).astype(np.float32)
    e = p % 64
    part = e // 32
    d = e % 32
    row = (np.arange(NLAT) // 64).astype(np.float32)
    col = (np.arange(NLAT) % 64).astype(np.float32)
    pos = np.where(part[:, None] == 0, row[None, :], col[None, :]).astype(np.float32)
    ang = pos * invf2[d % 16][:, None]
    tab[1, :, 0] = np.cos(ang)
    tab[1, :, 1] = np.sin(ang) * np.where(d < 16, -1.0, 1.0)[:, None]
    tab = tab.reshape(2, 128, 2, 4, 512).transpose(0, 1, 3, 2, 4)
    return np.ascontiguousarray(tab)


P_N1 = 0
P_N2 = 8
P_BADA = 16
P_RDLP = 64
P_RDLB = 66
P_BG = 74
P_WG = 76
P_GNW = 332
P_DLAM = 588
P_DNWC = 844
NLP = 848


def _layer_params(inp):
    L = DEPTH
    o = np.zeros((L, 128, NLP), np.float32)
    p = np.arange(128)
    for l in range(L):
        o[l, :, P_N1:P_N1 + 8] = inp["norm1_w"][l].reshape(8, 128).T
        o[l, :, P_N2:P_N2 + 8] = inp["norm2_w"][l].reshape(8, 128).T
        o[l, :, P_BADA:P_BADA + 48] = inp["b_ada"][l].reshape(48, 128).T
        o[l, :, P_RDLP:P_RDLP + 2] = inp["ret_decay_logit"][l][:, p // 32].T
        o[l, :, P_RDLB:P_RDLB + 8] = inp["ret_decay_logit"][l].reshape(1, 8)
        o[l, :, P_BG:P_BG + 2] = inp["gla_b_gate"][l].T
        o[l, 0:16, P_WG:P_WG + 128] = inp["gla_w_gate"][l, 0]
        o[l, 16:32, P_WG + 128:P_WG + 256] = inp["gla_w_gate"][l, 1]
        o[l, :, P_GNW:P_GNW + 256] = np.tile(inp["gla_norm_w"][l], 4)[None, :]
        o[l, :, P_DLAM:P_DLAM + 256] = inp["diff_lambda"][l].reshape(1, 256)
        o[l, :, P_DNWC] = inp["diff_norm_w"][l]
    return o


def build(depth=DEPTH, taps=()):
    nc = bass.Bass("TRN2", target_bir_lowering=False)
    dt = lambda n, s, kind="ExternalInput": nc.dram_tensor(n, list(s), F32, kind=kind).ap()
    d_x = dt("xin", [D, T])
    d_c = dt("cin", [128, 8, 2])
    d_wada = dt("wada", [DEPTH, D, 6 * D])
    d_wpk = dt("wpk", [DEPTH, 128, NUL, UNIT])
    d_lp = dt("lp", [DEPTH, 128, NLP])
    d_fn = dt("fnw", [128, 8])
    d_const = dt("cst", [128, NCONST])
    d_rope = dt("rope", [2, 128, 4, 2, 512])
    d_out = dt("out", [D, NLAT], kind="ExternalOutput")
    d_xd = dt("xd", [D, T], kind="Internal")
    d_ot = nc.dram_tensor("otrg", [128, 4, T], BF16, kind="Internal").ap()
    d_taps = {}
    for (name, shape, tdt) in taps:
        d_taps[name] = nc.dram_tensor("tap_" + name, list(shape), tdt, kind="ExternalOutput").ap()
    xd_v = d_xd.rearrange("(c p) t -> p c t", p=128)

    S = Sched()
    es = ExitStack()
    NW = 52800
    sb = es.enter_context(nc.sbuf_tensor("sb", [128, NW], F32))
    A = Arena(sb, S, NW)
    psb = []
    PSLOTS = {}
    for i in range(8):
        psb.append(es.enter_context(nc.psum_tensor("ps%d" % i, [128, 512], F32)))
        S.newbuf("ps%d" % i, PSLOTS.get("ps%d" % i, 1))
    S.newbuf("dram_out", 1)
    S.newbuf("dram_in", 1)
    S.newbuf("xd", 8 * NT)
    S.newbuf("otd", 1)

    class Pool_:
        def __init__(self, banks):
            self.banks = banks
            self.i = 0

        def next(self):
            b = self.banks[self.i % len(self.banks)]
            self.i += 1
            return psb[b], "ps%d" % b

    def W1(name, s=0, n=1):
        if name in PSLOTS and s == 0 and n == 1:
            return (name, 0, PSLOTS[name])
        return (name, s, s + n)

    hT = A.alloc("hT", BF16, [8, T], nslots=8 * NT)
    cst = A.alloc("cst", F32, [NCS])
    ident = A.alloc("ident", BF16, [128])
    ones_bf = A.alloc("ones", BF16, [128])
    bm4 = A.alloc("bm4", BF16, [4, 128])
    modv = A.alloc("modv", F32, [DEPTH, 48, 2])
    lpar = A.alloc("lpar", F32, [NLP])
    drv = A.alloc("drv", F32, [2, 2, 8])
    fnw = A.alloc("fnw", F32, [8])
    NST, NBF = 2, 5
    wst = A.alloc("wst", F32, [NST, UNIT], nslots=NST)
    wbf = A.alloc("wbf", BF16, [NBF, UNIT], nslots=NBF)

    def xs(c, t0, w):
        return (c * NT + t0 // 128, c * NT + (t0 + w + 127) // 128)

    def xsl(name, c, t0, w):
        lo, hi = xs(c, t0, w)
        return (name, lo, hi)

    def xall(name, t0, w):
        return [xsl(name, c, t0, w) for c in range(8)]

    class WStream:
        def __init__(self, total):
            self.total = total
            self.nd = 0
            self.ncst = 0
            self.ng = 0

        def _dma(self, j):
            l, u = divmod(j, NUL)
            s = j % NST
            S.add("sp", lambda e, l=l, u=u, s=s: e.dma_start(out=wst[:, s, :], in_=d_wpk[l, :, u, :]),
                  reads=[W1("dram_in")], writes=[W1("wst", s)], dma=True)

        def _cast(self, j):
            s = j % NST
            b = j % NBF
            S.add("pool", lambda e, s=s, b=b: e.tensor_copy(out=wbf[:, b, :], in_=wst[:, s, :]),
                  reads=[W1("wst", s)], writes=[W1("wbf", b)])

        def get(self, n=1):
            i = self.ng
            self.ng += n
            assert n <= NBF and self.ng <= self.total
            last_cast = min(i + NBF - 1, self.total - 1)
            while self.ncst <= last_cast:
                while self.nd <= min(self.ncst + NST - 1, self.total - 1):
                    self._dma(self.nd)
                    self.nd += 1
                self._cast(self.ncst)
                self.ncst += 1
            return [(wbf[:, (i + k) % NBF, :], (i + k) % NBF) for k in range(n)]

    WS = WStream(depth * NUL)

    S.add("sp", lambda e: e.dma_start(out=cst[:, :], in_=d_const[:, 0:NCS]), reads=[W1("dram_in")], writes=[W1("cst")], dma=True)
    S.add("sp", lambda e: e.dma_start(out=fnw[:, :], in_=d_fn[:, :]), reads=[W1("dram_in")], writes=[W1("fnw")], dma=True)
    for c in range(8):
        S.add("sp", lambda e, c=c: e.dma_start(out=d_xd[128 * c:128 * c + 128, :], in_=d_x[128 * c:128 * c + 128, :]),
              reads=[W1("dram_in")], writes=[("xd", c * NT, (c + 1) * NT)], dma=True)
    S.add("dve", lambda e: e.tensor_copy(out=ident[:, :], in_=cst[:, C_ID:C_ID + 128]), reads=[W1("cst")], writes=[W1("ident")])
    S.add("dve", lambda e: e.memset(ones_bf[:, :], 1.0), writes=[W1("ones")])
    mkb = A.mark()
    bmf = A.alloc("bmf", F32, [512])
    S.add("sp", lambda e: e.dma_start(out=bmf[:, :], in_=d_const[:, C_BM4:C_BM4 + 512]), reads=[W1("dram_in")], writes=[W1("bmf")], dma=True)
    S.add("dve", lambda e: e.tensor_copy(out=bm4[:, :, :], in_=bmf[:, :].rearrange("p (a b) -> p a b", b=128)),
          reads=[W1("bmf")], writes=[W1("bm4")])
    A.release(mkb)

    mk = A.mark()
    condT = A.alloc("condT", F32, [8, 2])
    ast = A.alloc("ast", F32, [2, 8, 512], nslots=2)
    mrow = A.alloc("mrow", F32, [6 * D])
    S.add("sp", lambda e: e.dma_start(out=condT[:, :, :], in_=d_c[:, :, :]), reads=[W1("dram_in")], writes=[W1("condT")], dma=True)
    S.add("act", lambda e: e.activation(out=condT[:, :, :], in_=condT[:, :, :], func=AF.Silu), reads=[W1("condT")], writes=[W1("condT")])
    mps, mpsn = psb[7], "ps7"
    PR = Pool_([0, 1, 2, 3])
    nslab = 0
    for l in range(depth):
        wv = d_wada[l].rearrange("(kc p) n -> p kc n", p=128)
        for s in range(12):
            r = nslab % 2
            nslab += 1
            S.add("sp", lambda e, r=r, s=s, wv=wv: e.dma_start(out=ast[:, r, :, :], in_=wv[:, :, 512 * s:512 * s + 512]),
                  reads=[W1("dram_in")], writes=[W1("ast", r)], dma=True)
            ps, psn = PR.next()

            def f(e, r=r, ps=ps):
                for kc in range(8):
                    ins = e.matmul(ps[0:2, :], lhsT=condT[:, kc, :], rhs=ast[:, r, kc, :], start=(kc == 0), stop=(kc == 7))
                return ins
            S.add("pe", f, reads=[W1("ast", r), W1("condT")], writes=[W1(psn)])
            S.add("act", lambda e, ps=ps, s=s: e.activation(out=mrow[0:2, 512 * s:512 * s + 512], in_=ps[0:2, :], func=AF.Copy),
                  reads=[W1(psn)], writes=[W1("mrow")])

        def g(e, l=l):
            for j in range(48):
                col = (l * 48 + j) * 2
                ins = e.transpose(mps[:, col:col + 2], mrow[0:2, 128 * j:128 * j + 128], cst[0:2, C_ID:C_ID + 2])
            return ins
        S.add("pe", g, reads=[W1("mrow"), W1("cst")], writes=[W1(mpsn)])
    S.add("dve", lambda e: e.tensor_copy(out=modv[:, 0:depth, :, :].rearrange("p l j w -> p (l j w)"), in_=mps[:, 0:depth * 96]),
          reads=[W1(mpsn)], writes=[W1("modv")])
    A.release(mk)

    PA = Pool_([0, 1, 2, 3])
    PB = Pool_([4, 5])
    PC = Pool_([6, 7])

    def tap(name, ap, reads):
        if name in d_taps:
            S.add("sp", lambda e: e.dma_start(out=d_taps[name], in_=ap), reads=reads, writes=[W1("dram_out")], dma=True)

    def norm_stats(xt, xtn_reads, sq, sqn, lnv, lnn, rstd, rsn, w):
        S.add("act", lambda e: e.activation(out=sq[:, :, 0:w], in_=xt[:, :, 0:w], func=AF.Square), reads=xtn_reads, writes=[W1(sqn)])
        ps, psn = PA.next()

        def f(e):
            for c in range(8):
                ins = e.matmul(ps[:, 0:w], lhsT=ones_bf[:, :], rhs=sq[:, c, 0:w], start=(c == 0), stop=(c == 7))
            return ins
        S.add("pe", f, reads=[W1(sqn), W1("ones")], writes=[W1(psn)])
        S.add("act", lambda e: e.activation(out=lnv[:, 0:w], in_=ps[:, 0:w], func=AF.Ln, scale=1.0 / D, bias=EPS), reads=[W1(psn)], writes=[W1(lnn)])
        S.add("act", lambda e: e.activation(out=rstd[:, 0:w], in_=lnv[:, 0:w], func=AF.Exp, scale=-0.5), reads=[W1(lnn)], writes=[W1(rsn)])

    def rmsnorm_mod(l, which):
        mk = A.mark()
        xt = A.alloc("n_xt", F32, [2, 8, 512], nslots=2)
        sq = A.alloc("n_sq", BF16, [8, 512])
        lnv = A.alloc("n_ln", F32, [512])
        rstd = A.alloc("n_rstd", F32, [512])
        tmp = A.alloc("n_tmp", F32, [2, 512], nslots=2)
        k = 0
        for bi, (t0, w) in enumerate(BTS):
            wi = 1 if bi == 4 else 0
            xr_ = bi % 2
            S.add("sp", lambda e, xr_=xr_, t0=t0, w=w: e.dma_start(out=xt[:, xr_, :, 0:w], in_=xd_v[:, :, t0:t0 + w]),
                  reads=xall("xd", t0, w), writes=[W1("n_xt", xr_)], dma=True)
            norm_stats(xt[:, xr_, :, :], [W1("n_xt", xr_)], sq, "n_sq", lnv, "n_ln", rstd, "n_rstd", w)
            for c in range(8):
                r = k % 2
                k += 1
                S.add("dve", lambda e, c=c, r=r, xr_=xr_, w=w, wi=wi: e.scalar_tensor_tensor(
                    out=tmp[:, r, 0:w], in0=xt[:, xr_, c, 0:w], scalar=drv[:, which, wi, c:c + 1], in1=rstd[:, 0:w],
                    op0=ALU.mult, op1=ALU.mult),
                    reads=[W1("n_xt", xr_), W1("drv"), W1("n_rstd")], writes=[W1("n_tmp", r)])
                shj = (0 if which == 0 else 3) * 8 + c
                S.add("act", lambda e, c=c, r=r, t0=t0, w=w, wi=wi, shj=shj: e.activation(
                    out=hT[:, c, t0:t0 + w], in_=tmp[:, r, 0:w], func=AF.Identity, bias=modv[:, l, shj, wi:wi + 1]),
                    reads=[W1("n_tmp", r), W1("modv")], writes=[xsl("hT", c, t0, w)])
        A.release(mk)

    def layer_params(l):
        S.add("sp", lambda e: e.dma_start(out=lpar[:, :], in_=d_lp[l, :, :]), reads=[W1("dram_in")], writes=[W1("lpar")], dma=True)
        for wi in range(2):
            S.add("dve", lambda e, wi=wi: e.tensor_tensor(out=modv[:, l, :, wi], in0=modv[:, l, :, wi], in1=lpar[:, P_BADA:P_BADA + 48], op=ALU.add),
                  reads=[W1("modv"), W1("lpar")], writes=[W1("modv")])
        for which in range(2):
            sj = (1 if which == 0 else 4) * 8
            nw0 = P_N1 if which == 0 else P_N2
            for wi in range(2):
                S.add("dve", lambda e, which=which, wi=wi, sj=sj, nw0=nw0: e.scalar_tensor_tensor(
                    out=drv[:, which, wi, :], in0=modv[:, l, sj:sj + 8, wi], scalar=1.0, in1=lpar[:, nw0:nw0 + 8],
                    op0=ALU.add, op1=ALU.mult),
                    reads=[W1("modv"), W1("lpar")], writes=[W1("drv")])

    class ResidPass:
        def __init__(self, l, gate_which, tiles, ring):
            self.l = l
            self.g = gate_which
            self.tiles = tiles
            self.i = 0
            self.il = 0
            self.xr, self.xrn, self.n = ring
            self.LA = self.n - 1

        def _load(self, k):
            dm, t0, w = self.tiles[k]
            r = k % self.n
            xr, xrn = self.xr, self.xrn
            S.add("sp", lambda e: e.dma_start(out=xr[:, r, 0:w], in_=xd_v[:, dm, t0:t0 + w]),
                  reads=[xsl("xd", dm, t0, w)], writes=[W1(xrn, r)], dma=True)

        def prefetch(self):
            while self.il <= min(self.i + self.LA, len(self.tiles) - 1):
                self._load(self.il)
                self.il += 1

        def evac(self, ps, psn):
            self.prefetch()
            dm, t0, w = self.tiles[self.i]
            r = self.i % self.n
            self.i += 1
            gj = self.g * 8 + dm
            wi = 1 if t0 >= NLAT else 0
            l = self.l
            xr, xrn = self.xr, self.xrn
            S.add("dve", lambda e: e.scalar_tensor_tensor(
                out=xr[:, r, 0:w], in0=ps[:, 0:w], scalar=modv[:, l, gj, wi:wi + 1], in1=xr[:, r, 0:w],
                op0=ALU.mult, op1=ALU.add),
                reads=[W1(psn), W1("modv"), W1(xrn, r)], writes=[W1(xrn, r)])
            S.add("sp", lambda e: e.dma_start(out=xd_v[:, dm, t0:t0 + w], in_=xr[:, r, 0:w]),
                  reads=[W1(xrn, r)], writes=[xsl("xd", dm, t0, w)], dma=True)

    def proj_fm(unit, ucol, M, evac, tiles=BTS):
        wap, ws = unit
        for (t0, w) in tiles:
            ps, psn = PA.next()

            def f(e, ps=ps, t0=t0, w=w):
                for kc in range(8):
                    ins = e.matmul(ps[0:M, 0:w], lhsT=wap[:, 256 * kc + ucol:256 * kc + ucol + M], rhs=hT[:, kc, t0:t0 + w],
                                   start=(kc == 0), stop=(kc == 7))
                return ins
            S.add("pe", f, reads=[W1("wbf", ws)] + xall("hT", t0, w), writes=[W1(psn)])
            evac(ps, psn, t0, w)

    def proj_tm(ulist, evac):
        (w0, s0), (w1, s1) = ulist
        for j in range(NT):
            ps, psn = PA.next()

            def f(e, ps=ps, j=j):
                for half, wap in ((0, w0), (1, w1)):
                    for kc in range(8):
                        ins = e.matmul(ps[:, 256 * half:256 * half + 256], lhsT=hT[:, kc, 128 * j:128 * j + 128],
                                       rhs=wap[:, 256 * kc:256 * kc + 256], start=(kc == 0), stop=(kc == 7))
                return ins
            S.add("pe", f, reads=[W1("wbf", s0), W1("wbf", s1)] + xall("hT", 128 * j, 128), writes=[W1(psn)])
            evac(ps, psn, j)

    def rope_proj(kind, umain, uswap, ucol, dsl, slotf):
        mk = A.mark()
        tab = A.alloc("rp_tab", F32, [2, 2, 512], nslots=2)
        t1 = A.alloc("rp_t1", F32, [2, 512], nslots=2)
        t2 = A.alloc("rp_t2", F32, [2, 512], nslots=2)
        for bi, (t0, w) in enumerate(BTS):
            if bi == 4:
                def ev(ps, psn, t0, w):
                    S.add("act", lambda e: e.activation(out=dsl(t0, w), in_=ps[:, 0:w], func=AF.Copy),
                          reads=[W1(psn)], writes=[slotf(t0, w)])
                proj_fm(umain, ucol, 128, ev, tiles=[(t0, w)])
                continue
            r = bi % 2
            S.add("sp", lambda e, r=r, bi=bi: e.dma_start(out=tab[:, r, :, :], in_=d_rope[kind, :, bi, :, :]),
                  reads=[W1("dram_in")], writes=[W1("rp_tab", r)], dma=True)

            def ev1(ps, psn, t0, w, r=r):
                S.add("dve", lambda e: e.tensor_tensor(out=t1[:, r, 0:w], in0=ps[:, 0:w], in1=tab[:, r, 0, 0:w], op=ALU.mult),
                      reads=[W1(psn), W1("rp_tab", r)], writes=[W1("rp_t1", r)])

            def ev2(ps, psn, t0, w, r=r):
                S.add("dve", lambda e: e.tensor_tensor(out=t2[:, r, 0:w], in0=ps[:, 0:w], in1=tab[:, r, 1, 0:w], op=ALU.mult),
                      reads=[W1(psn), W1("rp_tab", r)], writes=[W1("rp_t2", r)])
                S.add("pool", lambda e: e.tensor_tensor(out=dsl(t0, w), in0=t1[:, r, 0:w], in1=t2[:, r, 0:w], op=ALU.add),
                      reads=[W1("rp_t1", r), W1("rp_t2", r)], writes=[slotf(t0, w)])
            proj_fm(umain, ucol, 128, ev1, tiles=[(t0, w)])
            proj_fm(uswap, ucol, 128, ev2, tiles=[(t0, w)])
        A.release(mk)

    def linattn_phase(l, oT):
        mk0 = A.mark()
        qT = A.alloc("la_qT", BF16, [T], nslots=NT)
        kT = A.alloc("la_kT", BF16, [T], nslots=NT)
        FWD = [16, 17] + list(range(16))
        BWD = [17, 16] + list(range(15, -1, -1))
        lat = lambda name, j, n=1: (name, j, j + n)

        def tok_slot(name):
            return lambda t0, w: (name, t0 // 128, (t0 + w + 127) // 128)

        u = WS.get(2)
        rope_proj(0, u[0], u[1], 0, lambda t0, w: qT[:, t0:t0 + w], tok_slot("la_qT"))
        rope_proj(0, u[0], u[1], 128, lambda t0, w: kT[:, t0:t0 + w], tok_slot("la_kT"))

        vg = A.alloc("la_vg", BF16, [NT, 512], nslots=NT)
        ktm = A.alloc("la_ktm", BF16, [8, 128], nslots=8)
        sprev = A.alloc("la_sprev", BF16, [2, NT, 256], nslots=2 * NT)
        sst = A.alloc("la_S", F32, [4, 256], nslots=4)
        um = A.alloc("la_um", F32, [6, 256], nslots=6)
        qx = A.alloc("la_qx", BF16, [2, 4, 128], nslots=2)
        at = A.alloc("la_at", BF16, [6, 4, 128], nslots=6)
        qd = A.alloc("la_qd", BF16, [3, 2, 128], nslots=3)
        osb = A.alloc("la_osb", F32, [2, 256], nslots=2)
        otm = A.alloc("la_otm", BF16, [2, 256], nslots=2)
        sm = A.alloc("la_sm", F32, [2, 32], nslots=2)
        par = A.alloc("la_par", F32, [1408])
        tmpb = A.alloc("la_tmpb", F32, [2, 256], nslots=2)

        def common_chunks(scoreK, scoreQ, masks, interQ, decs, udecs, kdec, normf, ocol):
            nd = len(scoreK)
            nk = 0
            for d in range(2):
                S.add("dve", lambda e, d=d: e.memset(sst[:, 2 * d, :], 0.0), writes=[W1("la_S", 2 * d)])
            ORD = (FWD, BWD)

            def emit_T(step, d):
                j = ORD[d][step]
                src, srcslot, tb = kdec[d]
                tb_ = 2 * d + step % 2
                psn = "ps%d" % tb_
                pst = psb[tb_][:, 0:64].bitcast(BF16)
                psl = W1(psn)
                S.add("pe", lambda e: e.transpose(pst[:, 0:128], src(j), ident[:, :]), reads=[srcslot(j), W1("ident")], writes=[psl])
                kr = 4 * d + step % 4
                if tb is None:
                    S.add("act", lambda e: e.activation(out=ktm[:, kr, :], in_=pst[:, 0:128], func=AF.Copy), reads=[psl], writes=[W1("la_ktm", kr)])
                else:
                    S.add("dve", lambda e: e.tensor_tensor(
                        out=ktm[:, kr, :].rearrange("p (h x) -> p h x", x=32), in0=pst[:, 0:128].rearrange("p (h x) -> p h x", x=32),
                        in1=tb.unsqueeze(2).to_broadcast([128, 4, 32]), op=ALU.mult),
                        reads=[psl, W1("la_par")], writes=[W1("la_ktm", kr)])

            def emit_U(step, d):
                j = ORD[d][step]
                kr = 4 * d + step % 4
                us = 3 * d + step % 3
                ub_ = 4 + 2 * d + step % 2
                ps2 = psb[ub_][:, 0:256]
                psl2 = W1("ps%d" % ub_)
                S.add("pe", lambda e: e.matmul(ps2, lhsT=ktm[:, kr, :], rhs=vg[:, j, 0:256], start=True, stop=True),
                      reads=[W1("la_ktm", kr), lat("la_vg", j)], writes=[psl2])
                if udecs[d] is None:
                    S.add("dve", lambda e: e.tensor_tensor(out=um[:, us, :], in0=ps2, in1=cst[:, C_SDM:C_SDM + 256], op=ALU.mult),
                          reads=[psl2, W1("cst")], writes=[W1("la_um", us)])
                else:
                    uap, urd = udecs[d](j)
                    S.add("dve", lambda e: e.scalar_tensor_tensor(
                        out=um[:, us, :], in0=ps2, scalar=uap, in1=cst[:, C_SDM:C_SDM + 256], op0=ALU.mult, op1=ALU.mult),
                        reads=[psl2, W1("cst"), urd], writes=[W1("la_um", us)])

            def emit_S(step, d):
                j = ORD[d][step]
                us = 3 * d + step % 3
                cur = 2 * d + step % 2
                nxt = 2 * d + (step + 1) % 2
                S.add("act", lambda e: e.activation(out=sprev[:, d, j, :], in_=sst[:, cur, :], func=AF.Copy),
                      reads=[W1("la_S", cur)], writes=[("la_sprev", d * NT + j, d * NT + j + 1)])
                if step + 1 < NT:
                    dap, drd = decs[d](j)
                    S.add("dve", lambda e: e.scalar_tensor_tensor(
                        out=sst[:, nxt, :], in0=sst[:, cur, :], scalar=dap, in1=um[:, us, :], op0=ALU.mult, op1=ALU.add),
                        reads=[W1("la_S", cur), W1("la_um", us), drd], writes=[W1("la_S", nxt)])
            for st0 in range(3):
                for d in range(2):
                    emit_T(st0, d)
            for st0 in range(2):
                for d in range(2):
                    emit_U(st0, d)
            for step in range(NT):
                for d in range(2):
                    if step + 3 < NT:
                        emit_T(step + 3, d)
                for d in range(2):
                    if step + 2 < NT:
                        emit_U(step + 2, d)
                for d in range(2):
                    emit_S(step, d)
            stA = {}

            def stage_A(j):
                r = j % 3
                ats = []
                for d in range(nd):
                    qr = (j * nd + d) % 2
                    S.add("pool", lambda e, d=d, qr=qr: e.tensor_tensor(
                        out=qx[:, qr, :, :], in0=scoreQ[d][0](j).unsqueeze(1).to_broadcast([128, 4, 128]), in1=bm4[:, :, :], op=ALU.mult),
                        reads=[scoreQ[d][1](j), W1("bm4")], writes=[W1("la_qx", qr)])
                    ps, psn = PA.next()
                    S.add("pe", lambda e, ps=ps, d=d, qr=qr: e.matmul(ps[:, :], lhsT=scoreK[d][0](j), rhs=qx[:, qr, :, :].rearrange("p a b -> p (a b)"),
                                                                 start=True, stop=True),
                          reads=[scoreK[d][1](j), W1("la_qx", qr)], writes=[W1(psn)])
                    ar = (j % 3) * 2 + d
                    S.add("dve", lambda e, ps=ps, d=d, ar=ar: e.tensor_tensor(
                        out=at[:, ar, :, :], in0=ps[:, :].rearrange("p (a b) -> p a b", b=128), in1=masks[d], op=ALU.mult),
                        reads=[W1(psn), W1("la_par"), W1("cst")], writes=[W1("la_at", ar)])
                    ats.append(ar)
                iq = []
                for d in range(2):
                    if interQ[d][2] is None:
                        iq.append((interQ[d][0](j), interQ[d][1](j)))
                    else:
                        tb = interQ[d][2]
                        S.add("pool", lambda e, d=d, tb=tb: e.tensor_tensor(out=qd[:, r, d, :], in0=interQ[d][0](j), in1=tb, op=ALU.mult),
                              reads=[interQ[d][1](j), W1("la_par")], writes=[W1("la_qd", r)])
                        iq.append((qd[:, r, d, :], W1("la_qd", r)))
                stA[j] = (ats, iq)

            def stage_B(j):
                r = j % 2
                ats, iq = stA.pop(j)
                ps, psn = PB.next()

                def f(e):
                    e.matmul(ps[:, 0:256], lhsT=iq[0][0], rhs=sprev[:, 0, j, :], start=True, stop=False)
                    ins = e.matmul(ps[:, 0:256], lhsT=iq[1][0], rhs=sprev[:, 1, j, :], start=False, stop=False)
                    n = len(ats) * 4
                    k = 0
                    for ar in ats:
                        for h in range(4):
                            k += 1
                            ins = e.matmul(ps[:, 64 * h:64 * h + 64], lhsT=at[:, ar, h, :], rhs=vg[:, j, 64 * h:64 * h + 64],
                                           start=False, stop=(k == n))
                    return ins
                S.add("pe", f, reads=[iq[0][1], iq[1][1], ("la_sprev", j, j + 1), ("la_sprev", NT + j, NT + j + 1), lat("la_vg", j)]
                      + [W1("la_at", a) for a in ats], writes=[W1(psn)])
                normf(ps, psn, j, r)

            stage_A(0)
            stage_A(1)
            for j in range(NT):
                if j + 2 < NT:
                    stage_A(j + 2)
                stage_B(j)
                if j >= 1:
                    finish_chunk(j - 1, (j - 1) % 2, ocol)
            finish_chunk(NT - 1, (NT - 1) % 2, ocol)

        def finish_chunk(j, r, ocol):
            for hh in range(2):
                ps, psn = PC.next()
                pst = ps[:, 0:64].bitcast(BF16)
                S.add("pe", lambda e, pst=pst, hh=hh: e.transpose(pst[:, 0:128], otm[:, r, 128 * hh:128 * hh + 128], ident[:, :]),
                      reads=[W1("la_otm", r), W1("ident")], writes=[W1(psn)])
                S.add("act", lambda e, pst=pst, hh=hh: e.activation(out=oT[:, ocol + hh, 128 * j:128 * j + 128], in_=pst[:, 0:128], func=AF.Copy),
                      reads=[W1(psn)], writes=[xsl("oT", ocol + hh, 128 * j, 128)])

        def vg_evac(ps, psn, j):
            S.add("dve", lambda e: e.tensor_copy(out=vg[:, j, 0:256], in_=ps[:, 0:256]), reads=[W1(psn)], writes=[lat("la_vg", j)])
            S.add("act", lambda e: e.activation(out=vg[:, j, 256:512], in_=ps[:, 256:512], func=AF.Silu), reads=[W1(psn), lat("la_vg", j)], writes=[lat("la_vg", j)])

        u = WS.get(2)
        proj_tm(u, vg_evac)
        KS = 32 ** -0.5

        def logsig(out_ap, in_ap):
            S.add("act", lambda e: e.activation(out=out_ap, in_=in_ap, func=AF.Exp, scale=-1.0), reads=[W1("lpar"), W1("la_par")], writes=[W1("la_par")])
            S.add("act", lambda e: e.activation(out=out_ap, in_=out_ap, func=AF.Ln, bias=1.0), reads=[W1("la_par")], writes=[W1("la_par")])
            S.add("dve", lambda e: e.tensor_scalar(out=out_ap, in0=out_ap, scalar1=-1.0, scalar2=None, op0=ALU.mult), reads=[W1("la_par")], writes=[W1("la_par")])
        logsig(par[:, 0:2], lpar[:, P_RDLP:P_RDLP + 2])
        logsig(par[:, 2:10], lpar[:, P_RDLB:P_RDLB + 8])
        S.add("act", lambda e: e.activation(out=par[:, 10:12], in_=par[:, 0:2], func=AF.Exp, scale=128.0), reads=[W1("la_par")], writes=[W1("la_par")])
        for d in range(2):
            S.add("dve", lambda e, d=d: e.tensor_scalar(out=par[:, 16 + 4 * d:20 + 4 * d], in0=par[:, 2 + 4 * d:6 + 4 * d],
                                                       scalar1=cst[:, C_JC + d:C_JC + d + 1], scalar2=None, op0=ALU.mult),
                  reads=[W1("la_par"), W1("cst")], writes=[W1("la_par")])
            S.add("act", lambda e, d=d: e.activation(out=par[:, 16 + 4 * d:20 + 4 * d], in_=par[:, 16 + 4 * d:20 + 4 * d], func=AF.Exp),
                  reads=[W1("la_par")], writes=[W1("la_par")])
            S.add("dve", lambda e, d=d: e.tensor_scalar(out=par[:, 16 + 4 * d:20 + 4 * d], in0=par[:, 16 + 4 * d:20 + 4 * d], scalar1=KS, scalar2=None, op0=ALU.mult),
                  reads=[W1("la_par")], writes=[W1("la_par")])
            io = C_IP1 if d == 0 else C_IREV
            S.add("act", lambda e, d=d, io=io: e.activation(out=par[:, 128 + 128 * d:256 + 128 * d], in_=cst[:, io:io + 128], func=AF.Exp, scale=par[:, d:d + 1]),
                  reads=[W1("la_par"), W1("cst")], writes=[W1("la_par")])
        for h in range(4):
            mo = 512 + 128 * h
            S.add("dve", lambda e, h=h, mo=mo: e.tensor_scalar(out=par[:, mo:mo + 128], in0=cst[:, C_RPOS:C_RPOS + 128], scalar1=par[:, 2 + h:3 + h], scalar2=None, op0=ALU.mult),
                  reads=[W1("la_par"), W1("cst")], writes=[W1("la_par")])
            S.add("dve", lambda e, h=h, mo=mo: e.scalar_tensor_tensor(out=par[:, mo:mo + 128], in0=cst[:, C_RNEG:C_RNEG + 128], scalar=par[:, 6 + h:7 + h], in1=par[:, mo:mo + 128],
                                                                 op0=ALU.mult, op1=ALU.add),
                  reads=[W1("la_par"), W1("cst")], writes=[W1("la_par")])
            S.add("act", lambda e, mo=mo: e.activation(out=par[:, mo:mo + 128], in_=par[:, mo:mo + 128], func=AF.Exp), reads=[W1("la_par")], writes=[W1("la_par")])
            S.add("dve", lambda e, mo=mo: e.tensor_tensor(out=par[:, mo:mo + 128], in0=par[:, mo:mo + 128], in1=cst[:, C_ID:C_ID + 128], op=ALU.add),
                  reads=[W1("la_par"), W1("cst")], writes=[W1("la_par")])
            S.add("dve", lambda e, mo=mo: e.tensor_scalar(out=par[:, mo:mo + 128], in0=par[:, mo:mo + 128], scalar1=KS, scalar2=None, op0=ALU.mult),
                  reads=[W1("la_par")], writes=[W1("la_par")])

        kslice = lambda j: kT[:, 128 * j:128 * j + 128]
        qslice = lambda j: qT[:, 128 * j:128 * j + 128]
        ksl = lambda j: lat("la_kT", j)
        qsl = lambda j: lat("la_qT", j)

        def ret_norm(ps, psn, j, r):
            S.add("act", lambda e: e.activation(out=osb[:, r, :], in_=ps[:, 0:256], func=AF.Copy), reads=[W1(psn)], writes=[W1("la_osb", r)])
            o3 = osb[:, r, :].rearrange("p (h x) -> p h x", x=64)
            S.add("dve", lambda e: e.tensor_reduce(out=sm[:, r, 0:4], in_=o3, axis=AX.X, op=ALU.add), reads=[W1("la_osb", r)], writes=[W1("la_sm", r)])
            S.add("pool", lambda e: e.tensor_tensor(out=tmpb[:, r, :], in0=osb[:, r, :], in1=osb[:, r, :], op=ALU.mult), reads=[W1("la_osb", r)], writes=[W1("la_tmpb", r)])
            S.add("dve", lambda e: e.tensor_reduce(out=sm[:, r, 4:8], in_=tmpb[:, r, :].rearrange("p (h x) -> p h x", x=64), axis=AX.X, op=ALU.add),
                  reads=[W1("la_tmpb", r), W1("la_sm", r)], writes=[W1("la_sm", r)])
            S.add("dve", lambda e: e.tensor_scalar(out=sm[:, r, 8:12], in0=sm[:, r, 0:4], scalar1=1.0 / 64, scalar2=None, op0=ALU.mult), reads=[W1("la_sm", r)], writes=[W1("la_sm", r)])
            S.add("dve", lambda e: e.tensor_tensor(out=sm[:, r, 12:16], in0=sm[:, r, 8:12], in1=sm[:, r, 8:12], op=ALU.mult), reads=[W1("la_sm", r)], writes=[W1("la_sm", r)])
            S.add("dve", lambda e: e.scalar_tensor_tensor(out=sm[:, r, 16:20], in0=sm[:, r, 4:8], scalar=1.0 / 64, in1=sm[:, r, 12:16], op0=ALU.mult, op1=ALU.subtract),
                  reads=[W1("la_sm", r)], writes=[W1("la_sm", r)])
            S.add("act", lambda e: e.activation(out=sm[:, r, 20:24], in_=sm[:, r, 16:20], func=AF.Ln, bias=GN_EPS), reads=[W1("la_sm", r)], writes=[W1("la_sm", r)])
            S.add("act", lambda e: e.activation(out=sm[:, r, 24:28], in_=sm[:, r, 20:24], func=AF.Exp, scale=-0.5), reads=[W1("la_sm", r)], writes=[W1("la_sm", r)])
            S.add("dve", lambda e: e.tensor_tensor(out=o3, in0=o3, in1=sm[:, r, 8:12].unsqueeze(2).to_broadcast([128, 4, 64]), op=ALU.subtract),
                  reads=[W1("la_osb", r), W1("la_sm", r)], writes=[W1("la_osb", r)])
            S.add("dve", lambda e: e.tensor_tensor(out=o3, in0=o3, in1=sm[:, r, 24:28].unsqueeze(2).to_broadcast([128, 4, 64]), op=ALU.mult),
                  reads=[W1("la_osb", r), W1("la_sm", r)], writes=[W1("la_osb", r)])
            S.add("pool", lambda e: e.tensor_tensor(out=otm[:, r, :], in0=osb[:, r, :], in1=vg[:, j, 256:512], op=ALU.mult),
                  reads=[W1("la_osb", r), lat("la_vg", j)], writes=[W1("la_otm", r)])

        MT = par[:, 512:1024].rearrange("p (a b) -> p a b", b=128)
        PR = W1("la_par")
        common_chunks(
            scoreK=[(kslice, ksl)], scoreQ=[(qslice, qsl)], masks=[MT],
            interQ=[(qslice, qsl, par[:, 128:256]), (qslice, qsl, par[:, 256:384])],
            decs=[lambda j: (par[:, 10:11], PR), lambda j: (par[:, 11:12], PR)], udecs=[None, None],
            kdec=[(kslice, ksl, par[:, 16:20]), (kslice, ksl, par[:, 20:24])],
            normf=ret_norm, ocol=0)
        tap("oret", oT[:, 0:2, :], [("oT", 0, 2 * NT)])

        mk1 = A.mark()
        lr = A.alloc("gl_lr", BF16, [T], nslots=NT)
        gq = A.alloc("gl_q", BF16, [2, T], nslots=2 * NT)
        gk = A.alloc("gl_k", BF16, [2, T], nslots=2 * NT)
        g1 = A.alloc("gl_g1", F32, [2, 128], nslots=2)
        g2 = A.alloc("gl_g2", F32, [2, 128], nslots=2)
        g3 = A.alloc("gl_g3", F32, [2, 128], nslots=2)
        gdec = A.alloc("gl_dec", F32, [2, NT])
        wgb = A.alloc("gl_wg", BF16, [256])
        u = WS.get(2)

        def ev_q(ps, psn, t0, w):
            S.add("act", lambda e: e.activation(out=qT[:, t0:t0 + w], in_=ps[:, 0:w], func=AF.Copy), reads=[W1(psn)], writes=[tok_slot("la_qT")(t0, w)])

        def ev_k(ps, psn, t0, w):
            S.add("act", lambda e: e.activation(out=kT[:, t0:t0 + w], in_=ps[:, 0:w], func=AF.Copy), reads=[W1(psn)], writes=[tok_slot("la_kT")(t0, w)])

        def ev_lr(ps, psn, t0, w):
            S.add("act", lambda e: e.activation(out=lr[0:32, t0:t0 + w], in_=ps[0:32, 0:w], func=AF.Copy), reads=[W1(psn)], writes=[tok_slot("gl_lr")(t0, w)])
        proj_fm(u[0], 0, 128, ev_q)
        proj_fm(u[0], 128, 128, ev_k)
        proj_fm(u[1], 0, 32, ev_lr)
        u = WS.get(2)
        proj_tm(u, vg_evac)
        S.add("dve", lambda e: e.tensor_copy(out=wgb[0:32, :], in_=lpar[0:32, P_WG:P_WG + 256]), reads=[W1("lpar")], writes=[W1("gl_wg")])
        S.add("dve", lambda e: e.tensor_scalar(out=par[:, 1024:1026], in0=lpar[:, P_BG:P_BG + 2], scalar1=-1.0, scalar2=None, op0=ALU.mult),
              reads=[W1("lpar"), W1("la_par")], writes=[W1("la_par")])
        S.add("dve", lambda e: e.tensor_copy(out=par[:, 1100:1356], in_=lpar[:, P_GNW:P_GNW + 256]), reads=[W1("lpar"), W1("la_par")], writes=[W1("la_par")])
        QS = 32 ** -0.5
        ng = 0
        ONE = cst[:, C_ONE:C_ONE + 128]
        for d in range(2):
            for j in range(NT):
                r = ng % 2
                ng += 1
                sl = slice(128 * j, 128 * j + 128)
                ps, psn = PA.next()
                S.add("pe", lambda e, ps=ps, d=d, sl=sl: e.matmul(ps[:, 0:128], lhsT=wgb[0:32, 128 * d:128 * d + 128], rhs=lr[0:32, sl], start=True, stop=True),
                      reads=[W1("gl_wg"), lat("gl_lr", j)], writes=[W1(psn)])
                S.add("act", lambda e, ps=ps, d=d, r=r: e.activation(out=g1[:, r, :], in_=ps[:, 0:128], func=AF.Exp, scale=-1.0, bias=par[:, 1024 + d:1025 + d]),
                      reads=[W1(psn), W1("la_par")], writes=[W1("gl_g1", r)])
                S.add("act", lambda e, r=r: e.activation(out=g1[:, r, :], in_=g1[:, r, :], func=AF.Ln, bias=1.0), reads=[W1("gl_g1", r)], writes=[W1("gl_g1", r)])
                if d == 0:
                    S.add("dve", lambda e, r=r: e.tensor_tensor_scan(out=g2[:, r, :], data0=ONE, data1=g1[:, r, :], initial=0.0, op0=ALU.mult, op1=ALU.add),
                          reads=[W1("cst"), W1("gl_g1", r)], writes=[W1("gl_g2", r)])
                else:
                    S.add("dve", lambda e, r=r: e.tensor_tensor_scan(out=g2[:, r, ::-1], data0=ONE, data1=g1[:, r, ::-1], initial=0.0, op0=ALU.mult, op1=ALU.add),
                          reads=[W1("cst"), W1("gl_g1", r)], writes=[W1("gl_g2", r)])
                S.add("act", lambda e, r=r: e.activation(out=g3[:, r, :], in_=g2[:, r, :], func=AF.Exp, scale=-1.0 / 16), reads=[W1("gl_g2", r)], writes=[W1("gl_g3", r)])
                S.add("act", lambda e, r=r: e.activation(out=g1[:, r, :], in_=g2[:, r, :], func=AF.Exp, scale=1.0 / 16), reads=[W1("gl_g2", r), W1("gl_g1", r)], writes=[W1("gl_g1", r)])
                S.add("dve", lambda e, d=d, r=r, sl=sl: e.scalar_tensor_tensor(out=gq[:, d, sl], in0=qT[:, sl], scalar=QS, in1=g3[:, r, :], op0=ALU.mult, op1=ALU.mult),
                      reads=[lat("la_qT", j), W1("gl_g3", r)], writes=[("gl_q", d * NT + j, d * NT + j + 1)])
                S.add("pool", lambda e, d=d, r=r, sl=sl: e.tensor_tensor(out=gk[:, d, sl], in0=kT[:, sl], in1=g1[:, r, :], op=ALU.mult),
                      reads=[lat("la_kT", j), W1("gl_g1", r)], writes=[("gl_k", d * NT + j, d * NT + j + 1)])
                col = 127 if d == 0 else 0
                S.add("pool", lambda e, d=d, j=j, r=r, col=col: e.tensor_copy(out=gdec[:, d, j:j + 1], in_=g3[:, r, col:col + 1]),
                      reads=[W1("gl_g3", r)], writes=[W1("gl_dec")])

        def gla_norm(ps, psn, j, r):
            S.add("act", lambda e: e.activation(out=osb[:, r, :], in_=ps[:, 0:256], func=AF.Copy), reads=[W1(psn)], writes=[W1("la_osb", r)])
            o3 = osb[:, r, :].rearrange("p (h x) -> p h x", x=64)
            S.add("pool", lambda e: e.tensor_tensor(out=tmpb[:, r, :], in0=osb[:, r, :], in1=osb[:, r, :], op=ALU.mult), reads=[W1("la_osb", r)], writes=[W1("la_tmpb", r)])
            S.add("dve", lambda e: e.tensor_reduce(out=sm[:, r, 4:8], in_=tmpb[:, r, :].rearrange("p (h x) -> p h x", x=64), axis=AX.X, op=ALU.add),
                  reads=[W1("la_tmpb", r), W1("la_sm", r)], writes=[W1("la_sm", r)])
            S.add("act", lambda e: e.activation(out=sm[:, r, 20:24], in_=sm[:, r, 4:8], func=AF.Ln, scale=1.0 / 64, bias=EPS), reads=[W1("la_sm", r)], writes=[W1("la_sm", r)])
            S.add("act", lambda e: e.activation(out=sm[:, r, 24:28], in_=sm[:, r, 20:24], func=AF.Exp, scale=-0.5), reads=[W1("la_sm", r)], writes=[W1("la_sm", r)])
            S.add("dve", lambda e: e.tensor_tensor(out=o3, in0=o3, in1=sm[:, r, 24:28].unsqueeze(2).to_broadcast([128, 4, 64]), op=ALU.mult),
                  reads=[W1("la_osb", r), W1("la_sm", r)], writes=[W1("la_osb", r)])
            S.add("pool", lambda e: e.tensor_tensor(out=osb[:, r, :], in0=osb[:, r, :], in1=par[:, 1100:1356], op=ALU.mult),
                  reads=[W1("la_osb", r), W1("la_par")], writes=[W1("la_osb", r)])
            S.add("pool", lambda e: e.tensor_tensor(out=otm[:, r, :], in0=osb[:, r, :], in1=vg[:, j, 256:512], op=ALU.mult),
                  reads=[W1("la_osb", r), lat("la_vg", j)], writes=[W1("la_otm", r)])

        gsl = lambda nm, d: (lambda j: (nm, d * NT + j, d * NT + j + 1))
        LTm = cst[:, C_LT:C_LT + 128].unsqueeze(1).to_broadcast([128, 4, 128])
        UTm = cst[:, C_UT:C_UT + 128].unsqueeze(1).to_broadcast([128, 4, 128])
        gkf = lambda j: gk[:, 0, 128 * j:128 * j + 128]
        gkb = lambda j: gk[:, 1, 128 * j:128 * j + 128]
        gqf = lambda j: gq[:, 0, 128 * j:128 * j + 128]
        gqb = lambda j: gq[:, 1, 128 * j:128 * j + 128]
        GD = W1("gl_dec")
        decf = lambda j: (gdec[:, 0, j:j + 1], GD)
        decb = lambda j: (gdec[:, 1, j:j + 1], GD)
        common_chunks(
            scoreK=[(gkf, gsl("gl_k", 0)), (gkb, gsl("gl_k", 1))],
            scoreQ=[(gqf, gsl("gl_q", 0)), (gqb, gsl("gl_q", 1))],
            masks=[LTm, UTm],
            interQ=[(gqf, gsl("gl_q", 0), None), (gqb, gsl("gl_q", 1), None)],
            decs=[decf, decb], udecs=[decf, decb],
            kdec=[(gkf, gsl("gl_k", 0), None), (gkb, gsl("gl_k", 1), None)],
            normf=gla_norm, ocol=2)
        tap("ogla", oT[:, 2:4, :], [("oT", 2 * NT, 4 * NT)])
        A.release(mk1)
        A.release(mk0)

    def wout_rg(l, oT, units):
        tiles = [(dm, t0, w) for (t0, w) in BTS for dm in range(8)]
        mkx = A.mark()
        xr = A.alloc("xr_rg", F32, [12, 512], nslots=12)
        RP = ResidPass(l, 2, tiles, (xr, "xr_rg", 12))
        for (dm, t0, w) in tiles:
            wap, ws = units[dm // 4]
            ps, psn = PA.next()

            def f(e, ps=ps, wap=wap, dm=dm, t0=t0, w=w):
                for k in range(4):
                    off = 512 * k + 128 * (dm % 4)
                    ins = e.matmul(ps[:, 0:w], lhsT=wap[:, off:off + 128], rhs=oT[:, k, t0:t0 + w], start=(k == 0), stop=(k == 3))
                return ins
            S.add("pe", f, reads=[W1("wbf", ws)] + [xsl("oT", k, t0, w) for k in range(4)], writes=[W1(psn)])
            RP.evac(ps, psn)
        A.release(mkx)

    def diff_phase(l, lam_init):
        wrg = A.alloc("df_wrg", BF16, [2, UNIT], nslots=2)
        urg = WS.get(2)
        for i in range(2):
            S.add("act", lambda e, i=i: e.activation(out=wrg[:, i, :], in_=urg[i][0], func=AF.Copy), reads=[W1("wbf", urg[i][1])], writes=[W1("df_wrg", i)])
        kT = A.alloc("df_kT", BF16, [4, T], nslots=4 * NT)
        qT = A.alloc("df_qT", BF16, [4, T], nslots=4 * NT)

        def slotc(name, c):
            return lambda t0, w: xsl(name, c, t0, w)
        u = WS.get(4)
        for c in range(4):
            rope_proj(1, u[c // 2], u[2 + c // 2], 128 * (c % 2), (lambda c: (lambda t0, w: kT[:, c, t0:t0 + w]))(c), slotc("df_kT", c))
        u = WS.get(4)
        for c in range(4):
            rope_proj(1, u[c // 2], u[2 + c // 2], 128 * (c % 2), (lambda c: (lambda t0, w: qT[:, c, t0:t0 + w]))(c), slotc("df_qT", c))
        va = A.alloc("df_va", BF16, [NT, 512], nslots=NT)
        qx = A.alloc("df_qx", BF16, [4, 2, 512])
        pt = A.alloc("df_pt", BF16, [4, 512], nslots=4)
        pacc = A.alloc("df_pacc", F32, [2, 512], nslots=2)
        tt = A.alloc("df_t", F32, [2, 512], nslots=2)
        oo = A.alloc("df_o", F32, [2, 512], nslots=2)
        lns = A.alloc("df_lns", F32, [2, 512], nslots=2)
        rs = A.alloc("df_rs", F32, [2, 512], nslots=2)
        sqb = A.alloc("df_sqb", BF16, [2, 512], nslots=2)
        odT = A.alloc("df_odT", BF16, [4, 512], nslots=4)
        tmp = A.alloc("df_tmp", F32, [128])
        lamv = A.alloc("df_lam", F32, [8])
        wod = A.alloc("df_wod", BF16, [2, UNIT], nslots=2)
        xrd = A.alloc("xr_d", F32, [3, 512], nslots=3)
        otl = A.alloc("df_otl", BF16, [4, 512])

        S.add("dve", lambda e: e.tensor_tensor(out=tmp[:, 0:64], in0=lpar[:, P_DLAM:P_DLAM + 64], in1=lpar[:, P_DLAM + 64:P_DLAM + 128], op=ALU.mult),
              reads=[W1("lpar")], writes=[W1("df_tmp")])
        S.add("dve", lambda e: e.tensor_tensor(out=tmp[:, 64:128], in0=lpar[:, P_DLAM + 128:P_DLAM + 192], in1=lpar[:, P_DLAM + 192:P_DLAM + 256], op=ALU.mult),
              reads=[W1("lpar"), W1("df_tmp")], writes=[W1("df_tmp")])
        S.add("dve", lambda e: e.tensor_reduce(out=lamv[:, 0:2], in_=tmp[:, 0:128].rearrange("p (a b) -> p a b", b=64), axis=AX.X, op=ALU.add),
              reads=[W1("df_tmp")], writes=[W1("df_lam")])
        S.add("act", lambda e: e.activation(out=lamv[:, 2:4], in_=lamv[:, 0:2], func=AF.Exp), reads=[W1("df_lam")], writes=[W1("df_lam")])
        S.add("dve", lambda e: e.tensor_tensor(out=lamv[:, 4:5], in0=lamv[:, 3:4], in1=lamv[:, 2:3], op=ALU.subtract), reads=[W1("df_lam")], writes=[W1("df_lam")])
        S.add("dve", lambda e: e.tensor_scalar(out=lamv[:, 4:5], in0=lamv[:, 4:5], scalar1=-float(lam_init), scalar2=None, op0=ALU.add), reads=[W1("df_lam")], writes=[W1("df_lam")])
        S.add("dve", lambda e: e.tensor_scalar(out=lamv[:, 5:6], in0=lpar[:, P_DNWC:P_DNWC + 1], scalar1=float(1.0 - lam_init), scalar2=None, op0=ALU.mult),
              reads=[W1("lpar"), W1("df_lam")], writes=[W1("df_lam")])
        S.add("pool", lambda e: e.memset(qx[:, :, :, :], 0.0), writes=[W1("df_qx")])
        u = WS.get(2)

        def va_evac(ps, psn, j):
            S.add("act", lambda e: e.activation(out=va[:, j, :], in_=ps[:, :], func=AF.Copy), reads=[W1(psn)], writes=[("df_va", j, j + 1)])
        proj_tm(u, va_evac)
        tap("dk", kT[:, :, :], [("df_kT", 0, 4 * NT)])
        tap("dq", qT[:, :, :], [("df_qT", 0, 4 * NT)])
        ud = WS.get(2)
        for i in range(2):
            S.add("act", lambda e, i=i: e.activation(out=wod[:, i, :], in_=ud[i][0], func=AF.Copy), reads=[W1("wbf", ud[i][1])], writes=[W1("df_wod", i)])

        PS_S = Pool_([0, 1, 2])
        PS_ACC = Pool_([4, 5])
        PS_N = Pool_([6, 7])
        ONEF = cst[:, C_ONE:C_ONE + 128]
        npt = 0
        ng = 0
        for bi, (t0, w) in enumerate(BTS):
            keys = list(range(NT)) if bi < 4 else [16, 17]
            nk = len(keys)
            S.add("sp", lambda e, t0=t0, w=w: e.dma_start(out=otl[:, :, 0:w], in_=d_ot[:, :, t0:t0 + w]), reads=[W1("otd")], writes=[W1("df_otl")], dma=True)
            S.add("pool", lambda e, t0=t0, w=w: e.tensor_copy(out=qx[0:64, :, 0, 0:w], in_=qT[0:64, :, t0:t0 + w]),
                  reads=[xsl("df_qT", h, t0, w) for h in range(4)] + [W1("df_qx")], writes=[W1("df_qx")])
            S.add("pool", lambda e, t0=t0, w=w: e.tensor_copy(out=qx[64:128, :, 1, 0:w], in_=qT[64:128, :, t0:t0 + w]),
                  reads=[xsl("df_qT", h, t0, w) for h in range(4)] + [W1("df_qx")], writes=[W1("df_qx")])
            LA = 2
            stream = []
            for h in range(4):
                for m in range(2):
                    g = ng % 2
                    ng += 1
                    acc, accn = PS_ACC.next()
                    for ki, kt in enumerate(keys):
                        stream.append((h, m, g, acc, accn, ki, kt))
            sps = {}

            def emit_S(idx, w=w):
                h, m, g, acc, accn, ki, kt = stream[idx]
                ps, psn = PS_S.next()
                sps[idx] = (ps, psn)
                S.add("pe", lambda e: e.matmul(ps[:, 0:w], lhsT=kT[:, h, 128 * kt:128 * kt + 128], rhs=qx[:, h, m, 0:w], start=True, stop=True),
                      reads=[xsl("df_kT", h, 128 * kt, 128), W1("df_qx")], writes=[W1(psn)])

            def epilogue(h, m, g, acc, accn, w=w):
                pn, pnn = PS_N.next()
                S.add("pe", lambda e: e.matmul(pn[:, 0:w], lhsT=ONEF, rhs=pacc[:, g, 0:w], start=True, stop=True),
                      reads=[W1("cst"), W1("df_pacc", g)], writes=[W1(pnn)])
                S.add("act", lambda e: e.activation(out=lns[:, g, 0:w], in_=pn[:, 0:w], func=AF.Ln), reads=[W1(pnn)], writes=[W1("df_lns", g)])
                S.add("act", lambda e: e.activation(out=rs[:, g, 0:w], in_=lns[:, g, 0:w], func=AF.Exp, scale=-1.0), reads=[W1("df_lns", g)], writes=[W1("df_rs", g)])
                hr = h % 2
                if m == 0:
                    S.add("dve", lambda e: e.tensor_tensor(out=tt[:, hr, 0:w], in0=acc[:, 0:w], in1=rs[:, g, 0:w], op=ALU.mult),
                          reads=[W1(accn), W1("df_rs", g)], writes=[W1("df_t", hr)])
                    return
                S.add("dve", lambda e: e.scalar_tensor_tensor(
                    out=oo[:, hr, 0:w], in0=rs[:, g, 0:w], scalar=lamv[:, 4:5], in1=acc[:, 0:w], op0=ALU.mult, op1=ALU.mult),
                    reads=[W1(accn), W1("df_rs", g), W1("df_lam")], writes=[W1("df_o", hr)])
                S.add("pool", lambda e: e.tensor_tensor(out=oo[:, hr, 0:w], in0=oo[:, hr, 0:w], in1=tt[:, hr, 0:w], op=ALU.add),
                      reads=[W1("df_o", hr), W1("df_t", hr)], writes=[W1("df_o", hr)])
                S.add("pool", lambda e: e.tensor_tensor(out=sqb[:, hr, 0:w], in0=oo[:, hr, 0:w], in1=oo[:, hr, 0:w], op=ALU.mult),
                      reads=[W1("df_o", hr)], writes=[W1("df_sqb", hr)])
                pn2, pnn2 = PS_N.next()
                S.add("pe", lambda e: e.matmul(pn2[:, 0:w], lhsT=ones_bf[:, :], rhs=sqb[:, hr, 0:w], start=True, stop=True),
                      reads=[W1("ones"), W1("df_sqb", hr)], writes=[W1(pnn2)])
                S.add("act", lambda e: e.activation(out=lns[:, g, 0:w], in_=pn2[:, 0:w], func=AF.Ln, scale=1.0 / 128, bias=EPS),
                      reads=[W1(pnn2), W1("df_lns", g)], writes=[W1("df_lns", g)])
                S.add("act", lambda e: e.activation(out=rs[:, g, 0:w], in_=lns[:, g, 0:w], func=AF.Exp, scale=-0.5),
                      reads=[W1("df_lns", g), W1("df_rs", g)], writes=[W1("df_rs", g)])
                S.add("dve", lambda e: e.scalar_tensor_tensor(
                    out=odT[:, h, 0:w], in0=oo[:, hr, 0:w], scalar=lamv[:, 5:6], in1=rs[:, g, 0:w], op0=ALU.mult, op1=ALU.mult),
                    reads=[W1("df_o", hr), W1("df_rs", g), W1("df_lam")], writes=[W1("df_odT", h)])

            pending = []
            for idx in range(min(LA, len(stream))):
                emit_S(idx)
            for idx, (h, m, g, acc, accn, ki, kt) in enumerate(stream):
                if idx + LA < len(stream):
                    emit_S(idx + LA)
                ps, psn = sps.pop(idx)
                pr = npt % 4
                npt += 1
                S.add("act", lambda e, ps=ps, pr=pr, w=w: e.activation(out=pt[:, pr, 0:w], in_=ps[:, 0:w], func=AF.Exp, scale=0.125),
                      reads=[W1(psn)], writes=[W1("df_pt", pr)])
                S.add("pe", lambda e, acc=acc, pr=pr, kt=kt, h=h, ki=ki, w=w, nk=nk: e.matmul(
                    acc[:, 0:w], lhsT=va[:, kt, 128 * h:128 * h + 128], rhs=pt[:, pr, 0:w], start=(ki == 0), stop=(ki == nk - 1)),
                    reads=[W1("df_pt", pr), ("df_va", kt, kt + 1)], writes=[W1(accn)])
                if ki == 0:
                    S.add("dve", lambda e, g=g, pr=pr, w=w: e.tensor_copy(out=pacc[:, g, 0:w], in_=pt[:, pr, 0:w]),
                          reads=[W1("df_pt", pr)], writes=[W1("df_pacc", g)])
                else:
                    S.add("dve", lambda e, g=g, pr=pr, w=w: e.tensor_tensor(out=pacc[:, g, 0:w], in0=pacc[:, g, 0:w], in1=pt[:, pr, 0:w], op=ALU.add),
                          reads=[W1("df_pt", pr), W1("df_pacc", g)], writes=[W1("df_pacc", g)])
                if pending and ki == min(3, nk - 1):
                    epilogue(*pending.pop(0))
                if ki == nk - 1:
                    pending.append((h, m, g, acc, accn))
            while pending:
                epilogue(*pending.pop(0))
            if "odT" in d_taps:
                S.add("sp", lambda e, t0=t0, w=w: e.dma_start(out=d_taps["odT"][:, :, t0:t0 + w], in_=odT[:, :, 0:w]), reads=[("df_odT", 0, 4)], writes=[W1("dram_out")], dma=True)
            tiles = [(dm, t0, w) for dm in range(8)]
            RP = ResidPass(l, 2, tiles, (xrd, "xr_d", 3))
            for dm in range(8):
                ps, psn = PS_S.next()

                def f(e, ps=ps, dm=dm, w=w):
                    for k in range(4):
                        off = 512 * k + 128 * (dm % 4)
                        e.matmul(ps[:, 0:w], lhsT=wrg[:, dm // 4, off:off + 128], rhs=otl[:, k, 0:w], start=(k == 0), stop=False)
                    for k in range(4):
                        off = 512 * k + 128 * (dm % 4)
                        ins = e.matmul(ps[:, 0:w], lhsT=wod[:, dm // 4, off:off + 128], rhs=odT[:, k, 0:w], start=False, stop=(k == 3))
                    return ins
                S.add("pe", f, reads=[("df_wod", dm // 4, dm // 4 + 1), ("df_wrg", dm // 4, dm // 4 + 1), ("df_odT", 0, 4), W1("df_otl")], writes=[W1(psn)])
                RP.evac(ps, psn)

    def ffn_phase(l):
        mk = A.mark()
        actT = A.alloc("ff_act", BF16, [12, T], nslots=12 * NT)
        wdn = A.alloc("ff_wdn", BF16, [6, UNIT], nslots=6)
        sg = A.alloc("ff_sg", F32, [2, 512], nslots=2)
        xrf = A.alloc("xr_f", F32, [8, 512], nslots=8)
        k = 0
        for (f0, nf) in FGROUPS:
            for fi in range(nf):
                (wap, ws), = WS.get(1)
                for (t0, w) in BTS:
                    pg, pgn = PA.next()
                    pu, pun = PA.next()

                    def f(e, pg=pg, pu=pu, wap=wap, t0=t0, w=w):
                        for kc in range(8):
                            e.matmul(pg[:, 0:w], lhsT=wap[:, 256 * kc:256 * kc + 128], rhs=hT[:, kc, t0:t0 + w], start=(kc == 0), stop=(kc == 7))
                        for kc in range(8):
                            ins = e.matmul(pu[:, 0:w], lhsT=wap[:, 256 * kc + 128:256 * kc + 256], rhs=hT[:, kc, t0:t0 + w], start=(kc == 0), stop=(kc == 7))
                        return ins
                    S.add("pe", f, reads=[W1("wbf", ws)] + xall("hT", t0, w), writes=[W1(pgn), W1(pun)])
                    r = k % 2
                    k += 1
                    S.add("act", lambda e, pg=pg, r=r, w=w: e.activation(out=sg[:, r, 0:w], in_=pg[:, 0:w], func=AF.Silu), reads=[W1(pgn)], writes=[W1("ff_sg", r)])
                    S.add("dve", lambda e, pu=pu, r=r, fi=fi, t0=t0, w=w: e.tensor_tensor(out=actT[:, fi, t0:t0 + w], in0=pu[:, 0:w], in1=sg[:, r, 0:w], op=ALU.mult),
                          reads=[W1(pun), W1("ff_sg", r)], writes=[xsl("ff_act", fi, t0, w)])
            nu = nf // 2
            for i in range(nu):
                (wap, ws), = WS.get(1)
                S.add("act", lambda e, i=i, wap=wap: e.activation(out=wdn[:, i, :], in_=wap, func=AF.Copy), reads=[W1("wbf", ws)], writes=[W1("ff_wdn", i)])
            tiles = [(dm, t0, w) for (t0, w) in BTS for dm in range(8)]
            RP = ResidPass(l, 5, tiles, (xrf, "xr_f", 8))
            for (dm, t0, w) in tiles:
                ps, psn = PA.next()

                def f(e, ps=ps, dm=dm, t0=t0, w=w, nf=nf):
                    for fi in range(nf):
                        off = 1024 * (fi % 2) + 128 * dm
                        ins = e.matmul(ps[:, 0:w], lhsT=wdn[:, fi // 2, off:off + 128], rhs=actT[:, fi, t0:t0 + w], start=(fi == 0), stop=(fi == nf - 1))
                    return ins
                S.add("pe", f, reads=[("ff_wdn", 0, nu)] + [xsl("ff_act", fi, t0, w) for fi in range(nf)], writes=[W1(psn)])
                RP.evac(ps, psn)
        A.release(mk)

    for l in range(depth):
        lam_init = 0.8 - 0.6 * math.exp(-0.3 * l)
        layer_params(l)
        rmsnorm_mod(l, 0)
        tap("h%d" % l, hT[:, :, :], [("hT", 0, 8 * NT)])
        mkL = A.mark()
        oT = A.alloc("oT", BF16, [4, T], nslots=4 * NT)
        linattn_phase(l, oT)
        S.add("sp", lambda e: e.dma_start(out=d_ot[:, :, :], in_=oT[:, :, :]), reads=[("oT", 0, 4 * NT)], writes=[W1("otd")], dma=True)
        A.release(mkL)
        mkD = A.mark()
        diff_phase(l, lam_init)
        A.release(mkD)
        tap("xattn%d" % l, d_xd, [("xd", 0, 8 * NT)])
        rmsnorm_mod(l, 1)
        ffn_phase(l)
        tap("x%d" % l, d_xd, [("xd", 0, 8 * NT)])

    mk = A.mark()
    xt = A.alloc("f_xt", F32, [2, 8, 512], nslots=2)
    sq = A.alloc("f_sq", BF16, [8, 512])
    lnv = A.alloc("f_ln", F32, [512])
    rstd = A.alloc("f_rstd", F32, [512])
    out_v = d_out.rearrange("(c p) t -> p c t", p=128)
    for bi, (t0, w) in enumerate(BTS[:4]):
        r = bi % 2
        S.add("sp", lambda e, r=r, t0=t0, w=w: e.dma_start(out=xt[:, r, :, 0:w], in_=xd_v[:, :, t0:t0 + w]),
              reads=xall("xd", t0, w), writes=[W1("f_xt", r)], dma=True)
        norm_stats(xt[:, r, :, :], [W1("f_xt", r)], sq, "f_sq", lnv, "f_ln", rstd, "f_rstd", w)
        for c in range(8):
            S.add("dve", lambda e, c=c, r=r, w=w: e.scalar_tensor_tensor(out=xt[:, r, c, 0:w], in0=xt[:, r, c, 0:w], scalar=fnw[:, c:c + 1], in1=rstd[:, 0:w],
                                                                  op0=ALU.mult, op1=ALU.mult),
                  reads=[W1("f_xt", r), W1("fnw"), W1("f_rstd")], writes=[W1("f_xt", r)])
        S.add("sp", lambda e, r=r, t0=t0, w=w: e.dma_start(out=out_v[:, :, t0:t0 + w], in_=xt[:, r, :, 0:w]), reads=[W1("f_xt", r)], writes=[W1("dram_out")], dma=True)
    A.release(mk)
    S.add("sp", None, reads=[W1("dram_out")])
    S.emit(nc, es)
    es.close()
    return nc


_CACHE = {}


def _host_inputs(inp):
    wpk = _pack_weights(np.asarray(inp["w_in"], np.float32), np.asarray(inp["w_out"], np.float32),
                        np.asarray(inp["w_ffn_in"], np.float32), np.asarray(inp["w_ffn_out"], np.float32))
    npi = {k: np.asarray(v, np.float32) for k, v in inp.items()}
    lp = _layer_params(npi)
    cst = _const_pack()
    rope = _rope_tables()
    fnw = np.ascontiguousarray(npi["final_norm_w"].reshape(8, 128).T)
    wada = np.ascontiguousarray(npi["w_ada"])
    maps = []
    for b in range(8):
        xin = np.ascontiguousarray(np.concatenate([npi["x"][b].T, npi["ctx"][b].T], axis=1))
        cin = np.stack([npi["c"][b].reshape(8, 128).T, npi["c_ctx"].reshape(8, 128).T], axis=2)
        maps.append({"xin": xin, "cin": np.ascontiguousarray(cin), "wada": wada, "wpk": wpk, "lp": lp,
                     "fnw": fnw, "cst": cst, "rope": rope})
    return maps


def kernel(**inputs):
    maps = _host_inputs(inputs)
    if "nc" not in _CACHE:
        _CACHE["nc"] = build()
    res = run_bass_kernel_spmd(_CACHE["nc"], maps, core_ids=list(range(8)))
    out = np.stack([np.ascontiguousarray(r["out"].T) for r in res.results], axis=0)
    return out.astype(np.float32)
```

```python
import math
import numpy as np
from contextlib import ExitStack
import concourse.bass as bass
import concourse.mybir as mybir
from concourse.bass_utils import run_bass_kernel_spmd

F32 = mybir.dt.float32
BF16 = mybir.dt.bfloat16
AF = mybir.ActivationFunctionType
ALU = mybir.AluOpType
AX = mybir.AxisListType

D = 1024
NLAT = 2048
NCTX = 256
T = NLAT + NCTX
NT = T // 128
DEPTH = 4
DFF = 2816
NF = DFF // 128
PROJW = 3104
EPS = 1e-6
GN_EPS = 1e-5
BTS = [(0, 512), (512, 512), (1024, 512), (1536, 512), (2048, 256)]
UNIT = 2048
FGROUPS = [(0, 12), (12, 10)]

ENGS = ("pe", "act", "dve", "pool", "sp")


class Op:
    __slots__ = ("eng", "fn", "deps", "sig", "sem", "sigval", "idx", "dma", "dmak")

    def __init__(self, eng, fn, dma, idx):
        self.eng = eng
        self.fn = fn
        self.dma = dma
        self.idx = idx
        self.deps = {}
        self.sig = False
        self.sem = None
        self.sigval = 0
        self.dmak = 0


class Sched:
    def __init__(self):
        self.ops = []
        self.st = {}
        self.ndma = 0

    def newbuf(self, name, nslots=1, inherit=()):
        assert name not in self.st, name
        inh = frozenset(inherit)
        self.st[name] = [[None, {}, inh] for _ in range(nslots)]

    def final_ops(self, name):
        out = set()
        for w, rd, inh in self.st[name]:
            if w is not None:
                out.add(w)
            out.update(rd.values())
            out.update(inh)
        return out

    def delbuf(self, name):
        del self.st[name]

    def add(self, eng, fn, reads=(), writes=(), dma=False):
        op = Op(eng, fn, dma, len(self.ops))
        deps = op.deps
        for (name, lo, hi) in reads:
            st = self.st[name]
            assert 0 <= lo < hi <= len(st), (name, lo, hi, len(st))
            for s in range(lo, hi):
                w = st[s][0]
                if w is not None:
                    deps[w] = "raw"
                for o in st[s][2]:
                    deps.setdefault(o, "raw")
        for (name, lo, hi) in writes:
            st = self.st[name]
            assert 0 <= lo < hi <= len(st), (name, lo, hi, len(st))
            for s in range(lo, hi):
                w = st[s][0]
                if w is not None and w is not op:
                    deps.setdefault(w, "waw")
                for r in st[s][1].values():
                    if r is not op:
                        deps.setdefault(r, "war")
                for o in st[s][2]:
                    deps.setdefault(o, "war")
        rk = ("d", op.idx) if dma else eng
        for (name, lo, hi) in reads:
            st = self.st[name]
            for s in range(lo, hi):
                st[s][1][rk] = op
        for (name, lo, hi) in writes:
            st = self.st[name]
            for s in range(lo, hi):
                st[s][0] = op
                st[s][1] = {}
                st[s][2] = frozenset()
        for o in list(deps):
            if (not o.dma) and (not dma) and o.eng == eng and deps[o] != "raw":
                del deps[o]
        for o in deps:
            o.sig = True
        if dma:
            op.sig = True
        self.ops.append(op)
        return op

    def emit(self, nc, es):
        NDS = 16
        sems = {e: es.enter_context(nc.semaphore("s_" + e)) for e in ENGS}
        dsems = [es.enter_context(nc.semaphore("d%d" % i)) for i in range(NDS)]
        cnt = {e: 0 for e in ENGS}
        dcnt = [0] * NDS
        k = 0
        for op in self.ops:
            if not op.sig:
                continue
            if op.dma:
                j = k % NDS
                k += 1
                dcnt[j] += 16
                op.sem = dsems[j]
                op.sigval = dcnt[j]
                op.dmak = dcnt[j] - 16
            else:
                cnt[op.eng] += 1
                op.sem = sems[op.eng]
                op.sigval = cnt[op.eng]
        per = {e: [] for e in ENGS}
        for op in self.ops:
            per[op.eng].append(op)
        block = es.enter_context(nc.Block())

        def run(e, ops):
            waited = {}
            for op in ops:
                need = {}
                for o in op.deps:
                    key = id(o.sem)
                    if key not in need or need[key][1] < o.sigval:
                        need[key] = (o.sem, o.sigval)
                for key, (sem, val) in need.items():
                    if waited.get(key, 0) < val:
                        e.wait_ge(sem, val)
                        waited[key] = val
                if op.fn is None:
                    continue
                if op.dma and op.dmak > 0 and waited.get(id(op.sem), 0) < op.dmak:
                    e.wait_ge(op.sem, op.dmak)
                    waited[id(op.sem)] = op.dmak
                inst = op.fn(e)
                if op.sig:
                    inst.then_inc(op.sem, 16 if op.dma else 1)

        block.tensor(lambda e: run(e, per["pe"]))
        block.scalar(lambda e: run(e, per["act"]))
        block.vector(lambda e: run(e, per["dve"]))
        block.gpsimd(lambda e: run(e, per["pool"]))
        def run_sp(e):
            run(e, per["sp"])
            for j in range(NDS):
                if dcnt[j]:
                    e.wait_ge(dsems[j], dcnt[j])
        block.sync(run_sp)


class Arena:
    def __init__(self, sb, sched, nwords):
        self.sb = sb
        self.S = sched
        self.nwords = nwords
        self.top = 0
        self.stack = []
        self.retired = []

    def alloc(self, name, dtype, fshape, nslots=1):
        nel = 1
        for d in fshape:
            nel *= d
        nbytes = nel * (2 if dtype == BF16 else 4)
        nw = (nbytes + 3) // 4
        nw = (nw + 15) // 16 * 16
        off = self.top
        assert off + nw <= self.nwords, ("SBUF arena overflow", name, off, nw, self.nwords)
        self.top += nw
        inherit = set()
        for (lo, hi, ops) in self.retired:
            if lo < off + nw and hi > off:
                inherit |= ops
        self.S.newbuf(name, nslots, inherit)
        self.stack.append((name, off, nw))
        ap = self.sb[:, off:off + nw]
        if dtype == BF16:
            ap = ap.bitcast(BF16)
        ap = ap[:, 0:nel]
        if len(fshape) == 2:
            ap = ap.rearrange("p (a b) -> p a b", b=fshape[1])
        elif len(fshape) == 3:
            ap = ap.rearrange("p (a b c) -> p a b c", b=fshape[1], c=fshape[2])
        elif len(fshape) == 4:
            ap = ap.rearrange("p (a b c d) -> p a b c d", b=fshape[1], c=fshape[2], d=fshape[3])
        return ap

    def mark(self):
        return len(self.stack)

    def release(self, mark):
        while len(self.stack) > mark:
            name, off, nw = self.stack.pop()
            self.retired.append((off, off + nw, self.S.final_ops(name)))
            self.S.delbuf(name)
            self.top = off


def _swap_cols(base, nheads_blocks, blk):
    idx = np.arange(nheads_blocks * blk)
    b = idx // blk
    d = idx % blk
    return base + b * blk + (d + blk // 2) % blk


def _proj_units():
    u = []
    rq = np.arange(0, 128)
    rk = np.arange(128, 256)
    u.append(np.concatenate([rq, rk]))
    u.append(np.concatenate([_swap_cols(0, 4, 32), _swap_cols(128, 4, 32)]))
    u.append(np.arange(256, 512))
    u.append(np.arange(512, 768))
    u.append(np.arange(768, 1024))
    g1 = -np.ones(256, np.int64)
    g1[:32] = np.arange(1536, 1568)
    u.append(g1)
    u.append(np.arange(1024, 1280))
    u.append(np.arange(1280, 1536))
    u.append(np.arange(2080, 2336))
    u.append(np.arange(2336, 2592))
    sw = _swap_cols(2080, 16, 32)
    u.append(sw[:256])
    u.append(sw[256:])
    u.append(np.arange(1568, 1824))
    u.append(np.arange(1824, 2080))
    sw = _swap_cols(1568, 16, 32)
    u.append(sw[:256])
    u.append(sw[256:])
    u.append(np.arange(2592, 2848))
    u.append(np.arange(2848, 3104))
    return u


NPU = 18
NUL = NPU + 4 + 22 + 11


def _pack_weights(w_in, w_out, w_ffn_in, w_ffn_out):
    L = w_in.shape[0]
    out = np.zeros((L, 128, NUL, UNIT), np.float32)
    units = _proj_units()
    for l in range(L):
        wi = w_in[l].reshape(8, 128, PROJW)
        wo = w_out[l].reshape(8, 128, D)
        fi = w_ffn_in[l].reshape(8, 128, 2 * DFF)
        fo = w_ffn_out[l].reshape(NF, 128, D)
        k = 0
        def wout_units(part, k):
            for j in range(2):
                blk = wo[4 * part:4 * part + 4, :, 512 * j:512 * j + 512].transpose(1, 0, 2)
                out[l, :, k, :] = blk.reshape(128, UNIT)
                k += 1
            return k
        for ui, cols in enumerate(units):
            if ui == 8:
                k = wout_units(0, k)
            blk = np.zeros((128, 8, 256), np.float32)
            ok = cols >= 0
            blk[:, :, ok] = wi[:, :, cols[ok]].transpose(1, 0, 2)
            out[l, :, k, :] = blk.reshape(128, UNIT)
            k += 1
        k = wout_units(1, k)
        for (f0, nf) in FGROUPS:
            for f in range(f0, f0 + nf):
                blk = np.concatenate([fi[:, :, 128 * f:128 * f + 128],
                                      fi[:, :, DFF + 128 * f:DFF + 128 * f + 128]], axis=2)
                out[l, :, k, :] = blk.transpose(1, 0, 2).reshape(128, UNIT)
                k += 1
            for f in range(f0, f0 + nf, 2):
                blk = fo[f:f + 2].transpose(1, 0, 2)
                out[l, :, k, :] = blk.reshape(128, UNIT)
                k += 1
        assert k == NUL
    return out


C_ID = 0
C_SDM = 128
C_LT = 384
C_UT = 512
C_RPOS = 640
C_RNEG = 768
C_IP1 = 896
C_IREV = 1024
C_JC = 1152
C_ONE = 1168
NCS = 1296
C_BM4 = 1296
NCONST = 1808


def _const_pack():
    c = np.zeros((128, NCONST), np.float32)
    p = np.arange(128)
    c[:, C_ID:C_ID + 128] = np.eye(128)
    bm = np.zeros((128, 4, 128), np.float32)
    for h in range(4):
        bm[32 * h:32 * h + 32, h, :] = 1.0
    c[:, C_BM4:C_BM4 + 512] = bm.reshape(128, 512)
    sd = np.zeros((128, 4, 64), np.float32)
    for h in range(4):
        sd[32 * h:32 * h + 32, h, :] = 1.0
    c[:, C_SDM:C_SDM + 256] = sd.reshape(128, 256)
    j = p[:, None]
    i = p[None, :]
    c[:, C_LT:C_LT + 128] = (j <= i)
    c[:, C_UT:C_UT + 128] = (j >= i)
    c[:, C_RPOS:C_RPOS + 128] = np.maximum(i - j, 0)
    c[:, C_RNEG:C_RNEG + 128] = np.maximum(j - i, 0)
    c[:, C_IP1:C_IP1 + 128] = np.broadcast_to(i + 1, (128, 128))
    c[:, C_IREV:C_IREV + 128] = np.broadcast_to(128 - i, (128, 128))
    c[:, C_JC] = 127 - p
    c[:, C_JC + 1] = p
    c[:, C_ONE:C_ONE + 128] = 1.0
    return c


def _rope_tables():
    t = np.arange(NLAT, dtype=np.float32)
    p = np.arange(128)
    tab = np.zeros((2, 128, 2, NLAT), np.float32)
    invf = (1.0 / (np.float32(10000.0) ** np.linspace(0.0, 1.0, 16, dtype=np.float32))).astype(np.float32)
    d = p % 32
    ang = t[None, :] * invf[d % 16][:, None]
    tab[0, :, 0] = np.cos(ang)
    tab[0, :, 1] = np.sin(ang) * np.where(d < 16, -1.0, 1.0)[:, None]
    half = 16
    invf2 = (1.0 / (np.float32(10000.0) ** (np.arange(half, dtype=np.float32) / half))).astype(np.float32)
    e = p % 64
    part = e // 32
    d = e % 32
    row = (np.arange(NLAT) // 64).astype(np.float32)
    col = (np.arange(NLAT) % 64).astype(np.float32)
    pos = np.where(part[:, None] == 0, row[None, :], col[None, :]).astype(np.float32)
    ang = pos * invf2[d % 16][:, None]
    tab[1, :, 0] = np.cos(ang)
    tab[1, :, 1] = np.sin(ang) * np.where(d < 16, -1.0, 1.0)[:, None]
    tab = tab.reshape(2, 128, 2, 4, 512).transpose(0, 1, 3, 2, 4)
    return np.ascontiguousarray(tab)


P_N1 = 0
P_N2 = 8
P_BADA = 16
P_RDLP = 64
P_RDLB = 66
P_BG = 74
P_WG = 76
P_GNW = 332
P_DLAM = 588
P_DNWC = 844
NLP = 848


def _layer_params(inp):
    L = DEPTH
    o = np.zeros((L, 128, NLP), np.float32)
    p = np.arange(128)
    for l in range(L):
        o[l, :, P_N1:P_N1 + 8] = inp["norm1_w"][l].reshape(8, 128).T
        o[l, :, P_N2:P_N2 + 8] = inp["norm2_w"][l].reshape(8, 128).T
        o[l, :, P_BADA:P_BADA + 48] = inp["b_ada"][l].reshape(48, 128).T
        o[l, :, P_RDLP:P_RDLP + 2] = inp["ret_decay_logit"][l][:, p // 32].T
        o[l, :, P_RDLB:P_RDLB + 8] = inp["ret_decay_logit"][l].reshape(1, 8)
        o[l, :, P_BG:P_BG + 2] = inp["gla_b_gate"][l].T
        o[l, 0:16, P_WG:P_WG + 128] = inp["gla_w_gate"][l, 0]
        o[l, 16:32, P_WG + 128:P_WG + 256] = inp["gla_w_gate"][l, 1]
        o[l, :, P_GNW:P_GNW + 256] = np.tile(inp["gla_norm_w"][l], 4)[None, :]
        o[l, :, P_DLAM:P_DLAM + 256] = inp["diff_lambda"][l].reshape(1, 256)
        o[l, :, P_DNWC] = inp["diff_norm_w"][l]
    return o


def build(depth=DEPTH, taps=()):
    nc = bass.Bass("TRN2", target_bir_lowering=False)
    dt = lambda n, s, kind="ExternalInput": nc.dram_tensor(n, list(s), F32, kind=kind).ap()
    d_x = dt("xin", [D, T])
    d_c = dt("cin", [128, 8, 2])
    d_wada = dt("wada", [DEPTH, D, 6 * D])
    d_wpk = dt("wpk", [DEPTH, 128, NUL, UNIT])
    d_lp = dt("lp", [DEPTH, 128, NLP])
    d_fn = dt("fnw", [128, 8])
    d_const = dt("cst", [128, NCONST])
    d_rope = dt("rope", [2, 128, 4, 2, 512])
    d_out = dt("out", [D, NLAT], kind="ExternalOutput")
    d_xd = dt("xd", [D, T], kind="Internal")
    d_ot = nc.dram_tensor("otrg", [128, 4, T], BF16, kind="Internal").ap()
    d_taps = {}
    for (name, shape, tdt) in taps:
        d_taps[name] = nc.dram_tensor("tap_" + name, list(shape), tdt, kind="ExternalOutput").ap()
    xd_v = d_xd.rearrange("(c p) t -> p c t", p=128)

    S = Sched()
    es = ExitStack()
    NW = 52800
    sb = es.enter_context(nc.sbuf_tensor("sb", [128, NW], F32))
    A = Arena(sb, S, NW)
    psb = []
    PSLOTS = {}
    for i in range(8):
        psb.append(es.enter_context(nc.psum_tensor("ps%d" % i, [128, 512], F32)))
        S.newbuf("ps%d" % i, PSLOTS.get("ps%d" % i, 1))
    S.newbuf("dram_out", 1)
    S.newbuf("dram_in", 1)
    S.newbuf("xd", 8 * NT)
    S.newbuf("otd", 1)

    class Pool_:
        def __init__(self, banks):
            self.banks = banks
            self.i = 0

        def next(self):
            b = self.banks[self.i % len(self.banks)]
            self.i += 1
            return psb[b], "ps%d" % b

    def W1(name, s=0, n=1):
        if name in PSLOTS and s == 0 and n == 1:
            return (name, 0, PSLOTS[name])
        return (name, s, s + n)

    hT = A.alloc("hT", BF16, [8, T], nslots=8 * NT)
    cst = A.alloc("cst", F32, [NCS])
    ident = A.alloc("ident", BF16, [128])
    ones_bf = A.alloc("ones", BF16, [128])
    bm4 = A.alloc("bm4", BF16, [4, 128])
    modv = A.alloc("modv", F32, [DEPTH, 48, 2])
    lpar = A.alloc("lpar", F32, [NLP])
    drv = A.alloc("drv", F32, [2, 2, 8])
    fnw = A.alloc("fnw", F32, [8])
    NST, NBF = 2, 4
    wst = A.alloc("wst", F32, [NST, UNIT], nslots=NST)
    wbf = A.alloc("wbf", BF16, [NBF, UNIT], nslots=NBF)

    def xs(c, t0, w):
        return (c * NT + t0 // 128, c * NT + (t0 + w + 127) // 128)

    def xsl(name, c, t0, w):
        lo, hi = xs(c, t0, w)
        return (name, lo, hi)

    def xall(name, t0, w):
        return [xsl(name, c, t0, w) for c in range(8)]

    class WStream:
        def __init__(self, total):
            self.total = total
            self.nd = 0
            self.ncst = 0
            self.ng = 0

        def _dma(self, j):
            l, u = divmod(j, NUL)
            s = j % NST
            S.add("sp", lambda e, l=l, u=u, s=s: e.dma_start(out=wst[:, s, :], in_=d_wpk[l, :, u, :]),
                  reads=[W1("dram_in")], writes=[W1("wst", s)], dma=True)

        def _cast(self, j):
            s = j % NST
            b = j % NBF
            S.add("pool", lambda e, s=s, b=b: e.tensor_copy(out=wbf[:, b, :], in_=wst[:, s, :]),
                  reads=[W1("wst", s)], writes=[W1("wbf", b)])

        def get(self, n=1):
            i = self.ng
            self.ng += n
            assert n <= NBF and self.ng <= self.total
            last_cast = min(i + NBF - 1, self.total - 1)
            while self.ncst <= last_cast:
                while self.nd <= min(self.ncst + NST - 1, self.total - 1):
                    self._dma(self.nd)
                    self.nd += 1
                self._cast(self.ncst)
                self.ncst += 1
            return [(wbf[:, (i + k) % NBF, :], (i + k) % NBF) for k in range(n)]

    WS = WStream(depth * NUL)

    S.add("sp", lambda e: e.dma_start(out=cst[:, :], in_=d_const[:, 0:NCS]), reads=[W1("dram_in")], writes=[W1("cst")], dma=True)
    S.add("sp", lambda e: e.dma_start(out=fnw[:, :], in_=d_fn[:, :]), reads=[W1("dram_in")], writes=[W1("fnw")], dma=True)
    for c in range(8):
        S.add("sp", lambda e, c=c: e.dma_start(out=d_xd[128 * c:128 * c + 128, :], in_=d_x[128 * c:128 * c + 128, :]),
              reads=[W1("dram_in")], writes=[("xd", c * NT, (c + 1) * NT)], dma=True)
    S.add("dve", lambda e: e.tensor_copy(out=ident[:, :], in_=cst[:, C_ID:C_ID + 128]), reads=[W1("cst")], writes=[W1("ident")])
    S.add("dve", lambda e: e.memset(ones_bf[:, :], 1.0), writes=[W1("ones")])
    mkb = A.mark()
    bmf = A.alloc("bmf", F32, [512])
    S.add("sp", lambda e: e.dma_start(out=bmf[:, :], in_=d_const[:, C_BM4:C_BM4 + 512]), reads=[W1("dram_in")], writes=[W1("bmf")], dma=True)
    S.add("dve", lambda e: e.tensor_copy(out=bm4[:, :, :], in_=bmf[:, :].rearrange("p (a b) -> p a b", b=128)),
          reads=[W1("bmf")], writes=[W1("bm4")])
    A.release(mkb)

    mk = A.mark()
    condT = A.alloc("condT", F32, [8, 2])
    ast = A.alloc("ast", F32, [2, 8, 512], nslots=2)
    mrow = A.alloc("mrow", F32, [6 * D])
    S.add("sp", lambda e: e.dma_start(out=condT[:, :, :], in_=d_c[:, :, :]), reads=[W1("dram_in")], writes=[W1("condT")], dma=True)
    S.add("act", lambda e: e.activation(out=condT[:, :, :], in_=condT[:, :, :], func=AF.Silu), reads=[W1("condT")], writes=[W1("condT")])
    mps, mpsn = psb[7], "ps7"
    PR = Pool_([0, 1, 2, 3])
    nslab = 0
    for l in range(depth):
        wv = d_wada[l].rearrange("(kc p) n -> p kc n", p=128)
        for s in range(12):
            r = nslab % 2
            nslab += 1
            S.add("sp", lambda e, r=r, s=s, wv=wv: e.dma_start(out=ast[:, r, :, :], in_=wv[:, :, 512 * s:512 * s + 512]),
                  reads=[W1("dram_in")], writes=[W1("ast", r)], dma=True)
            ps, psn = PR.next()

            def f(e, r=r, ps=ps):
                for kc in range(8):
                    ins = e.matmul(ps[0:2, :], lhsT=condT[:, kc, :], rhs=ast[:, r, kc, :], start=(kc == 0), stop=(kc == 7))
                return ins
            S.add("pe", f, reads=[W1("ast", r), W1("condT")], writes=[W1(psn)])
            S.add("act", lambda e, ps=ps, s=s: e.activation(out=mrow[0:2, 512 * s:512 * s + 512], in_=ps[0:2, :], func=AF.Copy),
                  reads=[W1(psn)], writes=[W1("mrow")])

        def g(e, l=l):
            for j in range(48):
                col = (l * 48 + j) * 2
                ins = e.transpose(mps[:, col:col + 2], mrow[0:2, 128 * j:128 * j + 128], cst[0:2, C_ID:C_ID + 2])
            return ins
        S.add("pe", g, reads=[W1("mrow"), W1("cst")], writes=[W1(mpsn)])
    S.add("dve", lambda e: e.tensor_copy(out=modv[:, 0:depth, :, :].rearrange("p l j w -> p (l j w)"), in_=mps[:, 0:depth * 96]),
          reads=[W1(mpsn)], writes=[W1("modv")])
    A.release(mk)

    PA = Pool_([0, 1, 2, 3])
    PB = Pool_([4, 5])
    PC = Pool_([6, 7])

    def tap(name, ap, reads):
        if name in d_taps:
            S.add("sp", lambda e: e.dma_start(out=d_taps[name], in_=ap), reads=reads, writes=[W1("dram_out")], dma=True)

    def norm_stats(xt, xtn_reads, sq, sqn, lnv, lnn, rstd, rsn, w):
        S.add("act", lambda e: e.activation(out=sq[:, :, 0:w], in_=xt[:, :, 0:w], func=AF.Square), reads=xtn_reads, writes=[W1(sqn)])
        ps, psn = PA.next()

        def f(e):
            for c in range(8):
                ins = e.matmul(ps[:, 0:w], lhsT=ones_bf[:, :], rhs=sq[:, c, 0:w], start=(c == 0), stop=(c == 7))
            return ins
        S.add("pe", f, reads=[W1(sqn), W1("ones")], writes=[W1(psn)])
        S.add("act", lambda e: e.activation(out=lnv[:, 0:w], in_=ps[:, 0:w], func=AF.Ln, scale=1.0 / D, bias=EPS), reads=[W1(psn)], writes=[W1(lnn)])
        S.add("act", lambda e: e.activation(out=rstd[:, 0:w], in_=lnv[:, 0:w], func=AF.Exp, scale=-0.5), reads=[W1(lnn)], writes=[W1(rsn)])

    def rmsnorm_mod(l, which):
        mk = A.mark()
        xt = A.alloc("n_xt", F32, [2, 8, 512], nslots=2)
        sq = A.alloc("n_sq", BF16, [8, 512])
        lnv = A.alloc("n_ln", F32, [512])
        rstd = A.alloc("n_rstd", F32, [512])
        tmp = A.alloc("n_tmp", F32, [2, 512], nslots=2)
        k = 0
        for bi, (t0, w) in enumerate(BTS):
            wi = 1 if bi == 4 else 0
            xr_ = bi % 2
            S.add("sp", lambda e, xr_=xr_, t0=t0, w=w: e.dma_start(out=xt[:, xr_, :, 0:w], in_=xd_v[:, :, t0:t0 + w]),
                  reads=xall("xd", t0, w), writes=[W1("n_xt", xr_)], dma=True)
            norm_stats(xt[:, xr_, :, :], [W1("n_xt", xr_)], sq, "n_sq", lnv, "n_ln", rstd, "n_rstd", w)
            for c in range(8):
                r = k % 2
                k += 1
                S.add("dve", lambda e, c=c, r=r, xr_=xr_, w=w, wi=wi: e.scalar_tensor_tensor(
                    out=tmp[:, r, 0:w], in0=xt[:, xr_, c, 0:w], scalar=drv[:, which, wi, c:c + 1], in1=rstd[:, 0:w],
                    op0=ALU.mult, op1=ALU.mult),
                    reads=[W1("n_xt", xr_), W1("drv"), W1("n_rstd")], writes=[W1("n_tmp", r)])
                shj = (0 if which == 0 else 3) * 8 + c
                S.add("act", lambda e, c=c, r=r, t0=t0, w=w, wi=wi, shj=shj: e.activation(
                    out=hT[:, c, t0:t0 + w], in_=tmp[:, r, 0:w], func=AF.Identity, bias=modv[:, l, shj, wi:wi + 1]),
                    reads=[W1("n_tmp", r), W1("modv")], writes=[xsl("hT", c, t0, w)])
        A.release(mk)

    def layer_params(l):
        S.add("sp", lambda e: e.dma_start(out=lpar[:, :], in_=d_lp[l, :, :]), reads=[W1("dram_in")], writes=[W1("lpar")], dma=True)
        for wi in range(2):
            S.add("dve", lambda e, wi=wi: e.tensor_tensor(out=modv[:, l, :, wi], in0=modv[:, l, :, wi], in1=lpar[:, P_BADA:P_BADA + 48], op=ALU.add),
                  reads=[W1("modv"), W1("lpar")], writes=[W1("modv")])
        for which in range(2):
            sj = (1 if which == 0 else 4) * 8
            nw0 = P_N1 if which == 0 else P_N2
            for wi in range(2):
                S.add("dve", lambda e, which=which, wi=wi, sj=sj, nw0=nw0: e.scalar_tensor_tensor(
                    out=drv[:, which, wi, :], in0=modv[:, l, sj:sj + 8, wi], scalar=1.0, in1=lpar[:, nw0:nw0 + 8],
                    op0=ALU.add, op1=ALU.mult),
                    reads=[W1("modv"), W1("lpar")], writes=[W1("drv")])

    class ResidPass:
        def __init__(self, l, gate_which, tiles, ring):
            self.l = l
            self.g = gate_which
            self.tiles = tiles
            self.i = 0
            self.il = 0
            self.xr, self.xrn, self.n = ring
            self.LA = self.n - 1

        def _load(self, k):
            dm, t0, w = self.tiles[k]
            r = k % self.n
            xr, xrn = self.xr, self.xrn
            S.add("sp", lambda e: e.dma_start(out=xr[:, r, 0:w], in_=xd_v[:, dm, t0:t0 + w]),
                  reads=[xsl("xd", dm, t0, w)], writes=[W1(xrn, r)], dma=True)

        def prefetch(self):
            while self.il <= min(self.i + self.LA, len(self.tiles) - 1):
                self._load(self.il)
                self.il += 1

        def evac(self, ps, psn):
            self.prefetch()
            dm, t0, w = self.tiles[self.i]
            r = self.i % self.n
            self.i += 1
            gj = self.g * 8 + dm
            wi = 1 if t0 >= NLAT else 0
            l = self.l
            xr, xrn = self.xr, self.xrn
            S.add("dve", lambda e: e.scalar_tensor_tensor(
                out=xr[:, r, 0:w], in0=ps[:, 0:w], scalar=modv[:, l, gj, wi:wi + 1], in1=xr[:, r, 0:w],
                op0=ALU.mult, op1=ALU.add),
                reads=[W1(psn), W1("modv"), W1(xrn, r)], writes=[W1(xrn, r)])
            S.add("sp", lambda e: e.dma_start(out=xd_v[:, dm, t0:t0 + w], in_=xr[:, r, 0:w]),
                  reads=[W1(xrn, r)], writes=[xsl("xd", dm, t0, w)], dma=True)

    def proj_fm(unit, ucol, M, evac, tiles=BTS):
        wap, ws = unit
        for (t0, w) in tiles:
            ps, psn = PA.next()

            def f(e, ps=ps, t0=t0, w=w):
                for kc in range(8):
                    ins = e.matmul(ps[0:M, 0:w], lhsT=wap[:, 256 * kc + ucol:256 * kc + ucol + M], rhs=hT[:, kc, t0:t0 + w],
                                   start=(kc == 0), stop=(kc == 7))
                return ins
            S.add("pe", f, reads=[W1("wbf", ws)] + xall("hT", t0, w), writes=[W1(psn)])
            evac(ps, psn, t0, w)

    def proj_tm(ulist, evac):
        (w0, s0), (w1, s1) = ulist
        for j in range(NT):
            ps, psn = PA.next()

            def f(e, ps=ps, j=j):
                for half, wap in ((0, w0), (1, w1)):
                    for kc in range(8):
                        ins = e.matmul(ps[:, 256 * half:256 * half + 256], lhsT=hT[:, kc, 128 * j:128 * j + 128],
                                       rhs=wap[:, 256 * kc:256 * kc + 256], start=(kc == 0), stop=(kc == 7))
                return ins
            S.add("pe", f, reads=[W1("wbf", s0), W1("wbf", s1)] + xall("hT", 128 * j, 128), writes=[W1(psn)])
            evac(ps, psn, j)

    def rope_proj(kind, umain, uswap, ucol, dsl, slotf):
        mk = A.mark()
        tab = A.alloc("rp_tab", F32, [2, 2, 512], nslots=2)
        t1 = A.alloc("rp_t1", F32, [2, 512], nslots=2)
        t2 = A.alloc("rp_t2", F32, [2, 512], nslots=2)
        for bi, (t0, w) in enumerate(BTS):
            if bi == 4:
                def ev(ps, psn, t0, w):
                    S.add("act", lambda e: e.activation(out=dsl(t0, w), in_=ps[:, 0:w], func=AF.Copy),
                          reads=[W1(psn)], writes=[slotf(t0, w)])
                proj_fm(umain, ucol, 128, ev, tiles=[(t0, w)])
                continue
            r = bi % 2
            S.add("sp", lambda e, r=r, bi=bi: e.dma_start(out=tab[:, r, :, :], in_=d_rope[kind, :, bi, :, :]),
                  reads=[W1("dram_in")], writes=[W1("rp_tab", r)], dma=True)

            def ev1(ps, psn, t0, w, r=r):
                S.add("dve", lambda e: e.tensor_tensor(out=t1[:, r, 0:w], in0=ps[:, 0:w], in1=tab[:, r, 0, 0:w], op=ALU.mult),
                      reads=[W1(psn), W1("rp_tab", r)], writes=[W1("rp_t1", r)])

            def ev2(ps, psn, t0, w, r=r):
                S.add("dve", lambda e: e.tensor_tensor(out=t2[:, r, 0:w], in0=ps[:, 0:w], in1=tab[:, r, 1, 0:w], op=ALU.mult),
                      reads=[W1(psn), W1("rp_tab", r)], writes=[W1("rp_t2", r)])
                S.add("pool", lambda e: e.tensor_tensor(out=dsl(t0, w), in0=t1[:, r, 0:w], in1=t2[:, r, 0:w], op=ALU.add),
                      reads=[W1("rp_t1", r), W1("rp_t2", r)], writes=[slotf(t0, w)])
            proj_fm(umain, ucol, 128, ev1, tiles=[(t0, w)])
            proj_fm(uswap, ucol, 128, ev2, tiles=[(t0, w)])
        A.release(mk)

    def linattn_phase(l, oT):
        mk0 = A.mark()
        qT = A.alloc("la_qT", BF16, [T], nslots=NT)
        kT = A.alloc("la_kT", BF16, [T], nslots=NT)
        FWD = [16, 17] + list(range(16))
        BWD = [17, 16] + list(range(15, -1, -1))
        lat = lambda name, j, n=1: (name, j, j + n)

        def tok_slot(name):
            return lambda t0, w: (name, t0 // 128, (t0 + w + 127) // 128)

        u = WS.get(2)
        rope_proj(0, u[0], u[1], 0, lambda t0, w: qT[:, t0:t0 + w], tok_slot("la_qT"))
        rope_proj(0, u[0], u[1], 128, lambda t0, w: kT[:, t0:t0 + w], tok_slot("la_kT"))

        vg = A.alloc("la_vg", BF16, [NT, 512], nslots=NT)
        ktm = A.alloc("la_ktm", BF16, [8, 128], nslots=8)
        sprev = A.alloc("la_sprev", BF16, [2, NT, 256], nslots=2 * NT)
        sst = A.alloc("la_S", F32, [4, 256], nslots=4)
        um = A.alloc("la_um", F32, [6, 256], nslots=6)
        qx = A.alloc("la_qx", BF16, [2, 4, 128], nslots=2)
        at = A.alloc("la_at", BF16, [6, 4, 128], nslots=6)
        qd = A.alloc("la_qd", BF16, [3, 2, 128], nslots=3)
        osb = A.alloc("la_osb", F32, [3, 256], nslots=3)
        otm = A.alloc("la_otm", BF16, [3, 256], nslots=3)
        sm = A.alloc("la_sm", F32, [3, 32], nslots=3)
        par = A.alloc("la_par", F32, [1408])
        tmpb = A.alloc("la_tmpb", F32, [3, 256], nslots=3)

        def common_chunks(scoreK, scoreQ, masks, interQ, decs, udecs, kdec, normf, ocol):
            nd = len(scoreK)
            nk = 0
            for d in range(2):
                S.add("dve", lambda e, d=d: e.memset(sst[:, 2 * d, :], 0.0), writes=[W1("la_S", 2 * d)])
            ORD = (FWD, BWD)

            def emit_T(step, d):
                j = ORD[d][step]
                src, srcslot, tb = kdec[d]
                tb_ = 2 * d + step % 2
                psn = "ps%d" % tb_
                pst = psb[tb_][:, 0:64].bitcast(BF16)
                psl = W1(psn)
                S.add("pe", lambda e: e.transpose(pst[:, 0:128], src(j), ident[:, :]), reads=[srcslot(j), W1("ident")], writes=[psl])
                kr = 4 * d + step % 4
                if tb is None:
                    S.add("act", lambda e: e.activation(out=ktm[:, kr, :], in_=pst[:, 0:128], func=AF.Copy), reads=[psl], writes=[W1("la_ktm", kr)])
                else:
                    S.add("dve", lambda e: e.tensor_tensor(
                        out=ktm[:, kr, :].rearrange("p (h x) -> p h x", x=32), in0=pst[:, 0:128].rearrange("p (h x) -> p h x", x=32),
                        in1=tb.unsqueeze(2).to_broadcast([128, 4, 32]), op=ALU.mult),
                        reads=[psl, W1("la_par")], writes=[W1("la_ktm", kr)])

            def emit_U(step, d):
                j = ORD[d][step]
                kr = 4 * d + step % 4
                us = 3 * d + step % 3
                ub_ = 4 + 2 * d + step % 2
                ps2 = psb[ub_][:, 0:256]
                psl2 = W1("ps%d" % ub_)
                S.add("pe", lambda e: e.matmul(ps2, lhsT=ktm[:, kr, :], rhs=vg[:, j, 0:256], start=True, stop=True),
                      reads=[W1("la_ktm", kr), lat("la_vg", j)], writes=[psl2])
                if udecs[d] is None:
                    S.add("dve", lambda e: e.tensor_tensor(out=um[:, us, :], in0=ps2, in1=cst[:, C_SDM:C_SDM + 256], op=ALU.mult),
                          reads=[psl2, W1("cst")], writes=[W1("la_um", us)])
                else:
                    uap, urd = udecs[d](j)
                    S.add("dve", lambda e: e.scalar_tensor_tensor(
                        out=um[:, us, :], in0=ps2, scalar=uap, in1=cst[:, C_SDM:C_SDM + 256], op0=ALU.mult, op1=ALU.mult),
                        reads=[psl2, W1("cst"), urd], writes=[W1("la_um", us)])

            def emit_S(step, d):
                j = ORD[d][step]
                us = 3 * d + step % 3
                cur = 2 * d + step % 2
                nxt = 2 * d + (step + 1) % 2
                S.add("act", lambda e: e.activation(out=sprev[:, d, j, :], in_=sst[:, cur, :], func=AF.Copy),
                      reads=[W1("la_S", cur)], writes=[("la_sprev", d * NT + j, d * NT + j + 1)])
                if step + 1 < NT:
                    dap, drd = decs[d](j)
                    S.add("dve", lambda e: e.scalar_tensor_tensor(
                        out=sst[:, nxt, :], in0=sst[:, cur, :], scalar=dap, in1=um[:, us, :], op0=ALU.mult, op1=ALU.add),
                        reads=[W1("la_S", cur), W1("la_um", us), drd], writes=[W1("la_S", nxt)])
            for st0 in range(3):
                for d in range(2):
                    emit_T(st0, d)
            for st0 in range(2):
                for d in range(2):
                    emit_U(st0, d)
            for step in range(NT):
                for d in range(2):
                    if step + 3 < NT:
                        emit_T(step + 3, d)
                for d in range(2):
                    if step + 2 < NT:
                        emit_U(step + 2, d)
                for d in range(2):
                    emit_S(step, d)
            stA = {}

            def stage_A(j):
                r = j % 3
                ats = []
                for d in range(nd):
                    qr = (j * nd + d) % 2
                    S.add("pool", lambda e, d=d, qr=qr: e.tensor_tensor(
                        out=qx[:, qr, :, :], in0=scoreQ[d][0](j).unsqueeze(1).to_broadcast([128, 4, 128]), in1=bm4[:, :, :], op=ALU.mult),
                        reads=[scoreQ[d][1](j), W1("bm4")], writes=[W1("la_qx", qr)])
                    ps, psn = PA.next()
                    S.add("pe", lambda e, ps=ps, d=d, qr=qr: e.matmul(ps[:, :], lhsT=scoreK[d][0](j), rhs=qx[:, qr, :, :].rearrange("p a b -> p (a b)"),
                                                                 start=True, stop=True),
                          reads=[scoreK[d][1](j), W1("la_qx", qr)], writes=[W1(psn)])
                    ar = (j % 3) * 2 + d
                    S.add("dve", lambda e, ps=ps, d=d, ar=ar: e.tensor_tensor(
                        out=at[:, ar, :, :], in0=ps[:, :].rearrange("p (a b) -> p a b", b=128), in1=masks[d], op=ALU.mult),
                        reads=[W1(psn), W1("la_par"), W1("cst")], writes=[W1("la_at", ar)])
                    ats.append(ar)
                iq = []
                for d in range(2):
                    if interQ[d][2] is None:
                        iq.append((interQ[d][0](j), interQ[d][1](j)))
                    else:
                        tb = interQ[d][2]
                        S.add("pool", lambda e, d=d, tb=tb: e.tensor_tensor(out=qd[:, r, d, :], in0=interQ[d][0](j), in1=tb, op=ALU.mult),
                              reads=[interQ[d][1](j), W1("la_par")], writes=[W1("la_qd", r)])
                        iq.append((qd[:, r, d, :], W1("la_qd", r)))
                stA[j] = (ats, iq)

            def stage_B(j):
                r = j % 3
                ats, iq = stA.pop(j)
                ps, psn = PB.next()

                def f(e):
                    e.matmul(ps[:, 0:256], lhsT=iq[0][0], rhs=sprev[:, 0, j, :], start=True, stop=False)
                    ins = e.matmul(ps[:, 0:256], lhsT=iq[1][0], rhs=sprev[:, 1, j, :], start=False, stop=False)
                    n = len(ats) * 4
                    k = 0
                    for ar in ats:
                        for h in range(4):
                            k += 1
                            ins = e.matmul(ps[:, 64 * h:64 * h + 64], lhsT=at[:, ar, h, :], rhs=vg[:, j, 64 * h:64 * h + 64],
                                           start=False, stop=(k == n))
                    return ins
                S.add("pe", f, reads=[iq[0][1], iq[1][1], ("la_sprev", j, j + 1), ("la_sprev", NT + j, NT + j + 1), lat("la_vg", j)]
                      + [W1("la_at", a) for a in ats], writes=[W1(psn)])
                normf[0](ps, psn, j, r)

            stage_A(0)
            stage_A(1)
            for j in range(NT):
                if j + 2 < NT:
                    stage_A(j + 2)
                stage_B(j)
                if j >= 1:
                    normf[1](j - 1, (j - 1) % 3)
                if j >= 2:
                    finish_chunk(j - 2, (j - 2) % 3, ocol)
            normf[1](NT - 1, (NT - 1) % 3)
            finish_chunk(NT - 2, (NT - 2) % 3, ocol)
            finish_chunk(NT - 1, (NT - 1) % 3, ocol)

        def finish_chunk(j, r, ocol):
            for hh in range(2):
                ps, psn = PC.next()
                pst = ps[:, 0:64].bitcast(BF16)
                S.add("pe", lambda e, pst=pst, hh=hh: e.transpose(pst[:, 0:128], otm[:, r, 128 * hh:128 * hh + 128], ident[:, :]),
                      reads=[W1("la_otm", r), W1("ident")], writes=[W1(psn)])
                S.add("act", lambda e, pst=pst, hh=hh: e.activation(out=oT[:, ocol + hh, 128 * j:128 * j + 128], in_=pst[:, 0:128], func=AF.Copy),
                      reads=[W1(psn)], writes=[xsl("oT", ocol + hh, 128 * j, 128)])

        def vg_evac(ps, psn, j):
            S.add("dve", lambda e: e.tensor_copy(out=vg[:, j, 0:256], in_=ps[:, 0:256]), reads=[W1(psn)], writes=[lat("la_vg", j)])
            S.add("act", lambda e: e.activation(out=vg[:, j, 256:512], in_=ps[:, 256:512], func=AF.Silu), reads=[W1(psn), lat("la_vg", j)], writes=[lat("la_vg", j)])

        u = WS.get(2)
        proj_tm(u, vg_evac)
        KS = 32 ** -0.5

        def logsig(out_ap, in_ap):
            S.add("act", lambda e: e.activation(out=out_ap, in_=in_ap, func=AF.Exp, scale=-1.0), reads=[W1("lpar"), W1("la_par")], writes=[W1("la_par")])
            S.add("act", lambda e: e.activation(out=out_ap, in_=out_ap, func=AF.Ln, bias=1.0), reads=[W1("la_par")], writes=[W1("la_par")])
            S.add("dve", lambda e: e.tensor_scalar(out=out_ap, in0=out_ap, scalar1=-1.0, scalar2=None, op0=ALU.mult), reads=[W1("la_par")], writes=[W1("la_par")])
        logsig(par[:, 0:2], lpar[:, P_RDLP:P_RDLP + 2])
        logsig(par[:, 2:10], lpar[:, P_RDLB:P_RDLB + 8])
        S.add("act", lambda e: e.activation(out=par[:, 10:12], in_=par[:, 0:2], func=AF.Exp, scale=128.0), reads=[W1("la_par")], writes=[W1("la_par")])
        for d in range(2):
            S.add("dve", lambda e, d=d: e.tensor_scalar(out=par[:, 16 + 4 * d:20 + 4 * d], in0=par[:, 2 + 4 * d:6 + 4 * d],
                                                       scalar1=cst[:, C_JC + d:C_JC + d + 1], scalar2=None, op0=ALU.mult),
                  reads=[W1("la_par"), W1("cst")], writes=[W1("la_par")])
            S.add("act", lambda e, d=d: e.activation(out=par[:, 16 + 4 * d:20 + 4 * d], in_=par[:, 16 + 4 * d:20 + 4 * d], func=AF.Exp),
                  reads=[W1("la_par")], writes=[W1("la_par")])
            S.add("dve", lambda e, d=d: e.tensor_scalar(out=par[:, 16 + 4 * d:20 + 4 * d], in0=par[:, 16 + 4 * d:20 + 4 * d], scalar1=KS, scalar2=None, op0=ALU.mult),
                  reads=[W1("la_par")], writes=[W1("la_par")])
            io = C_IP1 if d == 0 else C_IREV
            S.add("act", lambda e, d=d, io=io: e.activation(out=par[:, 128 + 128 * d:256 + 128 * d], in_=cst[:, io:io + 128], func=AF.Exp, scale=par[:, d:d + 1]),
                  reads=[W1("la_par"), W1("cst")], writes=[W1("la_par")])
        for h in range(4):
            mo = 512 + 128 * h
            S.add("dve", lambda e, h=h, mo=mo: e.tensor_scalar(out=par[:, mo:mo + 128], in0=cst[:, C_RPOS:C_RPOS + 128], scalar1=par[:, 2 + h:3 + h], scalar2=None, op0=ALU.mult),
                  reads=[W1("la_par"), W1("cst")], writes=[W1("la_par")])
            S.add("dve", lambda e, h=h, mo=mo: e.scalar_tensor_tensor(out=par[:, mo:mo + 128], in0=cst[:, C_RNEG:C_RNEG + 128], scalar=par[:, 6 + h:7 + h], in1=par[:, mo:mo + 128],
                                                                 op0=ALU.mult, op1=ALU.add),
                  reads=[W1("la_par"), W1("cst")], writes=[W1("la_par")])
            S.add("act", lambda e, mo=mo: e.activation(out=par[:, mo:mo + 128], in_=par[:, mo:mo + 128], func=AF.Exp), reads=[W1("la_par")], writes=[W1("la_par")])
            S.add("dve", lambda e, mo=mo: e.tensor_tensor(out=par[:, mo:mo + 128], in0=par[:, mo:mo + 128], in1=cst[:, C_ID:C_ID + 128], op=ALU.add),
                  reads=[W1("la_par"), W1("cst")], writes=[W1("la_par")])
            S.add("dve", lambda e, mo=mo: e.tensor_scalar(out=par[:, mo:mo + 128], in0=par[:, mo:mo + 128], scalar1=KS, scalar2=None, op0=ALU.mult),
                  reads=[W1("la_par")], writes=[W1("la_par")])

        kslice = lambda j: kT[:, 128 * j:128 * j + 128]
        qslice = lambda j: qT[:, 128 * j:128 * j + 128]
        ksl = lambda j: lat("la_kT", j)
        qsl = lambda j: lat("la_qT", j)

        def ret_norm1(ps, psn, j, r):
            S.add("act", lambda e: e.activation(out=osb[:, r, :], in_=ps[:, 0:256], func=AF.Copy), reads=[W1(psn)], writes=[W1("la_osb", r)])
            S.add("act", lambda e: e.activation(out=tmpb[:, r, :], in_=ps[:, 0:256], func=AF.Square), reads=[W1(psn)], writes=[W1("la_tmpb", r)])
            o3 = osb[:, r, :].rearrange("p (h x) -> p h x", x=64)
            S.add("dve", lambda e: e.tensor_reduce(out=sm[:, r, 0:4], in_=o3, axis=AX.X, op=ALU.add), reads=[W1("la_osb", r)], writes=[W1("la_sm", r)])
            S.add("dve", lambda e: e.tensor_reduce(out=sm[:, r, 4:8], in_=tmpb[:, r, :].rearrange("p (h x) -> p h x", x=64), axis=AX.X, op=ALU.add),
                  reads=[W1("la_tmpb", r), W1("la_sm", r)], writes=[W1("la_sm", r)])
            S.add("dve", lambda e: e.tensor_scalar(out=sm[:, r, 8:12], in0=sm[:, r, 0:4], scalar1=1.0 / 64, scalar2=None, op0=ALU.mult), reads=[W1("la_sm", r)], writes=[W1("la_sm", r)])
            S.add("dve", lambda e: e.tensor_tensor(out=sm[:, r, 12:16], in0=sm[:, r, 8:12], in1=sm[:, r, 8:12], op=ALU.mult), reads=[W1("la_sm", r)], writes=[W1("la_sm", r)])
            S.add("dve", lambda e: e.scalar_tensor_tensor(out=sm[:, r, 16:20], in0=sm[:, r, 4:8], scalar=1.0 / 64, in1=sm[:, r, 12:16], op0=ALU.mult, op1=ALU.subtract),
                  reads=[W1("la_sm", r)], writes=[W1("la_sm", r)])
            S.add("act", lambda e: e.activation(out=sm[:, r, 20:24], in_=sm[:, r, 16:20], func=AF.Ln, bias=GN_EPS), reads=[W1("la_sm", r)], writes=[W1("la_sm", r)])
            S.add("act", lambda e: e.activation(out=sm[:, r, 24:28], in_=sm[:, r, 20:24], func=AF.Exp, scale=-0.5), reads=[W1("la_sm", r)], writes=[W1("la_sm", r)])

        def ret_norm3(j, r):
            o3 = osb[:, r, :].rearrange("p (h x) -> p h x", x=64)
            S.add("dve", lambda e: e.tensor_tensor(out=o3, in0=o3, in1=sm[:, r, 8:12].unsqueeze(2).to_broadcast([128, 4, 64]), op=ALU.subtract),
                  reads=[W1("la_osb", r), W1("la_sm", r)], writes=[W1("la_osb", r)])
            S.add("dve", lambda e: e.tensor_tensor(out=o3, in0=o3, in1=sm[:, r, 24:28].unsqueeze(2).to_broadcast([128, 4, 64]), op=ALU.mult),
                  reads=[W1("la_osb", r), W1("la_sm", r)], writes=[W1("la_osb", r)])
            S.add("pool", lambda e: e.tensor_tensor(out=otm[:, r, :], in0=osb[:, r, :], in1=vg[:, j, 256:512], op=ALU.mult),
                  reads=[W1("la_osb", r), lat("la_vg", j)], writes=[W1("la_otm", r)])

        MT = par[:, 512:1024].rearrange("p (a b) -> p a b", b=128)
        PR = W1("la_par")
        common_chunks(
            scoreK=[(kslice, ksl)], scoreQ=[(qslice, qsl)], masks=[MT],
            interQ=[(qslice, qsl, par[:, 128:256]), (qslice, qsl, par[:, 256:384])],
            decs=[lambda j: (par[:, 10:11], PR), lambda j: (par[:, 11:12], PR)], udecs=[None, None],
            kdec=[(kslice, ksl, par[:, 16:20]), (kslice, ksl, par[:, 20:24])],
            normf=(ret_norm1, ret_norm3), ocol=0)
        tap("oret", oT[:, 0:2, :], [("oT", 0, 2 * NT)])

        mk1 = A.mark()
        lr = A.alloc("gl_lr", BF16, [T], nslots=NT)
        gq = A.alloc("gl_q", BF16, [2, T], nslots=2 * NT)
        gk = A.alloc("gl_k", BF16, [2, T], nslots=2 * NT)
        g1 = A.alloc("gl_g1", F32, [2, 128], nslots=2)
        g2 = A.alloc("gl_g2", F32, [2, 128], nslots=2)
        g3 = A.alloc("gl_g3", F32, [2, 128], nslots=2)
        gdec = A.alloc("gl_dec", F32, [2, NT])
        wgb = A.alloc("gl_wg", BF16, [256])
        u = WS.get(2)

        def ev_q(ps, psn, t0, w):
            S.add("act", lambda e: e.activation(out=qT[:, t0:t0 + w], in_=ps[:, 0:w], func=AF.Copy), reads=[W1(psn)], writes=[tok_slot("la_qT")(t0, w)])

        def ev_k(ps, psn, t0, w):
            S.add("act", lambda e: e.activation(out=kT[:, t0:t0 + w], in_=ps[:, 0:w], func=AF.Copy), reads=[W1(psn)], writes=[tok_slot("la_kT")(t0, w)])

        def ev_lr(ps, psn, t0, w):
            S.add("act", lambda e: e.activation(out=lr[0:32, t0:t0 + w], in_=ps[0:32, 0:w], func=AF.Copy), reads=[W1(psn)], writes=[tok_slot("gl_lr")(t0, w)])
        proj_fm(u[0], 0, 128, ev_q)
        proj_fm(u[0], 128, 128, ev_k)
        proj_fm(u[1], 0, 32, ev_lr)
        u = WS.get(2)
        proj_tm(u, vg_evac)
        S.add("dve", lambda e: e.tensor_copy(out=wgb[0:32, :], in_=lpar[0:32, P_WG:P_WG + 256]), reads=[W1("lpar")], writes=[W1("gl_wg")])
        S.add("dve", lambda e: e.tensor_scalar(out=par[:, 1024:1026], in0=lpar[:, P_BG:P_BG + 2], scalar1=-1.0, scalar2=None, op0=ALU.mult),
              reads=[W1("lpar"), W1("la_par")], writes=[W1("la_par")])
        S.add("dve", lambda e: e.tensor_copy(out=par[:, 1100:1356], in_=lpar[:, P_GNW:P_GNW + 256]), reads=[W1("lpar"), W1("la_par")], writes=[W1("la_par")])
        QS = 32 ** -0.5
        ng = 0
        ONE = cst[:, C_ONE:C_ONE + 128]
        for d in range(2):
            for j in range(NT):
                r = ng % 2
                ng += 1
                sl = slice(128 * j, 128 * j + 128)
                ps, psn = PA.next()
                S.add("pe", lambda e, ps=ps, d=d, sl=sl: e.matmul(ps[:, 0:128], lhsT=wgb[0:32, 128 * d:128 * d + 128], rhs=lr[0:32, sl], start=True, stop=True),
                      reads=[W1("gl_wg"), lat("gl_lr", j)], writes=[W1(psn)])
                S.add("act", lambda e, ps=ps, d=d, r=r: e.activation(out=g1[:, r, :], in_=ps[:, 0:128], func=AF.Exp, scale=-1.0, bias=par[:, 1024 + d:1025 + d]),
                      reads=[W1(psn), W1("la_par")], writes=[W1("gl_g1", r)])
                S.add("act", lambda e, r=r: e.activation(out=g1[:, r, :], in_=g1[:, r, :], func=AF.Ln, bias=1.0), reads=[W1("gl_g1", r)], writes=[W1("gl_g1", r)])
                if d == 0:
                    S.add("dve", lambda e, r=r: e.tensor_tensor_scan(out=g2[:, r, :], data0=ONE, data1=g1[:, r, :], initial=0.0, op0=ALU.mult, op1=ALU.add),
                          reads=[W1("cst"), W1("gl_g1", r)], writes=[W1("gl_g2", r)])
                else:
                    S.add("dve", lambda e, r=r: e.tensor_tensor_scan(out=g2[:, r, ::-1], data0=ONE, data1=g1[:, r, ::-1], initial=0.0, op0=ALU.mult, op1=ALU.add),
                          reads=[W1("cst"), W1("gl_g1", r)], writes=[W1("gl_g2", r)])
                S.add("act", lambda e, r=r: e.activation(out=g3[:, r, :], in_=g2[:, r, :], func=AF.Exp, scale=-1.0 / 16), reads=[W1("gl_g2", r)], writes=[W1("gl_g3", r)])
                S.add("act", lambda e, r=r: e.activation(out=g1[:, r, :], in_=g2[:, r, :], func=AF.Exp, scale=1.0 / 16), reads=[W1("gl_g2", r), W1("gl_g1", r)], writes=[W1("gl_g1", r)])
                S.add("dve", lambda e, d=d, r=r, sl=sl: e.scalar_tensor_tensor(out=gq[:, d, sl], in0=qT[:, sl], scalar=QS, in1=g3[:, r, :], op0=ALU.mult, op1=ALU.mult),
                      reads=[lat("la_qT", j), W1("gl_g3", r)], writes=[("gl_q", d * NT + j, d * NT + j + 1)])
                S.add("pool", lambda e, d=d, r=r, sl=sl: e.tensor_tensor(out=gk[:, d, sl], in0=kT[:, sl], in1=g1[:, r, :], op=ALU.mult),
                      reads=[lat("la_kT", j), W1("gl_g1", r)], writes=[("gl_k", d * NT + j, d * NT + j + 1)])
                col = 127 if d == 0 else 0
                S.add("pool", lambda e, d=d, j=j, r=r, col=col: e.tensor_copy(out=gdec[:, d, j:j + 1], in_=g3[:, r, col:col + 1]),
                      reads=[W1("gl_g3", r)], writes=[W1("gl_dec")])

        def gla_norm1(ps, psn, j, r):
            S.add("act", lambda e: e.activation(out=osb[:, r, :], in_=ps[:, 0:256], func=AF.Copy), reads=[W1(psn)], writes=[W1("la_osb", r)])
            S.add("act", lambda e: e.activation(out=tmpb[:, r, :], in_=ps[:, 0:256], func=AF.Square), reads=[W1(psn)], writes=[W1("la_tmpb", r)])
            S.add("dve", lambda e: e.tensor_reduce(out=sm[:, r, 4:8], in_=tmpb[:, r, :].rearrange("p (h x) -> p h x", x=64), axis=AX.X, op=ALU.add),
                  reads=[W1("la_tmpb", r), W1("la_sm", r)], writes=[W1("la_sm", r)])
            S.add("act", lambda e: e.activation(out=sm[:, r, 20:24], in_=sm[:, r, 4:8], func=AF.Ln, scale=1.0 / 64, bias=EPS), reads=[W1("la_sm", r)], writes=[W1("la_sm", r)])
            S.add("act", lambda e: e.activation(out=sm[:, r, 24:28], in_=sm[:, r, 20:24], func=AF.Exp, scale=-0.5), reads=[W1("la_sm", r)], writes=[W1("la_sm", r)])

        def gla_norm3(j, r):
            o3 = osb[:, r, :].rearrange("p (h x) -> p h x", x=64)
            S.add("dve", lambda e: e.tensor_tensor(out=o3, in0=o3, in1=sm[:, r, 24:28].unsqueeze(2).to_broadcast([128, 4, 64]), op=ALU.mult),
                  reads=[W1("la_osb", r), W1("la_sm", r)], writes=[W1("la_osb", r)])
            S.add("pool", lambda e: e.tensor_tensor(out=osb[:, r, :], in0=osb[:, r, :], in1=par[:, 1100:1356], op=ALU.mult),
                  reads=[W1("la_osb", r), W1("la_par")], writes=[W1("la_osb", r)])
            S.add("pool", lambda e: e.tensor_tensor(out=otm[:, r, :], in0=osb[:, r, :], in1=vg[:, j, 256:512], op=ALU.mult),
                  reads=[W1("la_osb", r), lat("la_vg", j)], writes=[W1("la_otm", r)])

        gsl = lambda nm, d: (lambda j: (nm, d * NT + j, d * NT + j + 1))
        LTm = cst[:, C_LT:C_LT + 128].unsqueeze(1).to_broadcast([128, 4, 128])
        UTm = cst[:, C_UT:C_UT + 128].unsqueeze(1).to_broadcast([128, 4, 128])
        gkf = lambda j: gk[:, 0, 128 * j:128 * j + 128]
        gkb = lambda j: gk[:, 1, 128 * j:128 * j + 128]
        gqf = lambda j: gq[:, 0, 128 * j:128 * j + 128]
        gqb = lambda j: gq[:, 1, 128 * j:128 * j + 128]
        GD = W1("gl_dec")
        decf = lambda j: (gdec[:, 0, j:j + 1], GD)
        decb = lambda j: (gdec[:, 1, j:j + 1], GD)
        common_chunks(
            scoreK=[(gkf, gsl("gl_k", 0)), (gkb, gsl("gl_k", 1))],
            scoreQ=[(gqf, gsl("gl_q", 0)), (gqb, gsl("gl_q", 1))],
            masks=[LTm, UTm],
            interQ=[(gqf, gsl("gl_q", 0), None), (gqb, gsl("gl_q", 1), None)],
            decs=[decf, decb], udecs=[decf, decb],
            kdec=[(gkf, gsl("gl_k", 0), None), (gkb, gsl("gl_k", 1), None)],
            normf=(gla_norm1, gla_norm3), ocol=2)
        tap("ogla", oT[:, 2:4, :], [("oT", 2 * NT, 4 * NT)])
        A.release(mk1)
        A.release(mk0)

    def wout_rg(l, oT, units):
        tiles = [(dm, t0, w) for (t0, w) in BTS for dm in range(8)]
        mkx = A.mark()
        xr = A.alloc("xr_rg", F32, [12, 512], nslots=12)
        RP = ResidPass(l, 2, tiles, (xr, "xr_rg", 12))
        for (dm, t0, w) in tiles:
            wap, ws = units[dm // 4]
            ps, psn = PA.next()

            def f(e, ps=ps, wap=wap, dm=dm, t0=t0, w=w):
                for k in range(4):
                    off = 512 * k + 128 * (dm % 4)
                    ins = e.matmul(ps[:, 0:w], lhsT=wap[:, off:off + 128], rhs=oT[:, k, t0:t0 + w], start=(k == 0), stop=(k == 3))
                return ins
            S.add("pe", f, reads=[W1("wbf", ws)] + [xsl("oT", k, t0, w) for k in range(4)], writes=[W1(psn)])
            RP.evac(ps, psn)
        A.release(mkx)

    def diff_phase(l, lam_init):
        wrg = A.alloc("df_wrg", BF16, [2, UNIT], nslots=2)
        urg = WS.get(2)
        for i in range(2):
            S.add("act", lambda e, i=i: e.activation(out=wrg[:, i, :], in_=urg[i][0], func=AF.Copy), reads=[W1("wbf", urg[i][1])], writes=[W1("df_wrg", i)])
        kT = A.alloc("df_kT", BF16, [4, T], nslots=4 * NT)
        qT = A.alloc("df_qT", BF16, [4, T], nslots=4 * NT)

        def slotc(name, c):
            return lambda t0, w: xsl(name, c, t0, w)
        u = WS.get(4)
        for c in range(4):
            rope_proj(1, u[c // 2], u[2 + c // 2], 128 * (c % 2), (lambda c: (lambda t0, w: kT[:, c, t0:t0 + w]))(c), slotc("df_kT", c))
        u = WS.get(4)
        for c in range(4):
            rope_proj(1, u[c // 2], u[2 + c // 2], 128 * (c % 2), (lambda c: (lambda t0, w: qT[:, c, t0:t0 + w]))(c), slotc("df_qT", c))
        va = A.alloc("df_va", BF16, [NT, 512], nslots=NT)
        qx = A.alloc("df_qx", BF16, [4, 2, 512])
        pt = A.alloc("df_pt", BF16, [4, 512], nslots=4)
        pacc = A.alloc("df_pacc", F32, [2, 512], nslots=2)
        tt = A.alloc("df_t", F32, [2, 512], nslots=2)
        oo = A.alloc("df_o", F32, [2, 512], nslots=2)
        lns = A.alloc("df_lns", F32, [2, 512], nslots=2)
        rs = A.alloc("df_rs", F32, [2, 512], nslots=2)
        sqb = A.alloc("df_sqb", BF16, [2, 512], nslots=2)
        odT = A.alloc("df_odT", BF16, [4, 512], nslots=4)
        tmp = A.alloc("df_tmp", F32, [128])
        lamv = A.alloc("df_lam", F32, [8])
        wod = A.alloc("df_wod", BF16, [2, UNIT], nslots=2)
        xrd = A.alloc("xr_d", F32, [3, 512], nslots=3)
        otl = A.alloc("df_otl", BF16, [4, 512])

        S.add("dve", lambda e: e.tensor_tensor(out=tmp[:, 0:64], in0=lpar[:, P_DLAM:P_DLAM + 64], in1=lpar[:, P_DLAM + 64:P_DLAM + 128], op=ALU.mult),
              reads=[W1("lpar")], writes=[W1("df_tmp")])
        S.add("dve", lambda e: e.tensor_tensor(out=tmp[:, 64:128], in0=lpar[:, P_DLAM + 128:P_DLAM + 192], in1=lpar[:, P_DLAM + 192:P_DLAM + 256], op=ALU.mult),
              reads=[W1("lpar"), W1("df_tmp")], writes=[W1("df_tmp")])
        S.add("dve", lambda e: e.tensor_reduce(out=lamv[:, 0:2], in_=tmp[:, 0:128].rearrange("p (a b) -> p a b", b=64), axis=AX.X, op=ALU.add),
              reads=[W1("df_tmp")], writes=[W1("df_lam")])
        S.add("act", lambda e: e.activation(out=lamv[:, 2:4], in_=lamv[:, 0:2], func=AF.Exp), reads=[W1("df_lam")], writes=[W1("df_lam")])
        S.add("dve", lambda e: e.tensor_tensor(out=lamv[:, 4:5], in0=lamv[:, 3:4], in1=lamv[:, 2:3], op=ALU.subtract), reads=[W1("df_lam")], writes=[W1("df_lam")])
        S.add("dve", lambda e: e.tensor_scalar(out=lamv[:, 4:5], in0=lamv[:, 4:5], scalar1=-float(lam_init), scalar2=None, op0=ALU.add), reads=[W1("df_lam")], writes=[W1("df_lam")])
        S.add("dve", lambda e: e.tensor_scalar(out=lamv[:, 5:6], in0=lpar[:, P_DNWC:P_DNWC + 1], scalar1=float(1.0 - lam_init), scalar2=None, op0=ALU.mult),
              reads=[W1("lpar"), W1("df_lam")], writes=[W1("df_lam")])
        S.add("pool", lambda e: e.memset(qx[:, :, :, :], 0.0), writes=[W1("df_qx")])
        u = WS.get(2)

        def va_evac(ps, psn, j):
            S.add("act", lambda e: e.activation(out=va[:, j, :], in_=ps[:, :], func=AF.Copy), reads=[W1(psn)], writes=[("df_va", j, j + 1)])
        proj_tm(u, va_evac)
        tap("dk", kT[:, :, :], [("df_kT", 0, 4 * NT)])
        tap("dq", qT[:, :, :], [("df_qT", 0, 4 * NT)])
        ud = WS.get(2)
        for i in range(2):
            S.add("act", lambda e, i=i: e.activation(out=wod[:, i, :], in_=ud[i][0], func=AF.Copy), reads=[W1("wbf", ud[i][1])], writes=[W1("df_wod", i)])

        PS_S = Pool_([0, 1, 2])
        PS_ACC = Pool_([4, 5])
        PS_N = Pool_([6, 7])
        ONEF = cst[:, C_ONE:C_ONE + 128]
        npt = 0
        ng = 0
        for bi, (t0, w) in enumerate(BTS):
            keys = list(range(NT)) if bi < 4 else [16, 17]
            nk = len(keys)
            S.add("sp", lambda e, t0=t0, w=w: e.dma_start(out=otl[:, :, 0:w], in_=d_ot[:, :, t0:t0 + w]), reads=[W1("otd")], writes=[W1("df_otl")], dma=True)
            S.add("pool", lambda e, t0=t0, w=w: e.tensor_copy(out=qx[0:64, :, 0, 0:w], in_=qT[0:64, :, t0:t0 + w]),
                  reads=[xsl("df_qT", h, t0, w) for h in range(4)] + [W1("df_qx")], writes=[W1("df_qx")])
            S.add("pool", lambda e, t0=t0, w=w: e.tensor_copy(out=qx[64:128, :, 1, 0:w], in_=qT[64:128, :, t0:t0 + w]),
                  reads=[xsl("df_qT", h, t0, w) for h in range(4)] + [W1("df_qx")], writes=[W1("df_qx")])
            LA = 2
            stream = []
            for h in range(4):
                for m in range(2):
                    g = ng % 2
                    ng += 1
                    acc, accn = PS_ACC.next()
                    for ki, kt in enumerate(keys):
                        stream.append((h, m, g, acc, accn, ki, kt))
            sps = {}

            def emit_S(idx, w=w):
                h, m, g, acc, accn, ki, kt = stream[idx]
                ps, psn = PS_S.next()
                sps[idx] = (ps, psn)
                S.add("pe", lambda e: e.matmul(ps[:, 0:w], lhsT=kT[:, h, 128 * kt:128 * kt + 128], rhs=qx[:, h, m, 0:w], start=True, stop=True),
                      reads=[xsl("df_kT", h, 128 * kt, 128), W1("df_qx")], writes=[W1(psn)])

            def epilogue(h, m, g, acc, accn, w=w):
                pn, pnn = PS_N.next()
                S.add("pe", lambda e: e.matmul(pn[:, 0:w], lhsT=ONEF, rhs=pacc[:, g, 0:w], start=True, stop=True),
                      reads=[W1("cst"), W1("df_pacc", g)], writes=[W1(pnn)])
                S.add("act", lambda e: e.activation(out=lns[:, g, 0:w], in_=pn[:, 0:w], func=AF.Ln), reads=[W1(pnn)], writes=[W1("df_lns", g)])
                S.add("act", lambda e: e.activation(out=rs[:, g, 0:w], in_=lns[:, g, 0:w], func=AF.Exp, scale=-1.0), reads=[W1("df_lns", g)], writes=[W1("df_rs", g)])
                hr = h % 2
                if m == 0:
                    S.add("dve", lambda e: e.tensor_tensor(out=tt[:, hr, 0:w], in0=acc[:, 0:w], in1=rs[:, g, 0:w], op=ALU.mult),
                          reads=[W1(accn), W1("df_rs", g)], writes=[W1("df_t", hr)])
                    return
                S.add("dve", lambda e: e.scalar_tensor_tensor(
                    out=oo[:, hr, 0:w], in0=rs[:, g, 0:w], scalar=lamv[:, 4:5], in1=acc[:, 0:w], op0=ALU.mult, op1=ALU.mult),
                    reads=[W1(accn), W1("df_rs", g), W1("df_lam")], writes=[W1("df_o", hr)])
                S.add("pool", lambda e: e.tensor_tensor(out=oo[:, hr, 0:w], in0=oo[:, hr, 0:w], in1=tt[:, hr, 0:w], op=ALU.add),
                      reads=[W1("df_o", hr), W1("df_t", hr)], writes=[W1("df_o", hr)])
                S.add("pool", lambda e: e.tensor_tensor(out=sqb[:, hr, 0:w], in0=oo[:, hr, 0:w], in1=oo[:, hr, 0:w], op=ALU.mult),
                      reads=[W1("df_o", hr)], writes=[W1("df_sqb", hr)])
                pn2, pnn2 = PS_N.next()
                S.add("pe", lambda e: e.matmul(pn2[:, 0:w], lhsT=ones_bf[:, :], rhs=sqb[:, hr, 0:w], start=True, stop=True),
                      reads=[W1("ones"), W1("df_sqb", hr)], writes=[W1(pnn2)])
                S.add("act", lambda e: e.activation(out=lns[:, g, 0:w], in_=pn2[:, 0:w], func=AF.Ln, scale=1.0 / 128, bias=EPS),
                      reads=[W1(pnn2), W1("df_lns", g)], writes=[W1("df_lns", g)])
                S.add("act", lambda e: e.activation(out=rs[:, g, 0:w], in_=lns[:, g, 0:w], func=AF.Exp, scale=-0.5),
                      reads=[W1("df_lns", g), W1("df_rs", g)], writes=[W1("df_rs", g)])
                S.add("dve", lambda e: e.scalar_tensor_tensor(
                    out=odT[:, h, 0:w], in0=oo[:, hr, 0:w], scalar=lamv[:, 5:6], in1=rs[:, g, 0:w], op0=ALU.mult, op1=ALU.mult),
                    reads=[W1("df_o", hr), W1("df_rs", g), W1("df_lam")], writes=[W1("df_odT", h)])

            pending = []
            for idx in range(min(LA, len(stream))):
                emit_S(idx)
            for idx, (h, m, g, acc, accn, ki, kt) in enumerate(stream):
                if idx + LA < len(stream):
                    emit_S(idx + LA)
                ps, psn = sps.pop(idx)
                pr = npt % 4
                npt += 1
                S.add("act", lambda e, ps=ps, pr=pr, w=w: e.activation(out=pt[:, pr, 0:w], in_=ps[:, 0:w], func=AF.Exp, scale=0.125),
                      reads=[W1(psn)], writes=[W1("df_pt", pr)])
                S.add("pe", lambda e, acc=acc, pr=pr, kt=kt, h=h, ki=ki, w=w, nk=nk: e.matmul(
                    acc[:, 0:w], lhsT=va[:, kt, 128 * h:128 * h + 128], rhs=pt[:, pr, 0:w], start=(ki == 0), stop=(ki == nk - 1)),
                    reads=[W1("df_pt", pr), ("df_va", kt, kt + 1)], writes=[W1(accn)])
                if ki == 0:
                    S.add("dve", lambda e, g=g, pr=pr, w=w: e.tensor_copy(out=pacc[:, g, 0:w], in_=pt[:, pr, 0:w]),
                          reads=[W1("df_pt", pr)], writes=[W1("df_pacc", g)])
                else:
                    S.add("dve", lambda e, g=g, pr=pr, w=w: e.tensor_tensor(out=pacc[:, g, 0:w], in0=pacc[:, g, 0:w], in1=pt[:, pr, 0:w], op=ALU.add),
                          reads=[W1("df_pt", pr), W1("df_pacc", g)], writes=[W1("df_pacc", g)])
                if pending and ki == min(3, nk - 1):
                    epilogue(*pending.pop(0))
                if ki == nk - 1:
                    pending.append((h, m, g, acc, accn))
            while pending:
                epilogue(*pending.pop(0))
            if "odT" in d_taps:
                S.add("sp", lambda e, t0=t0, w=w: e.dma_start(out=d_taps["odT"][:, :, t0:t0 + w], in_=odT[:, :, 0:w]), reads=[("df_odT", 0, 4)], writes=[W1("dram_out")], dma=True)
            tiles = [(dm, t0, w) for dm in range(8)]
            RP = ResidPass(l, 2, tiles, (xrd, "xr_d", 3))
            for dm in range(8):
                ps, psn = PS_S.next()

                def f(e, ps=ps, dm=dm, w=w):
                    for k in range(4):
                        off = 512 * k + 128 * (dm % 4)
                        e.matmul(ps[:, 0:w], lhsT=wrg[:, dm // 4, off:off + 128], rhs=otl[:, k, 0:w], start=(k == 0), stop=False)
                    for k in range(4):
                        off = 512 * k + 128 * (dm % 4)
                        ins = e.matmul(ps[:, 0:w], lhsT=wod[:, dm // 4, off:off + 128], rhs=odT[:, k, 0:w], start=False, stop=(k == 3))
                    return ins
                S.add("pe", f, reads=[("df_wod", dm // 4, dm // 4 + 1), ("df_wrg", dm // 4, dm // 4 + 1), ("df_odT", 0, 4), W1("df_otl")], writes=[W1(psn)])
                RP.evac(ps, psn)

    def ffn_phase(l):
        mk = A.mark()
        actT = A.alloc("ff_act", BF16, [12, T], nslots=12 * NT)
        wdn = A.alloc("ff_wdn", BF16, [6, UNIT], nslots=6)
        sg = A.alloc("ff_sg", F32, [2, 512], nslots=2)
        xrf = A.alloc("xr_f", F32, [8, 512], nslots=8)
        k = 0
        for (f0, nf) in FGROUPS:
            for fi in range(nf):
                (wap, ws), = WS.get(1)
                for (t0, w) in BTS:
                    pg, pgn = PA.next()
                    pu, pun = PA.next()

                    def f(e, pg=pg, pu=pu, wap=wap, t0=t0, w=w):
                        for kc in range(8):
                            e.matmul(pg[:, 0:w], lhsT=wap[:, 256 * kc:256 * kc + 128], rhs=hT[:, kc, t0:t0 + w], start=(kc == 0), stop=(kc == 7))
                        for kc in range(8):
                            ins = e.matmul(pu[:, 0:w], lhsT=wap[:, 256 * kc + 128:256 * kc + 256], rhs=hT[:, kc, t0:t0 + w], start=(kc == 0), stop=(kc == 7))
                        return ins
                    S.add("pe", f, reads=[W1("wbf", ws)] + xall("hT", t0, w), writes=[W1(pgn), W1(pun)])
                    r = k % 2
                    k += 1
                    S.add("act", lambda e, pg=pg, r=r, w=w: e.activation(out=sg[:, r, 0:w], in_=pg[:, 0:w], func=AF.Silu), reads=[W1(pgn)], writes=[W1("ff_sg", r)])
                    S.add("dve", lambda e, pu=pu, r=r, fi=fi, t0=t0, w=w: e.tensor_tensor(out=actT[:, fi, t0:t0 + w], in0=pu[:, 0:w], in1=sg[:, r, 0:w], op=ALU.mult),
                          reads=[W1(pun), W1("ff_sg", r)], writes=[xsl("ff_act", fi, t0, w)])
            nu = nf // 2
            for i in range(nu):
                (wap, ws), = WS.get(1)
                S.add("act", lambda e, i=i, wap=wap: e.activation(out=wdn[:, i, :], in_=wap, func=AF.Copy), reads=[W1("wbf", ws)], writes=[W1("ff_wdn", i)])
            tiles = [(dm, t0, w) for (t0, w) in BTS for dm in range(8)]
            RP = ResidPass(l, 5, tiles, (xrf, "xr_f", 8))
            for (dm, t0, w) in tiles:
                ps, psn = PA.next()

                def f(e, ps=ps, dm=dm, t0=t0, w=w, nf=nf):
                    for fi in range(nf):
                        off = 1024 * (fi % 2) + 128 * dm
                        ins = e.matmul(ps[:, 0:w], lhsT=wdn[:, fi // 2, off:off + 128], rhs=actT[:, fi, t0:t0 + w], start=(fi == 0), stop=(fi == nf - 1))
                    return ins
                S.add("pe", f, reads=[("ff_wdn", 0, nu)] + [xsl("ff_act", fi, t0, w) for fi in range(nf)], writes=[W1(psn)])
                RP.evac(ps, psn)
        A.release(mk)

    for l in range(depth):
        lam_init = 0.8 - 0.6 * math.exp(-0.3 * l)
        layer_params(l)
        rmsnorm_mod(l, 0)
        tap("h%d" % l, hT[:, :, :], [("hT", 0, 8 * NT)])
        mkL = A.mark()
        oT = A.alloc("oT", BF16, [4, T], nslots=4 * NT)
        linattn_phase(l, oT)
        S.add("sp", lambda e: e.dma_start(out=d_ot[:, :, :], in_=oT[:, :, :]), reads=[("oT", 0, 4 * NT)], writes=[W1("otd")], dma=True)
        A.release(mkL)
        mkD = A.mark()
        diff_phase(l, lam_init)
        A.release(mkD)
        tap("xattn%d" % l, d_xd, [("xd", 0, 8 * NT)])
        rmsnorm_mod(l, 1)
        ffn_phase(l)
        tap("x%d" % l, d_xd, [("xd", 0, 8 * NT)])

    mk = A.mark()
    xt = A.alloc("f_xt", F32, [2, 8, 512], nslots=2)
    sq = A.alloc("f_sq", BF16, [8, 512])
    lnv = A.alloc("f_ln", F32, [512])
    rstd = A.alloc("f_rstd", F32, [512])
    out_v = d_out.rearrange("(c p) t -> p c t", p=128)
    for bi, (t0, w) in enumerate(BTS[:4]):
        r = bi % 2
        S.add("sp", lambda e, r=r, t0=t0, w=w: e.dma_start(out=xt[:, r, :, 0:w], in_=xd_v[:, :, t0:t0 + w]),
              reads=xall("xd", t0, w), writes=[W1("f_xt", r)], dma=True)
        norm_stats(xt[:, r, :, :], [W1("f_xt", r)], sq, "f_sq", lnv, "f_ln", rstd, "f_rstd", w)
        for c in range(8):
            S.add("dve", lambda e, c=c, r=r, w=w: e.scalar_tensor_tensor(out=xt[:, r, c, 0:w], in0=xt[:, r, c, 0:w], scalar=fnw[:, c:c + 1], in1=rstd[:, 0:w],
                                                                  op0=ALU.mult, op1=ALU.mult),
                  reads=[W1("f_xt", r), W1("fnw"), W1("f_rstd")], writes=[W1("f_xt", r)])
        S.add("sp", lambda e, r=r, t0=t0, w=w: e.dma_start(out=out_v[:, :, t0:t0 + w], in_=xt[:, r, :, 0:w]), reads=[W1("f_xt", r)], writes=[W1("dram_out")], dma=True)
    A.release(mk)
    S.add("sp", None, reads=[W1("dram_out")])
    S.emit(nc, es)
    es.close()
    return nc


_CACHE = {}


def _host_inputs(inp):
    wpk = _pack_weights(np.asarray(inp["w_in"], np.float32), np.asarray(inp["w_out"], np.float32),
                        np.asarray(inp["w_ffn_in"], np.float32), np.asarray(inp["w_ffn_out"], np.float32))
    npi = {k: np.asarray(v, np.float32) for k, v in inp.items()}
    lp = _layer_params(npi)
    cst = _const_pack()
    rope = _rope_tables()
    fnw = np.ascontiguousarray(npi["final_norm_w"].reshape(8, 128).T)
    wada = np.ascontiguousarray(npi["w_ada"])
    maps = []
    for b in range(8):
        xin = np.ascontiguousarray(np.concatenate([npi["x"][b].T, npi["ctx"][b].T], axis=1))
        cin = np.stack([npi["c"][b].reshape(8, 128).T, npi["c_ctx"].reshape(8, 128).T], axis=2)
        maps.append({"xin": xin, "cin": np.ascontiguousarray(cin), "wada": wada, "wpk": wpk, "lp": lp,
                     "fnw": fnw, "cst": cst, "rope": rope})
    return maps


def kernel(**inputs):
    maps = _host_inputs(inputs)
    if "nc" not in _CACHE:
        _CACHE["nc"] = build()
    res = run_bass_kernel_spmd(_CACHE["nc"], maps, core_ids=list(range(8)))
    out = np.stack([np.ascontiguousarray(r["out"].T) for r in res.results], axis=0)
    return out.astype(np.float32)
```

```python
import math
import numpy as np
from contextlib import ExitStack
import concourse.bass as bass
import concourse.mybir as mybir
from concourse.bass_utils import run_bass_kernel_spmd

F32 = mybir.dt.float32
BF16 = mybir.dt.bfloat16
AF = mybir.ActivationFunctionType
ALU = mybir.AluOpType
AX = mybir.AxisListType

D = 1024
NLAT = 2048
NCTX = 256
T = NLAT + NCTX
NT = T // 128
DEPTH = 4
DFF = 2816
NF = DFF // 128
PROJW = 3104
EPS = 1e-6
GN_EPS = 1e-5
BTS = [(0, 512), (512, 512), (1024, 512), (1536, 512), (2048, 256)]
UNIT = 2048
FGROUPS = [(0, 12), (12, 10)]

ENGS = ("pe", "act", "dve", "pool", "sp")


class Op:
    __slots__ = ("eng", "fn", "deps", "sig", "sem", "sigval", "idx", "dma", "dmak")

    def __init__(self, eng, fn, dma, idx):
        self.eng = eng
        self.fn = fn
        self.dma = dma
        self.idx = idx
        self.deps = {}
        self.sig = False
        self.sem = None
        self.sigval = 0
        self.dmak = 0


class Sched:
    def __init__(self):
        self.ops = []
        self.st = {}
        self.ndma = 0

    def newbuf(self, name, nslots=1, inherit=()):
        assert name not in self.st, name
        inh = frozenset(inherit)
        self.st[name] = [[None, {}, inh] for _ in range(nslots)]

    def final_ops(self, name):
        out = set()
        for w, rd, inh in self.st[name]:
            if w is not None:
                out.add(w)
            out.update(rd.values())
            out.update(inh)
        return out

    def delbuf(self, name):
        del self.st[name]

    def add(self, eng, fn, reads=(), writes=(), dma=False):
        op = Op(eng, fn, dma, len(self.ops))
        deps = op.deps
        for (name, lo, hi) in reads:
            st = self.st[name]
            assert 0 <= lo < hi <= len(st), (name, lo, hi, len(st))
            for s in range(lo, hi):
                w = st[s][0]
                if w is not None:
                    deps[w] = "raw"
                for o in st[s][2]:
                    deps.setdefault(o, "raw")
        for (name, lo, hi) in writes:
            st = self.st[name]
            assert 0 <= lo < hi <= len(st), (name, lo, hi, len(st))
            for s in range(lo, hi):
                w = st[s][0]
                if w is not None and w is not op:
                    deps.setdefault(w, "waw")
                for r in st[s][1].values():
                    if r is not op:
                        deps.setdefault(r, "war")
                for o in st[s][2]:
                    deps.setdefault(o, "war")
        rk = ("d", op.idx) if dma else eng
        for (name, lo, hi) in reads:
            st = self.st[name]
            for s in range(lo, hi):
                st[s][1][rk] = op
        for (name, lo, hi) in writes:
            st = self.st[name]
            for s in range(lo, hi):
                st[s][0] = op
                st[s][1] = {}
                st[s][2] = frozenset()
        for o in list(deps):
            if (not o.dma) and (not dma) and o.eng == eng and deps[o] != "raw":
                del deps[o]
        for o in deps:
            o.sig = True
        if dma:
            op.sig = True
        self.ops.append(op)
        return op

    def emit(self, nc, es):
        NDS = 16
        sems = {e: es.enter_context(nc.semaphore("s_" + e)) for e in ENGS}
        dsems = [es.enter_context(nc.semaphore("d%d" % i)) for i in range(NDS)]
        cnt = {e: 0 for e in ENGS}
        dcnt = [0] * NDS
        k = 0
        for op in self.ops:
            if not op.sig:
                continue
            if op.dma:
                j = k % NDS
                k += 1
                dcnt[j] += 16
                op.sem = dsems[j]
                op.sigval = dcnt[j]
                op.dmak = dcnt[j] - 16
            else:
                cnt[op.eng] += 1
                op.sem = sems[op.eng]
                op.sigval = cnt[op.eng]
        per = {e: [] for e in ENGS}
        for op in self.ops:
            per[op.eng].append(op)
        block = es.enter_context(nc.Block())

        def run(e, ops):
            waited = {}
            for op in ops:
                need = {}
                for o in op.deps:
                    key = id(o.sem)
                    if key not in need or need[key][1] < o.sigval:
                        need[key] = (o.sem, o.sigval)
                for key, (sem, val) in need.items():
                    if waited.get(key, 0) < val:
                        e.wait_ge(sem, val)
                        waited[key] = val
                if op.fn is None:
                    continue
                if op.dma and op.dmak > 0 and waited.get(id(op.sem), 0) < op.dmak:
                    e.wait_ge(op.sem, op.dmak)
                    waited[id(op.sem)] = op.dmak
                inst = op.fn(e)
                if op.sig:
                    inst.then_inc(op.sem, 16 if op.dma else 1)

        block.tensor(lambda e: run(e, per["pe"]))
        block.scalar(lambda e: run(e, per["act"]))
        block.vector(lambda e: run(e, per["dve"]))
        block.gpsimd(lambda e: run(e, per["pool"]))
        def run_sp(e):
            run(e, per["sp"])
            for j in range(NDS):
                if dcnt[j]:
                    e.wait_ge(dsems[j], dcnt[j])
        block.sync(run_sp)


class Arena:
    def __init__(self, sb, sched, nwords):
        self.sb = sb
        self.S = sched
        self.nwords = nwords
        self.top = 0
        self.stack = []
        self.retired = []

    def alloc(self, name, dtype, fshape, nslots=1):
        nel = 1
        for d in fshape:
            nel *= d
        nbytes = nel * (2 if dtype == BF16 else 4)
        nw = (nbytes + 3) // 4
        nw = (nw + 15) // 16 * 16
        off = self.top
        assert off + nw <= self.nwords, ("SBUF arena overflow", name, off, nw, self.nwords)
        self.top += nw
        inherit = set()
        for (lo, hi, ops) in self.retired:
            if lo < off + nw and hi > off:
                inherit |= ops
        self.S.newbuf(name, nslots, inherit)
        self.stack.append((name, off, nw))
        ap = self.sb[:, off:off + nw]
        if dtype == BF16:
            ap = ap.bitcast(BF16)
        ap = ap[:, 0:nel]
        if len(fshape) == 2:
            ap = ap.rearrange("p (a b) -> p a b", b=fshape[1])
        elif len(fshape) == 3:
            ap = ap.rearrange("p (a b c) -> p a b c", b=fshape[1], c=fshape[2])
        elif len(fshape) == 4:
            ap = ap.rearrange("p (a b c d) -> p a b c d", b=fshape[1], c=fshape[2], d=fshape[3])
        return ap

    def mark(self):
        return len(self.stack)

    def release(self, mark):
        while len(self.stack) > mark:
            name, off, nw = self.stack.pop()
            self.retired.append((off, off + nw, self.S.final_ops(name)))
            self.S.delbuf(name)
            self.top = off


def _swap_cols(base, nheads_blocks, blk):
    idx = np.arange(nheads_blocks * blk)
    b = idx // blk
    d = idx % blk
    return base + b * blk + (d + blk // 2) % blk


def _proj_units():
    u = []
    rq = np.arange(0, 128)
    rk = np.arange(128, 256)
    u.append(np.concatenate([rq, rk]))
    u.append(np.concatenate([_swap_cols(0, 4, 32), _swap_cols(128, 4, 32)]))
    u.append(np.arange(256, 512))
    u.append(np.arange(512, 768))
    u.append(np.arange(768, 1024))
    g1 = -np.ones(256, np.int64)
    g1[:32] = np.arange(1536, 1568)
    u.append(g1)
    u.append(np.arange(1024, 1280))
    u.append(np.arange(1280, 1536))
    u.append(np.arange(2080, 2336))
    u.append(np.arange(2336, 2592))
    sw = _swap_cols(2080, 16, 32)
    u.append(sw[:256])
    u.append(sw[256:])
    u.append(np.arange(1568, 1824))
    u.append(np.arange(1824, 2080))
    sw = _swap_cols(1568, 16, 32)
    u.append(sw[:256])
    u.append(sw[256:])
    u.append(np.arange(2592, 2848))
    u.append(np.arange(2848, 3104))
    return u


NPU = 18
NUL = NPU + 4 + 22 + 11


def _pack_weights(w_in, w_out, w_ffn_in, w_ffn_out):
    L = w_in.shape[0]
    out = np.zeros((L, 128, NUL, UNIT), np.float32)
    units = _proj_units()
    for l in range(L):
        wi = w_in[l].reshape(8, 128, PROJW)
        wo = w_out[l].reshape(8, 128, D)
        fi = w_ffn_in[l].reshape(8, 128, 2 * DFF)
        fo = w_ffn_out[l].reshape(NF, 128, D)
        k = 0
        def wout_units(part, k):
            for j in range(2):
                blk = wo[4 * part:4 * part + 4, :, 512 * j:512 * j + 512].transpose(1, 0, 2)
                out[l, :, k, :] = blk.reshape(128, UNIT)
                k += 1
            return k
        for ui, cols in enumerate(units):
            if ui == 8:
                k = wout_units(0, k)
            blk = np.zeros((128, 8, 256), np.float32)
            ok = cols >= 0
            blk[:, :, ok] = wi[:, :, cols[ok]].transpose(1, 0, 2)
            out[l, :, k, :] = blk.reshape(128, UNIT)
            k += 1
        k = wout_units(1, k)
        for (f0, nf) in FGROUPS:
            for f in range(f0, f0 + nf):
                blk = np.concatenate([fi[:, :, 128 * f:128 * f + 128],
                                      fi[:, :, DFF + 128 * f:DFF + 128 * f + 128]], axis=2)
                out[l, :, k, :] = blk.transpose(1, 0, 2).reshape(128, UNIT)
                k += 1
            for f in range(f0, f0 + nf, 2):
                blk = fo[f:f + 2].transpose(1, 0, 2)
                out[l, :, k, :] = blk.reshape(128, UNIT)
                k += 1
        assert k == NUL
    return out


C_ID = 0
C_SDM = 128
C_LT = 384
C_UT = 512
C_RPOS = 640
C_RNEG = 768
C_IP1 = 896
C_IREV = 1024
C_JC = 1152
C_ONE = 1168
NCS = 1296
C_BM4 = 1296
NCONST = 1808


def _const_pack():
    c = np.zeros((128, NCONST), np.float32)
    p = np.arange(128)
    c[:, C_ID:C_ID + 128] = np.eye(128)
    bm = np.zeros((128, 4, 128), np.float32)
    for h in range(4):
        bm[32 * h:32 * h + 32, h, :] = 1.0
    c[:, C_BM4:C_BM4 + 512] = bm.reshape(128, 512)
    sd = np.zeros((128, 4, 64), np.float32)
    for h in range(4):
        sd[32 * h:32 * h + 32, h, :] = 1.0
    c[:, C_SDM:C_SDM + 256] = sd.reshape(128, 256)
    j = p[:, None]
    i = p[None, :]
    c[:, C_LT:C_LT + 128] = (j <= i)
    c[:, C_UT:C_UT + 128] = (j >= i)
    c[:, C_RPOS:C_RPOS + 128] = np.maximum(i - j, 0)
    c[:, C_RNEG:C_RNEG + 128] = np.maximum(j - i, 0)
    c[:, C_IP1:C_IP1 + 128] = np.broadcast_to(i + 1, (128, 128))
    c[:, C_IREV:C_IREV + 128] = np.broadcast_to(128 - i, (128, 128))
    c[:, C_JC] = 127 - p
    c[:, C_JC + 1] = p
    c[:, C_ONE:C_ONE + 128] = 1.0
    return c


def _rope_tables():
    t = np.arange(NLAT, dtype=np.float32)
    p = np.arange(128)
    tab = np.zeros((2, 128, 2, NLAT), np.float32)
    invf = (1.0 / (np.float32(10000.0) ** np.linspace(0.0, 1.0, 16, dtype=np.float32))).astype(np.float32)
    d = p % 32
    ang = t[None, :] * invf[d % 16][:, None]
    tab[0, :, 0] = np.cos(ang)
    tab[0, :, 1] = np.sin(ang) * np.where(d < 16, -1.0, 1.0)[:, None]
    half = 16
    invf2 = (1.0 / (np.float32(10000.0) ** (np.arange(half, dtype=np.float32) / half))).astype(np.float32)
    e = p % 64
    part = e // 32
    d = e % 32
    row = (np.arange(NLAT) // 64).astype(np.float32)
    col = (np.arange(NLAT) % 64).astype(np.float32)
    pos = np.where(part[:, None] == 0, row[None, :], col[None, :]).astype(np.float32)
    ang = pos * invf2[d % 16][:, None]
    tab[1, :, 0] = np.cos(ang)
    tab[1, :, 1] = np.sin(ang) * np.where(d < 16, -1.0, 1.0)[:, None]
    tab = tab.reshape(2, 128, 2, 4, 512).transpose(0, 1, 3, 2, 4)
    return np.ascontiguousarray(tab)


P_N1 = 0
P_N2 = 8
P_BADA = 16
P_RDLP = 64
P_RDLB = 66
P_BG = 74
P_WG = 76
P_GNW = 332
P_DLAM = 588
P_DNWC = 844
NLP = 848


def _layer_params(inp):
    L = DEPTH
    o = np.zeros((L, 128, NLP), np.float32)
    p = np.arange(128)
    for l in range(L):
        o[l, :, P_N1:P_N1 + 8] = inp["norm1_w"][l].reshape(8, 128).T
        o[l, :, P_N2:P_N2 + 8] = inp["norm2_w"][l].reshape(8, 128).T
        o[l, :, P_BADA:P_BADA + 48] = inp["b_ada"][l].reshape(48, 128).T
        o[l, :, P_RDLP:P_RDLP + 2] = inp["ret_decay_logit"][l][:, p // 32].T
        o[l, :, P_RDLB:P_RDLB + 8] = inp["ret_decay_logit"][l].reshape(1, 8)
        o[l, :, P_BG:P_BG + 2] = inp["gla_b_gate"][l].T
        o[l, 0:16, P_WG:P_WG + 128] = inp["gla_w_gate"][l, 0]
        o[l, 16:32, P_WG + 128:P_WG + 256] = inp["gla_w_gate"][l, 1]
        o[l, :, P_GNW:P_GNW + 256] = np.tile(inp["gla_norm_w"][l], 4)[None, :]
        o[l, :, P_DLAM:P_DLAM + 256] = inp["diff_lambda"][l].reshape(1, 256)
        o[l, :, P_DNWC] = inp["diff_norm_w"][l]
    return o


def build(depth=DEPTH, taps=()):
    nc = bass.Bass("TRN2", target_bir_lowering=False)
    dt = lambda n, s, kind="ExternalInput": nc.dram_tensor(n, list(s), F32, kind=kind).ap()
    d_x = dt("xin", [D, T])
    d_c = dt("cin", [128, 8, 2])
    d_wada = dt("wada", [DEPTH, D, 6 * D])
    d_wpk = dt("wpk", [DEPTH, 128, NUL, UNIT])
    d_lp = dt("lp", [DEPTH, 128, NLP])
    d_fn = dt("fnw", [128, 8])
    d_const = dt("cst", [128, NCONST])
    d_rope = dt("rope", [2, 128, 4, 2, 512])
    d_out = dt("out", [D, NLAT], kind="ExternalOutput")
    d_xd = dt("xd", [D, T], kind="Internal")
    d_ot = nc.dram_tensor("otrg", [128, 4, T], BF16, kind="Internal").ap()
    d_taps = {}
    for (name, shape, tdt) in taps:
        d_taps[name] = nc.dram_tensor("tap_" + name, list(shape), tdt, kind="ExternalOutput").ap()
    xd_v = d_xd.rearrange("(c p) t -> p c t", p=128)

    S = Sched()
    es = ExitStack()
    NW = 52800
    sb = es.enter_context(nc.sbuf_tensor("sb", [128, NW], F32))
    A = Arena(sb, S, NW)
    psb = []
    PSLOTS = {}
    for i in range(8):
        psb.append(es.enter_context(nc.psum_tensor("ps%d" % i, [128, 512], F32)))
        S.newbuf("ps%d" % i, PSLOTS.get("ps%d" % i, 1))
    S.newbuf("dram_out", 1)
    S.newbuf("dram_in", 1)
    S.newbuf("xd", 8 * NT)
    S.newbuf("otd", 1)

    class Pool_:
        def __init__(self, banks):
            self.banks = banks
            self.i = 0

        def next(self):
            b = self.banks[self.i % len(self.banks)]
            self.i += 1
            return psb[b], "ps%d" % b

    def W1(name, s=0, n=1):
        if name in PSLOTS and s == 0 and n == 1:
            return (name, 0, PSLOTS[name])
        return (name, s, s + n)

    hT = A.alloc("hT", BF16, [8, T], nslots=8 * NT)
    cst = A.alloc("cst", F32, [NCS])
    ident = A.alloc("ident", BF16, [128])
    ones_bf = A.alloc("ones", BF16, [128])
    bm4 = A.alloc("bm4", BF16, [4, 128])
    modv = A.alloc("modv", F32, [DEPTH, 48, 2])
    lpar = A.alloc("lpar", F32, [NLP])
    drv = A.alloc("drv", F32, [2, 2, 8])
    fnw = A.alloc("fnw", F32, [8])
    NST, NBF = 2, 4
    wst = A.alloc("wst", F32, [NST, UNIT], nslots=NST)
    wbf = A.alloc("wbf", BF16, [NBF, UNIT], nslots=NBF)

    def xs(c, t0, w):
        return (c * NT + t0 // 128, c * NT + (t0 + w + 127) // 128)

    def xsl(name, c, t0, w):
        lo, hi = xs(c, t0, w)
        return (name, lo, hi)

    def xall(name, t0, w):
        return [xsl(name, c, t0, w) for c in range(8)]

    class WStream:
        def __init__(self, total):
            self.total = total
            self.nd = 0
            self.ncst = 0
            self.ng = 0

        def _dma(self, j):
            l, u = divmod(j, NUL)
            s = j % NST
            S.add("sp", lambda e, l=l, u=u, s=s: e.dma_start(out=wst[:, s, :], in_=d_wpk[l, :, u, :]),
                  reads=[W1("dram_in")], writes=[W1("wst", s)], dma=True)

        def _cast(self, j):
            s = j % NST
            b = j % NBF
            S.add("pool", lambda e, s=s, b=b: e.tensor_copy(out=wbf[:, b, :], in_=wst[:, s, :]),
                  reads=[W1("wst", s)], writes=[W1("wbf", b)])

        def get(self, n=1):
            i = self.ng
            self.ng += n
            assert n <= NBF and self.ng <= self.total
            last_cast = min(i + NBF - 1, self.total - 1)
            while self.ncst <= last_cast:
                while self.nd <= min(self.ncst + NST - 1, self.total - 1):
                    self._dma(self.nd)
                    self.nd += 1
                self._cast(self.ncst)
                self.ncst += 1
            return [(wbf[:, (i + k) % NBF, :], (i + k) % NBF) for k in range(n)]

    WS = WStream(depth * NUL)

    S.add("sp", lambda e: e.dma_start(out=cst[:, :], in_=d_const[:, 0:NCS]), reads=[W1("dram_in")], writes=[W1("cst")], dma=True)
    S.add("sp", lambda e: e.dma_start(out=fnw[:, :], in_=d_fn[:, :]), reads=[W1("dram_in")], writes=[W1("fnw")], dma=True)
    for c in range(8):
        S.add("sp", lambda e, c=c: e.dma_start(out=d_xd[128 * c:128 * c + 128, :], in_=d_x[128 * c:128 * c + 128, :]),
              reads=[W1("dram_in")], writes=[("xd", c * NT, (c + 1) * NT)], dma=True)
    S.add("dve", lambda e: e.tensor_copy(out=ident[:, :], in_=cst[:, C_ID:C_ID + 128]), reads=[W1("cst")], writes=[W1("ident")])
    S.add("dve", lambda e: e.memset(ones_bf[:, :], 1.0), writes=[W1("ones")])
    mkb = A.mark()
    bmf = A.alloc("bmf", F32, [512])
    S.add("sp", lambda e: e.dma_start(out=bmf[:, :], in_=d_const[:, C_BM4:C_BM4 + 512]), reads=[W1("dram_in")], writes=[W1("bmf")], dma=True)
    S.add("dve", lambda e: e.tensor_copy(out=bm4[:, :, :], in_=bmf[:, :].rearrange("p (a b) -> p a b", b=128)),
          reads=[W1("bmf")], writes=[W1("bm4")])
    A.release(mkb)

    mk = A.mark()
    condT = A.alloc("condT", F32, [8, 2])
    ast = A.alloc("ast", F32, [2, 8, 512], nslots=2)
    mrow = A.alloc("mrow", F32, [6 * D])
    S.add("sp", lambda e: e.dma_start(out=condT[:, :, :], in_=d_c[:, :, :]), reads=[W1("dram_in")], writes=[W1("condT")], dma=True)
    S.add("act", lambda e: e.activation(out=condT[:, :, :], in_=condT[:, :, :], func=AF.Silu), reads=[W1("condT")], writes=[W1("condT")])
    mps, mpsn = psb[7], "ps7"
    PR = Pool_([0, 1, 2, 3])
    nslab = 0
    for l in range(depth):
        wv = d_wada[l].rearrange("(kc p) n -> p kc n", p=128)
        for s in range(12):
            r = nslab % 2
            nslab += 1
            S.add("sp", lambda e, r=r, s=s, wv=wv: e.dma_start(out=ast[:, r, :, :], in_=wv[:, :, 512 * s:512 * s + 512]),
                  reads=[W1("dram_in")], writes=[W1("ast", r)], dma=True)
            ps, psn = PR.next()

            def f(e, r=r, ps=ps):
                for kc in range(8):
                    ins = e.matmul(ps[0:2, :], lhsT=condT[:, kc, :], rhs=ast[:, r, kc, :], start=(kc == 0), stop=(kc == 7))
                return ins
            S.add("pe", f, reads=[W1("ast", r), W1("condT")], writes=[W1(psn)])
            S.add("act", lambda e, ps=ps, s=s: e.activation(out=mrow[0:2, 512 * s:512 * s + 512], in_=ps[0:2, :], func=AF.Copy),
                  reads=[W1(psn)], writes=[W1("mrow")])

        def g(e, l=l):
            for j in range(48):
                col = (l * 48 + j) * 2
                ins = e.transpose(mps[:, col:col + 2], mrow[0:2, 128 * j:128 * j + 128], cst[0:2, C_ID:C_ID + 2])
            return ins
        S.add("pe", g, reads=[W1("mrow"), W1("cst")], writes=[W1(mpsn)])
    S.add("dve", lambda e: e.tensor_copy(out=modv[:, 0:depth, :, :].rearrange("p l j w -> p (l j w)"), in_=mps[:, 0:depth * 96]),
          reads=[W1(mpsn)], writes=[W1("modv")])
    A.release(mk)

    PA = Pool_([0, 1, 2, 3])
    PB = Pool_([4, 5])
    PC = Pool_([6, 7])

    def tap(name, ap, reads):
        if name in d_taps:
            S.add("sp", lambda e: e.dma_start(out=d_taps[name], in_=ap), reads=reads, writes=[W1("dram_out")], dma=True)

    def norm_stats(xt, xtn_reads, sq, sqn, lnv, lnn, rstd, rsn, w):
        S.add("act", lambda e: e.activation(out=sq[:, :, 0:w], in_=xt[:, :, 0:w], func=AF.Square), reads=xtn_reads, writes=[W1(sqn)])
        ps, psn = PA.next()

        def f(e):
            for c in range(8):
                ins = e.matmul(ps[:, 0:w], lhsT=ones_bf[:, :], rhs=sq[:, c, 0:w], start=(c == 0), stop=(c == 7))
            return ins
        S.add("pe", f, reads=[W1(sqn), W1("ones")], writes=[W1(psn)])
        S.add("act", lambda e: e.activation(out=lnv[:, 0:w], in_=ps[:, 0:w], func=AF.Ln, scale=1.0 / D, bias=EPS), reads=[W1(psn)], writes=[W1(lnn)])
        S.add("act", lambda e: e.activation(out=rstd[:, 0:w], in_=lnv[:, 0:w], func=AF.Exp, scale=-0.5), reads=[W1(lnn)], writes=[W1(rsn)])

    def rmsnorm_mod(l, which, ntiles=5):
        mk = A.mark()
        xt = A.alloc("n_xt", F32, [2, 8, 512], nslots=2)
        sq = A.alloc("n_sq", BF16, [8, 512])
        lnv = A.alloc("n_ln", F32, [512])
        rstd = A.alloc("n_rstd", F32, [512])
        tmp = A.alloc("n_tmp", F32, [2, 512], nslots=2)
        k = 0
        for bi, (t0, w) in enumerate(BTS[:ntiles]):
            wi = 1 if bi == 4 else 0
            xr_ = bi % 2
            S.add("sp", lambda e, xr_=xr_, t0=t0, w=w: e.dma_start(out=xt[:, xr_, :, 0:w], in_=xd_v[:, :, t0:t0 + w]),
                  reads=xall("xd", t0, w), writes=[W1("n_xt", xr_)], dma=True)
            norm_stats(xt[:, xr_, :, :], [W1("n_xt", xr_)], sq, "n_sq", lnv, "n_ln", rstd, "n_rstd", w)
            for c in range(8):
                r = k % 2
                k += 1
                S.add("dve", lambda e, c=c, r=r, xr_=xr_, w=w, wi=wi: e.scalar_tensor_tensor(
                    out=tmp[:, r, 0:w], in0=xt[:, xr_, c, 0:w], scalar=drv[:, which, wi, c:c + 1], in1=rstd[:, 0:w],
                    op0=ALU.mult, op1=ALU.mult),
                    reads=[W1("n_xt", xr_), W1("drv"), W1("n_rstd")], writes=[W1("n_tmp", r)])
                shj = (0 if which == 0 else 3) * 8 + c
                S.add("act", lambda e, c=c, r=r, t0=t0, w=w, wi=wi, shj=shj: e.activation(
                    out=hT[:, c, t0:t0 + w], in_=tmp[:, r, 0:w], func=AF.Identity, bias=modv[:, l, shj, wi:wi + 1]),
                    reads=[W1("n_tmp", r), W1("modv")], writes=[xsl("hT", c, t0, w)])
        A.release(mk)

    def layer_params(l):
        S.add("sp", lambda e: e.dma_start(out=lpar[:, :], in_=d_lp[l, :, :]), reads=[W1("dram_in")], writes=[W1("lpar")], dma=True)
        for wi in range(2):
            S.add("dve", lambda e, wi=wi: e.tensor_tensor(out=modv[:, l, :, wi], in0=modv[:, l, :, wi], in1=lpar[:, P_BADA:P_BADA + 48], op=ALU.add),
                  reads=[W1("modv"), W1("lpar")], writes=[W1("modv")])
        for which in range(2):
            sj = (1 if which == 0 else 4) * 8
            nw0 = P_N1 if which == 0 else P_N2
            for wi in range(2):
                S.add("dve", lambda e, which=which, wi=wi, sj=sj, nw0=nw0: e.scalar_tensor_tensor(
                    out=drv[:, which, wi, :], in0=modv[:, l, sj:sj + 8, wi], scalar=1.0, in1=lpar[:, nw0:nw0 + 8],
                    op0=ALU.add, op1=ALU.mult),
                    reads=[W1("modv"), W1("lpar")], writes=[W1("drv")])

    class ResidPass:
        def __init__(self, l, gate_which, tiles, ring):
            self.l = l
            self.g = gate_which
            self.tiles = tiles
            self.i = 0
            self.il = 0
            self.xr, self.xrn, self.n = ring
            self.LA = self.n - 1

        def _load(self, k):
            dm, t0, w = self.tiles[k]
            r = k % self.n
            xr, xrn = self.xr, self.xrn
            S.add("sp", lambda e: e.dma_start(out=xr[:, r, 0:w], in_=xd_v[:, dm, t0:t0 + w]),
                  reads=[xsl("xd", dm, t0, w)], writes=[W1(xrn, r)], dma=True)

        def prefetch(self):
            while self.il <= min(self.i + self.LA, len(self.tiles) - 1):
                self._load(self.il)
                self.il += 1

        def evac(self, ps, psn):
            self.prefetch()
            dm, t0, w = self.tiles[self.i]
            r = self.i % self.n
            self.i += 1
            gj = self.g * 8 + dm
            wi = 1 if t0 >= NLAT else 0
            l = self.l
            xr, xrn = self.xr, self.xrn
            S.add("dve", lambda e: e.scalar_tensor_tensor(
                out=xr[:, r, 0:w], in0=ps[:, 0:w], scalar=modv[:, l, gj, wi:wi + 1], in1=xr[:, r, 0:w],
                op0=ALU.mult, op1=ALU.add),
                reads=[W1(psn), W1("modv"), W1(xrn, r)], writes=[W1(xrn, r)])
            S.add("sp", lambda e: e.dma_start(out=xd_v[:, dm, t0:t0 + w], in_=xr[:, r, 0:w]),
                  reads=[W1(xrn, r)], writes=[xsl("xd", dm, t0, w)], dma=True)

    def proj_fm(unit, ucol, M, evac, tiles=BTS):
        wap, ws = unit
        for (t0, w) in tiles:
            ps, psn = PA.next()

            def f(e, ps=ps, t0=t0, w=w):
                for kc in range(8):
                    ins = e.matmul(ps[0:M, 0:w], lhsT=wap[:, 256 * kc + ucol:256 * kc + ucol + M], rhs=hT[:, kc, t0:t0 + w],
                                   start=(kc == 0), stop=(kc == 7))
                return ins
            S.add("pe", f, reads=[W1("wbf", ws)] + xall("hT", t0, w), writes=[W1(psn)])
            evac(ps, psn, t0, w)

    def proj_tm(ulist, evac):
        (w0, s0), (w1, s1) = ulist
        for j in range(NT):
            ps, psn = PA.next()

            def f(e, ps=ps, j=j):
                for half, wap in ((0, w0), (1, w1)):
                    for kc in range(8):
                        ins = e.matmul(ps[:, 256 * half:256 * half + 256], lhsT=hT[:, kc, 128 * j:128 * j + 128],
                                       rhs=wap[:, 256 * kc:256 * kc + 256], start=(kc == 0), stop=(kc == 7))
                return ins
            S.add("pe", f, reads=[W1("wbf", s0), W1("wbf", s1)] + xall("hT", 128 * j, 128), writes=[W1(psn)])
            evac(ps, psn, j)

    def rope_proj(kind, umain, uswap, ucol, dsl, slotf):
        mk = A.mark()
        tab = A.alloc("rp_tab", F32, [2, 2, 512], nslots=2)
        t1 = A.alloc("rp_t1", F32, [2, 512], nslots=2)
        t2 = A.alloc("rp_t2", F32, [2, 512], nslots=2)
        for bi, (t0, w) in enumerate(BTS):
            if bi == 4:
                def ev(ps, psn, t0, w):
                    S.add("act", lambda e: e.activation(out=dsl(t0, w), in_=ps[:, 0:w], func=AF.Copy),
                          reads=[W1(psn)], writes=[slotf(t0, w)])
                proj_fm(umain, ucol, 128, ev, tiles=[(t0, w)])
                continue
            r = bi % 2
            S.add("sp", lambda e, r=r, bi=bi: e.dma_start(out=tab[:, r, :, :], in_=d_rope[kind, :, bi, :, :]),
                  reads=[W1("dram_in")], writes=[W1("rp_tab", r)], dma=True)

            def ev1(ps, psn, t0, w, r=r):
                S.add("dve", lambda e: e.tensor_tensor(out=t1[:, r, 0:w], in0=ps[:, 0:w], in1=tab[:, r, 0, 0:w], op=ALU.mult),
                      reads=[W1(psn), W1("rp_tab", r)], writes=[W1("rp_t1", r)])

            def ev2(ps, psn, t0, w, r=r):
                S.add("dve", lambda e: e.tensor_tensor(out=t2[:, r, 0:w], in0=ps[:, 0:w], in1=tab[:, r, 1, 0:w], op=ALU.mult),
                      reads=[W1(psn), W1("rp_tab", r)], writes=[W1("rp_t2", r)])
                S.add("pool", lambda e: e.tensor_tensor(out=dsl(t0, w), in0=t1[:, r, 0:w], in1=t2[:, r, 0:w], op=ALU.add),
                      reads=[W1("rp_t1", r), W1("rp_t2", r)], writes=[slotf(t0, w)])
            proj_fm(umain, ucol, 128, ev1, tiles=[(t0, w)])
            proj_fm(uswap, ucol, 128, ev2, tiles=[(t0, w)])
        A.release(mk)

    def linattn_phase(l, oT):
        mk0 = A.mark()
        qT = A.alloc("la_qT", BF16, [T], nslots=NT)
        kT = A.alloc("la_kT", BF16, [T], nslots=NT)
        FWD = [16, 17] + list(range(16))
        BWD = [17, 16] + list(range(15, -1, -1))
        lat = lambda name, j, n=1: (name, j, j + n)

        def tok_slot(name):
            return lambda t0, w: (name, t0 // 128, (t0 + w + 127) // 128)

        u = WS.get(2)
        rope_proj(0, u[0], u[1], 0, lambda t0, w: qT[:, t0:t0 + w], tok_slot("la_qT"))
        rope_proj(0, u[0], u[1], 128, lambda t0, w: kT[:, t0:t0 + w], tok_slot("la_kT"))

        vg = A.alloc("la_vg", BF16, [NT, 512], nslots=NT)
        ktm = A.alloc("la_ktm", BF16, [8, 128], nslots=8)
        sprev = A.alloc("la_sprev", BF16, [2, NT, 256], nslots=2 * NT)
        sst = A.alloc("la_S", F32, [4, 256], nslots=4)
        um = A.alloc("la_um", F32, [6, 256], nslots=6)
        qx = A.alloc("la_qx", BF16, [2, 4, 128], nslots=2)
        at = A.alloc("la_at", BF16, [6, 4, 128], nslots=6)
        qd = A.alloc("la_qd", BF16, [3, 2, 128], nslots=3)
        osb = A.alloc("la_osb", F32, [3, 256], nslots=3)
        otm = A.alloc("la_otm", BF16, [3, 256], nslots=3)
        sm = A.alloc("la_sm", F32, [3, 32], nslots=3)
        par = A.alloc("la_par", F32, [1408])
        tmpb = A.alloc("la_tmpb", F32, [3, 256], nslots=3)

        def common_chunks(scoreK, scoreQ, masks, interQ, decs, udecs, kdec, normf, ocol):
            nd = len(scoreK)
            nk = 0
            for d in range(2):
                S.add("dve", lambda e, d=d: e.memset(sst[:, 2 * d, :], 0.0), writes=[W1("la_S", 2 * d)])
            ORD = (FWD, BWD)

            def emit_T(step, d):
                j = ORD[d][step]
                src, srcslot, tb = kdec[d]
                tb_ = 2 * d + step % 2
                psn = "ps%d" % tb_
                pst = psb[tb_][:, 0:64].bitcast(BF16)
                psl = W1(psn)
                S.add("pe", lambda e: e.transpose(pst[:, 0:128], src(j), ident[:, :]), reads=[srcslot(j), W1("ident")], writes=[psl])
                kr = 4 * d + step % 4
                if tb is None:
                    S.add("act", lambda e: e.activation(out=ktm[:, kr, :], in_=pst[:, 0:128], func=AF.Copy), reads=[psl], writes=[W1("la_ktm", kr)])
                else:
                    S.add("dve", lambda e: e.tensor_tensor(
                        out=ktm[:, kr, :].rearrange("p (h x) -> p h x", x=32), in0=pst[:, 0:128].rearrange("p (h x) -> p h x", x=32),
                        in1=tb.unsqueeze(2).to_broadcast([128, 4, 32]), op=ALU.mult),
                        reads=[psl, W1("la_par")], writes=[W1("la_ktm", kr)])

            def emit_U(step, d):
                j = ORD[d][step]
                kr = 4 * d + step % 4
                us = 3 * d + step % 3
                ub_ = 4 + 2 * d + step % 2
                ps2 = psb[ub_][:, 0:256]
                psl2 = W1("ps%d" % ub_)
                S.add("pe", lambda e: e.matmul(ps2, lhsT=ktm[:, kr, :], rhs=vg[:, j, 0:256], start=True, stop=True),
                      reads=[W1("la_ktm", kr), lat("la_vg", j)], writes=[psl2])
                if udecs[d] is None:
                    S.add("dve", lambda e: e.tensor_tensor(out=um[:, us, :], in0=ps2, in1=cst[:, C_SDM:C_SDM + 256], op=ALU.mult),
                          reads=[psl2, W1("cst")], writes=[W1("la_um", us)])
                else:
                    uap, urd = udecs[d](j)
                    S.add("dve", lambda e: e.scalar_tensor_tensor(
                        out=um[:, us, :], in0=ps2, scalar=uap, in1=cst[:, C_SDM:C_SDM + 256], op0=ALU.mult, op1=ALU.mult),
                        reads=[psl2, W1("cst"), urd], writes=[W1("la_um", us)])

            def emit_S(step, d):
                j = ORD[d][step]
                us = 3 * d + step % 3
                cur = 2 * d + step % 2
                nxt = 2 * d + (step + 1) % 2
                S.add("act", lambda e: e.activation(out=sprev[:, d, j, :], in_=sst[:, cur, :], func=AF.Copy),
                      reads=[W1("la_S", cur)], writes=[("la_sprev", d * NT + j, d * NT + j + 1)])
                if step + 1 < NT:
                    dap, drd = decs[d](j)
                    S.add("dve", lambda e: e.scalar_tensor_tensor(
                        out=sst[:, nxt, :], in0=sst[:, cur, :], scalar=dap, in1=um[:, us, :], op0=ALU.mult, op1=ALU.add),
                        reads=[W1("la_S", cur), W1("la_um", us), drd], writes=[W1("la_S", nxt)])
            for st0 in range(3):
                for d in range(2):
                    emit_T(st0, d)
            for st0 in range(2):
                for d in range(2):
                    emit_U(st0, d)
            for step in range(NT):
                for d in range(2):
                    if step + 3 < NT:
                        emit_T(step + 3, d)
                for d in range(2):
                    if step + 2 < NT:
                        emit_U(step + 2, d)
                for d in range(2):
                    emit_S(step, d)
            stA = {}

            def stage_A(j):
                r = j % 3
                ats = []
                for d in range(nd):
                    qr = (j * nd + d) % 2
                    S.add("pool", lambda e, d=d, qr=qr: e.tensor_tensor(
                        out=qx[:, qr, :, :], in0=scoreQ[d][0](j).unsqueeze(1).to_broadcast([128, 4, 128]), in1=bm4[:, :, :], op=ALU.mult),
                        reads=[scoreQ[d][1](j), W1("bm4")], writes=[W1("la_qx", qr)])
                    ps, psn = PA.next()
                    S.add("pe", lambda e, ps=ps, d=d, qr=qr: e.matmul(ps[:, :], lhsT=scoreK[d][0](j), rhs=qx[:, qr, :, :].rearrange("p a b -> p (a b)"),
                                                                 start=True, stop=True),
                          reads=[scoreK[d][1](j), W1("la_qx", qr)], writes=[W1(psn)])
                    ar = (j % 3) * 2 + d
                    S.add("dve", lambda e, ps=ps, d=d, ar=ar: e.tensor_tensor(
                        out=at[:, ar, :, :], in0=ps[:, :].rearrange("p (a b) -> p a b", b=128), in1=masks[d], op=ALU.mult),
                        reads=[W1(psn), W1("la_par"), W1("cst")], writes=[W1("la_at", ar)])
                    ats.append(ar)
                iq = []
                for d in range(2):
                    if interQ[d][2] is None:
                        iq.append((interQ[d][0](j), interQ[d][1](j)))
                    else:
                        tb = interQ[d][2]
                        S.add("pool", lambda e, d=d, tb=tb: e.tensor_tensor(out=qd[:, r, d, :], in0=interQ[d][0](j), in1=tb, op=ALU.mult),
                              reads=[interQ[d][1](j), W1("la_par")], writes=[W1("la_qd", r)])
                        iq.append((qd[:, r, d, :], W1("la_qd", r)))
                stA[j] = (ats, iq)

            def stage_B(j):
                r = j % 3
                ats, iq = stA.pop(j)
                ps, psn = PB.next()

                def f(e):
                    e.matmul(ps[:, 0:256], lhsT=iq[0][0], rhs=sprev[:, 0, j, :], start=True, stop=False)
                    ins = e.matmul(ps[:, 0:256], lhsT=iq[1][0], rhs=sprev[:, 1, j, :], start=False, stop=False)
                    n = len(ats) * 4
                    k = 0
                    for ar in ats:
                        for h in range(4):
                            k += 1
                            ins = e.matmul(ps[:, 64 * h:64 * h + 64], lhsT=at[:, ar, h, :], rhs=vg[:, j, 64 * h:64 * h + 64],
                                           start=False, stop=(k == n))
                    return ins
                S.add("pe", f, reads=[iq[0][1], iq[1][1], ("la_sprev", j, j + 1), ("la_sprev", NT + j, NT + j + 1), lat("la_vg", j)]
                      + [W1("la_at", a) for a in ats], writes=[W1(psn)])
                normf[0](ps, psn, j, r)

            stage_A(0)
            stage_A(1)
            for j in range(NT):
                if j + 2 < NT:
                    stage_A(j + 2)
                stage_B(j)
                if j >= 1:
                    normf[1](j - 1, (j - 1) % 3)
                if j >= 2:
                    finish_chunk(j - 2, (j - 2) % 3, ocol)
            normf[1](NT - 1, (NT - 1) % 3)
            finish_chunk(NT - 2, (NT - 2) % 3, ocol)
            finish_chunk(NT - 1, (NT - 1) % 3, ocol)

        def finish_chunk(j, r, ocol):
            for hh in range(2):
                ps, psn = PC.next()
                pst = ps[:, 0:64].bitcast(BF16)
                S.add("pe", lambda e, pst=pst, hh=hh: e.transpose(pst[:, 0:128], otm[:, r, 128 * hh:128 * hh + 128], ident[:, :]),
                      reads=[W1("la_otm", r), W1("ident")], writes=[W1(psn)])
                S.add("act", lambda e, pst=pst, hh=hh: e.activation(out=oT[:, ocol + hh, 128 * j:128 * j + 128], in_=pst[:, 0:128], func=AF.Copy),
                      reads=[W1(psn)], writes=[xsl("oT", ocol + hh, 128 * j, 128)])

        def vg_evac(ps, psn, j):
            S.add("dve", lambda e: e.tensor_copy(out=vg[:, j, 0:256], in_=ps[:, 0:256]), reads=[W1(psn)], writes=[lat("la_vg", j)])
            S.add("act", lambda e: e.activation(out=vg[:, j, 256:512], in_=ps[:, 256:512], func=AF.Silu), reads=[W1(psn), lat("la_vg", j)], writes=[lat("la_vg", j)])

        u = WS.get(2)
        proj_tm(u, vg_evac)
        KS = 32 ** -0.5

        def logsig(out_ap, in_ap):
            S.add("act", lambda e: e.activation(out=out_ap, in_=in_ap, func=AF.Exp, scale=-1.0), reads=[W1("lpar"), W1("la_par")], writes=[W1("la_par")])
            S.add("act", lambda e: e.activation(out=out_ap, in_=out_ap, func=AF.Ln, bias=1.0), reads=[W1("la_par")], writes=[W1("la_par")])
            S.add("dve", lambda e: e.tensor_scalar(out=out_ap, in0=out_ap, scalar1=-1.0, scalar2=None, op0=ALU.mult), reads=[W1("la_par")], writes=[W1("la_par")])
        logsig(par[:, 0:2], lpar[:, P_RDLP:P_RDLP + 2])
        logsig(par[:, 2:10], lpar[:, P_RDLB:P_RDLB + 8])
        S.add("act", lambda e: e.activation(out=par[:, 10:12], in_=par[:, 0:2], func=AF.Exp, scale=128.0), reads=[W1("la_par")], writes=[W1("la_par")])
        for d in range(2):
            S.add("dve", lambda e, d=d: e.tensor_scalar(out=par[:, 16 + 4 * d:20 + 4 * d], in0=par[:, 2 + 4 * d:6 + 4 * d],
                                                       scalar1=cst[:, C_JC + d:C_JC + d + 1], scalar2=None, op0=ALU.mult),
                  reads=[W1("la_par"), W1("cst")], writes=[W1("la_par")])
            S.add("act", lambda e, d=d: e.activation(out=par[:, 16 + 4 * d:20 + 4 * d], in_=par[:, 16 + 4 * d:20 + 4 * d], func=AF.Exp),
                  reads=[W1("la_par")], writes=[W1("la_par")])
            S.add("dve", lambda e, d=d: e.tensor_scalar(out=par[:, 16 + 4 * d:20 + 4 * d], in0=par[:, 16 + 4 * d:20 + 4 * d], scalar1=KS, scalar2=None, op0=ALU.mult),
                  reads=[W1("la_par")], writes=[W1("la_par")])
            io = C_IP1 if d == 0 else C_IREV
            S.add("act", lambda e, d=d, io=io: e.activation(out=par[:, 128 + 128 * d:256 + 128 * d], in_=cst[:, io:io + 128], func=AF.Exp, scale=par[:, d:d + 1]),
                  reads=[W1("la_par"), W1("cst")], writes=[W1("la_par")])
        for h in range(4):
            mo = 512 + 128 * h
            S.add("dve", lambda e, h=h, mo=mo: e.tensor_scalar(out=par[:, mo:mo + 128], in0=cst[:, C_RPOS:C_RPOS + 128], scalar1=par[:, 2 + h:3 + h], scalar2=None, op0=ALU.mult),
                  reads=[W1("la_par"), W1("cst")], writes=[W1("la_par")])
            S.add("dve", lambda e, h=h, mo=mo: e.scalar_tensor_tensor(out=par[:, mo:mo + 128], in0=cst[:, C_RNEG:C_RNEG + 128], scalar=par[:, 6 + h:7 + h], in1=par[:, mo:mo + 128],
                                                                 op0=ALU.mult, op1=ALU.add),
                  reads=[W1("la_par"), W1("cst")], writes=[W1("la_par")])
            S.add("act", lambda e, mo=mo: e.activation(out=par[:, mo:mo + 128], in_=par[:, mo:mo + 128], func=AF.Exp), reads=[W1("la_par")], writes=[W1("la_par")])
            S.add("dve", lambda e, mo=mo: e.tensor_tensor(out=par[:, mo:mo + 128], in0=par[:, mo:mo + 128], in1=cst[:, C_ID:C_ID + 128], op=ALU.add),
                  reads=[W1("la_par"), W1("cst")], writes=[W1("la_par")])
            S.add("dve", lambda e, mo=mo: e.tensor_scalar(out=par[:, mo:mo + 128], in0=par[:, mo:mo + 128], scalar1=KS, scalar2=None, op0=ALU.mult),
                  reads=[W1("la_par")], writes=[W1("la_par")])

        kslice = lambda j: kT[:, 128 * j:128 * j + 128]
        qslice = lambda j: qT[:, 128 * j:128 * j + 128]
        ksl = lambda j: lat("la_kT", j)
        qsl = lambda j: lat("la_qT", j)

        def ret_norm1(ps, psn, j, r):
            S.add("act", lambda e: e.activation(out=osb[:, r, :], in_=ps[:, 0:256], func=AF.Copy), reads=[W1(psn)], writes=[W1("la_osb", r)])
            S.add("act", lambda e: e.activation(out=tmpb[:, r, :], in_=ps[:, 0:256], func=AF.Square), reads=[W1(psn)], writes=[W1("la_tmpb", r)])
            o3 = osb[:, r, :].rearrange("p (h x) -> p h x", x=64)
            S.add("dve", lambda e: e.tensor_reduce(out=sm[:, r, 0:4], in_=o3, axis=AX.X, op=ALU.add), reads=[W1("la_osb", r)], writes=[W1("la_sm", r)])
            S.add("dve", lambda e: e.tensor_reduce(out=sm[:, r, 4:8], in_=tmpb[:, r, :].rearrange("p (h x) -> p h x", x=64), axis=AX.X, op=ALU.add),
                  reads=[W1("la_tmpb", r), W1("la_sm", r)], writes=[W1("la_sm", r)])
            S.add("dve", lambda e: e.tensor_scalar(out=sm[:, r, 8:12], in0=sm[:, r, 0:4], scalar1=1.0 / 64, scalar2=None, op0=ALU.mult), reads=[W1("la_sm", r)], writes=[W1("la_sm", r)])
            S.add("dve", lambda e: e.tensor_tensor(out=sm[:, r, 12:16], in0=sm[:, r, 8:12], in1=sm[:, r, 8:12], op=ALU.mult), reads=[W1("la_sm", r)], writes=[W1("la_sm", r)])
            S.add("dve", lambda e: e.scalar_tensor_tensor(out=sm[:, r, 16:20], in0=sm[:, r, 4:8], scalar=1.0 / 64, in1=sm[:, r, 12:16], op0=ALU.mult, op1=ALU.subtract),
                  reads=[W1("la_sm", r)], writes=[W1("la_sm", r)])
            S.add("act", lambda e: e.activation(out=sm[:, r, 20:24], in_=sm[:, r, 16:20], func=AF.Ln, bias=GN_EPS), reads=[W1("la_sm", r)], writes=[W1("la_sm", r)])
            S.add("act", lambda e: e.activation(out=sm[:, r, 24:28], in_=sm[:, r, 20:24], func=AF.Exp, scale=-0.5), reads=[W1("la_sm", r)], writes=[W1("la_sm", r)])

        def ret_norm3(j, r):
            o3 = osb[:, r, :].rearrange("p (h x) -> p h x", x=64)
            S.add("dve", lambda e: e.tensor_tensor(out=o3, in0=o3, in1=sm[:, r, 8:12].unsqueeze(2).to_broadcast([128, 4, 64]), op=ALU.subtract),
                  reads=[W1("la_osb", r), W1("la_sm", r)], writes=[W1("la_osb", r)])
            S.add("dve", lambda e: e.tensor_tensor(out=o3, in0=o3, in1=sm[:, r, 24:28].unsqueeze(2).to_broadcast([128, 4, 64]), op=ALU.mult),
                  reads=[W1("la_osb", r), W1("la_sm", r)], writes=[W1("la_osb", r)])
            S.add("pool", lambda e: e.tensor_tensor(out=otm[:, r, :], in0=osb[:, r, :], in1=vg[:, j, 256:512], op=ALU.mult),
                  reads=[W1("la_osb", r), lat("la_vg", j)], writes=[W1("la_otm", r)])

        MT = par[:, 512:1024].rearrange("p (a b) -> p a b", b=128)
        PR = W1("la_par")
        common_chunks(
            scoreK=[(kslice, ksl)], scoreQ=[(qslice, qsl)], masks=[MT],
            interQ=[(qslice, qsl, par[:, 128:256]), (qslice, qsl, par[:, 256:384])],
            decs=[lambda j: (par[:, 10:11], PR), lambda j: (par[:, 11:12], PR)], udecs=[None, None],
            kdec=[(kslice, ksl, par[:, 16:20]), (kslice, ksl, par[:, 20:24])],
            normf=(ret_norm1, ret_norm3), ocol=0)
        tap("oret", oT[:, 0:2, :], [("oT", 0, 2 * NT)])

        mk1 = A.mark()
        lr = A.alloc("gl_lr", BF16, [T], nslots=NT)
        gq = A.alloc("gl_q", BF16, [2, T], nslots=2 * NT)
        gk = A.alloc("gl_k", BF16, [2, T], nslots=2 * NT)
        g1 = A.alloc("gl_g1", F32, [2, 128], nslots=2)
        g2 = A.alloc("gl_g2", F32, [2, 128], nslots=2)
        g3 = A.alloc("gl_g3", F32, [2, 128], nslots=2)
        gdec = A.alloc("gl_dec", F32, [2, NT])
        wgb = A.alloc("gl_wg", BF16, [256])
        u = WS.get(2)

        def ev_q(ps, psn, t0, w):
            S.add("act", lambda e: e.activation(out=qT[:, t0:t0 + w], in_=ps[:, 0:w], func=AF.Copy), reads=[W1(psn)], writes=[tok_slot("la_qT")(t0, w)])

        def ev_k(ps, psn, t0, w):
            S.add("act", lambda e: e.activation(out=kT[:, t0:t0 + w], in_=ps[:, 0:w], func=AF.Copy), reads=[W1(psn)], writes=[tok_slot("la_kT")(t0, w)])

        def ev_lr(ps, psn, t0, w):
            S.add("act", lambda e: e.activation(out=lr[0:32, t0:t0 + w], in_=ps[0:32, 0:w], func=AF.Copy), reads=[W1(psn)], writes=[tok_slot("gl_lr")(t0, w)])
        proj_fm(u[0], 0, 128, ev_q)
        proj_fm(u[0], 128, 128, ev_k)
        proj_fm(u[1], 0, 32, ev_lr)
        u = WS.get(2)
        proj_tm(u, vg_evac)
        S.add("dve", lambda e: e.tensor_copy(out=wgb[0:32, :], in_=lpar[0:32, P_WG:P_WG + 256]), reads=[W1("lpar")], writes=[W1("gl_wg")])
        S.add("dve", lambda e: e.tensor_scalar(out=par[:, 1024:1026], in0=lpar[:, P_BG:P_BG + 2], scalar1=-1.0, scalar2=None, op0=ALU.mult),
              reads=[W1("lpar"), W1("la_par")], writes=[W1("la_par")])
        S.add("dve", lambda e: e.tensor_copy(out=par[:, 1100:1356], in_=lpar[:, P_GNW:P_GNW + 256]), reads=[W1("lpar"), W1("la_par")], writes=[W1("la_par")])
        QS = 32 ** -0.5
        ng = 0
        ONE = cst[:, C_ONE:C_ONE + 128]
        for d in range(2):
            for j in range(NT):
                r = ng % 2
                ng += 1
                sl = slice(128 * j, 128 * j + 128)
                ps, psn = PA.next()
                S.add("pe", lambda e, ps=ps, d=d, sl=sl: e.matmul(ps[:, 0:128], lhsT=wgb[0:32, 128 * d:128 * d + 128], rhs=lr[0:32, sl], start=True, stop=True),
                      reads=[W1("gl_wg"), lat("gl_lr", j)], writes=[W1(psn)])
                S.add("act", lambda e, ps=ps, d=d, r=r: e.activation(out=g1[:, r, :], in_=ps[:, 0:128], func=AF.Exp, scale=-1.0, bias=par[:, 1024 + d:1025 + d]),
                      reads=[W1(psn), W1("la_par")], writes=[W1("gl_g1", r)])
                S.add("act", lambda e, r=r: e.activation(out=g1[:, r, :], in_=g1[:, r, :], func=AF.Ln, bias=1.0), reads=[W1("gl_g1", r)], writes=[W1("gl_g1", r)])
                if d == 0:
                    S.add("dve", lambda e, r=r: e.tensor_tensor_scan(out=g2[:, r, :], data0=ONE, data1=g1[:, r, :], initial=0.0, op0=ALU.mult, op1=ALU.add),
                          reads=[W1("cst"), W1("gl_g1", r)], writes=[W1("gl_g2", r)])
                else:
                    S.add("dve", lambda e, r=r: e.tensor_tensor_scan(out=g2[:, r, ::-1], data0=ONE, data1=g1[:, r, ::-1], initial=0.0, op0=ALU.mult, op1=ALU.add),
                          reads=[W1("cst"), W1("gl_g1", r)], writes=[W1("gl_g2", r)])
                S.add("act", lambda e, r=r: e.activation(out=g3[:, r, :], in_=g2[:, r, :], func=AF.Exp, scale=-1.0 / 16), reads=[W1("gl_g2", r)], writes=[W1("gl_g3", r)])
                S.add("act", lambda e, r=r: e.activation(out=g1[:, r, :], in_=g2[:, r, :], func=AF.Exp, scale=1.0 / 16), reads=[W1("gl_g2", r), W1("gl_g1", r)], writes=[W1("gl_g1", r)])
                S.add("dve", lambda e, d=d, r=r, sl=sl: e.scalar_tensor_tensor(out=gq[:, d, sl], in0=qT[:, sl], scalar=QS, in1=g3[:, r, :], op0=ALU.mult, op1=ALU.mult),
                      reads=[lat("la_qT", j), W1("gl_g3", r)], writes=[("gl_q", d * NT + j, d * NT + j + 1)])
                S.add("pool", lambda e, d=d, r=r, sl=sl: e.tensor_tensor(out=gk[:, d, sl], in0=kT[:, sl], in1=g1[:, r, :], op=ALU.mult),
                      reads=[lat("la_kT", j), W1("gl_g1", r)], writes=[("gl_k", d * NT + j, d * NT + j + 1)])
                col = 127 if d == 0 else 0
                S.add("pool", lambda e, d=d, j=j, r=r, col=col: e.tensor_copy(out=gdec[:, d, j:j + 1], in_=g3[:, r, col:col + 1]),
                      reads=[W1("gl_g3", r)], writes=[W1("gl_dec")])

        def gla_norm1(ps, psn, j, r):
            S.add("act", lambda e: e.activation(out=osb[:, r, :], in_=ps[:, 0:256], func=AF.Copy), reads=[W1(psn)], writes=[W1("la_osb", r)])
            S.add("act", lambda e: e.activation(out=tmpb[:, r, :], in_=ps[:, 0:256], func=AF.Square), reads=[W1(psn)], writes=[W1("la_tmpb", r)])
            S.add("dve", lambda e: e.tensor_reduce(out=sm[:, r, 4:8], in_=tmpb[:, r, :].rearrange("p (h x) -> p h x", x=64), axis=AX.X, op=ALU.add),
                  reads=[W1("la_tmpb", r), W1("la_sm", r)], writes=[W1("la_sm", r)])
            S.add("act", lambda e: e.activation(out=sm[:, r, 20:24], in_=sm[:, r, 4:8], func=AF.Ln, scale=1.0 / 64, bias=EPS), reads=[W1("la_sm", r)], writes=[W1("la_sm", r)])
            S.add("act", lambda e: e.activation(out=sm[:, r, 24:28], in_=sm[:, r, 20:24], func=AF.Exp, scale=-0.5), reads=[W1("la_sm", r)], writes=[W1("la_sm", r)])

        def gla_norm3(j, r):
            o3 = osb[:, r, :].rearrange("p (h x) -> p h x", x=64)
            S.add("dve", lambda e: e.tensor_tensor(out=o3, in0=o3, in1=sm[:, r, 24:28].unsqueeze(2).to_broadcast([128, 4, 64]), op=ALU.mult),
                  reads=[W1("la_osb", r), W1("la_sm", r)], writes=[W1("la_osb", r)])
            S.add("pool", lambda e: e.tensor_tensor(out=osb[:, r, :], in0=osb[:, r, :], in1=par[:, 1100:1356], op=ALU.mult),
                  reads=[W1("la_osb", r), W1("la_par")], writes=[W1("la_osb", r)])
            S.add("pool", lambda e: e.tensor_tensor(out=otm[:, r, :], in0=osb[:, r, :], in1=vg[:, j, 256:512], op=ALU.mult),
                  reads=[W1("la_osb", r), lat("la_vg", j)], writes=[W1("la_otm", r)])

        gsl = lambda nm, d: (lambda j: (nm, d * NT + j, d * NT + j + 1))
        LTm = cst[:, C_LT:C_LT + 128].unsqueeze(1).to_broadcast([128, 4, 128])
        UTm = cst[:, C_UT:C_UT + 128].unsqueeze(1).to_broadcast([128, 4, 128])
        gkf = lambda j: gk[:, 0, 128 * j:128 * j + 128]
        gkb = lambda j: gk[:, 1, 128 * j:128 * j + 128]
        gqf = lambda j: gq[:, 0, 128 * j:128 * j + 128]
        gqb = lambda j: gq[:, 1, 128 * j:128 * j + 128]
        GD = W1("gl_dec")
        decf = lambda j: (gdec[:, 0, j:j + 1], GD)
        decb = lambda j: (gdec[:, 1, j:j + 1], GD)
        common_chunks(
            scoreK=[(gkf, gsl("gl_k", 0)), (gkb, gsl("gl_k", 1))],
            scoreQ=[(gqf, gsl("gl_q", 0)), (gqb, gsl("gl_q", 1))],
            masks=[LTm, UTm],
            interQ=[(gqf, gsl("gl_q", 0), None), (gqb, gsl("gl_q", 1), None)],
            decs=[decf, decb], udecs=[decf, decb],
            kdec=[(gkf, gsl("gl_k", 0), None), (gkb, gsl("gl_k", 1), None)],
            normf=(gla_norm1, gla_norm3), ocol=2)
        tap("ogla", oT[:, 2:4, :], [("oT", 2 * NT, 4 * NT)])
        A.release(mk1)
        A.release(mk0)

    def wout_rg(l, oT, units):
        tiles = [(dm, t0, w) for (t0, w) in BTS for dm in range(8)]
        mkx = A.mark()
        xr = A.alloc("xr_rg", F32, [12, 512], nslots=12)
        RP = ResidPass(l, 2, tiles, (xr, "xr_rg", 12))
        for (dm, t0, w) in tiles:
            wap, ws = units[dm // 4]
            ps, psn = PA.next()

            def f(e, ps=ps, wap=wap, dm=dm, t0=t0, w=w):
                for k in range(4):
                    off = 512 * k + 128 * (dm % 4)
                    ins = e.matmul(ps[:, 0:w], lhsT=wap[:, off:off + 128], rhs=oT[:, k, t0:t0 + w], start=(k == 0), stop=(k == 3))
                return ins
            S.add("pe", f, reads=[W1("wbf", ws)] + [xsl("oT", k, t0, w) for k in range(4)], writes=[W1(psn)])
            RP.evac(ps, psn)
        A.release(mkx)

    def diff_phase(l, lam_init, ntiles=5):
        wrg = A.alloc("df_wrg", BF16, [2, UNIT], nslots=2)
        urg = WS.get(2)
        for i in range(2):
            S.add("act", lambda e, i=i: e.activation(out=wrg[:, i, :], in_=urg[i][0], func=AF.Copy), reads=[W1("wbf", urg[i][1])], writes=[W1("df_wrg", i)])
        kT = A.alloc("df_kT", BF16, [4, T], nslots=4 * NT)
        qT = A.alloc("df_qT", BF16, [4, T], nslots=4 * NT)

        def slotc(name, c):
            return lambda t0, w: xsl(name, c, t0, w)
        u = WS.get(4)
        for c in range(4):
            rope_proj(1, u[c // 2], u[2 + c // 2], 128 * (c % 2), (lambda c: (lambda t0, w: kT[:, c, t0:t0 + w]))(c), slotc("df_kT", c))
        u = WS.get(4)
        for c in range(4):
            rope_proj(1, u[c // 2], u[2 + c // 2], 128 * (c % 2), (lambda c: (lambda t0, w: qT[:, c, t0:t0 + w]))(c), slotc("df_qT", c))
        va = A.alloc("df_va", BF16, [NT, 512], nslots=NT)
        qx = A.alloc("df_qx", BF16, [4, 2, 512])
        pt = A.alloc("df_pt", BF16, [4, 512], nslots=4)
        pacc = A.alloc("df_pacc", F32, [2, 512], nslots=2)
        tt = A.alloc("df_t", F32, [2, 512], nslots=2)
        oo = A.alloc("df_o", F32, [2, 512], nslots=2)
        lns = A.alloc("df_lns", F32, [2, 512], nslots=2)
        rs = A.alloc("df_rs", F32, [2, 512], nslots=2)
        sqb = A.alloc("df_sqb", BF16, [2, 512], nslots=2)
        odT = A.alloc("df_odT", BF16, [4, 512], nslots=4)
        tmp = A.alloc("df_tmp", F32, [128])
        lamv = A.alloc("df_lam", F32, [8])
        wod = A.alloc("df_wod", BF16, [2, UNIT], nslots=2)
        xrd = A.alloc("xr_d", F32, [3, 512], nslots=3)
        otl = A.alloc("df_otl", BF16, [4, 512])

        S.add("dve", lambda e: e.tensor_tensor(out=tmp[:, 0:64], in0=lpar[:, P_DLAM:P_DLAM + 64], in1=lpar[:, P_DLAM + 64:P_DLAM + 128], op=ALU.mult),
              reads=[W1("lpar")], writes=[W1("df_tmp")])
        S.add("dve", lambda e: e.tensor_tensor(out=tmp[:, 64:128], in0=lpar[:, P_DLAM + 128:P_DLAM + 192], in1=lpar[:, P_DLAM + 192:P_DLAM + 256], op=ALU.mult),
              reads=[W1("lpar"), W1("df_tmp")], writes=[W1("df_tmp")])
        S.add("dve", lambda e: e.tensor_reduce(out=lamv[:, 0:2], in_=tmp[:, 0:128].rearrange("p (a b) -> p a b", b=64), axis=AX.X, op=ALU.add),
              reads=[W1("df_tmp")], writes=[W1("df_lam")])
        S.add("act", lambda e: e.activation(out=lamv[:, 2:4], in_=lamv[:, 0:2], func=AF.Exp), reads=[W1("df_lam")], writes=[W1("df_lam")])
        S.add("dve", lambda e: e.tensor_tensor(out=lamv[:, 4:5], in0=lamv[:, 3:4], in1=lamv[:, 2:3], op=ALU.subtract), reads=[W1("df_lam")], writes=[W1("df_lam")])
        S.add("dve", lambda e: e.tensor_scalar(out=lamv[:, 4:5], in0=lamv[:, 4:5], scalar1=-float(lam_init), scalar2=None, op0=ALU.add), reads=[W1("df_lam")], writes=[W1("df_lam")])
        S.add("dve", lambda e: e.tensor_scalar(out=lamv[:, 5:6], in0=lpar[:, P_DNWC:P_DNWC + 1], scalar1=float(1.0 - lam_init), scalar2=None, op0=ALU.mult),
              reads=[W1("lpar"), W1("df_lam")], writes=[W1("df_lam")])
        S.add("pool", lambda e: e.memset(qx[:, :, :, :], 0.0), writes=[W1("df_qx")])
        u = WS.get(2)

        def va_evac(ps, psn, j):
            S.add("act", lambda e: e.activation(out=va[:, j, :], in_=ps[:, :], func=AF.Copy), reads=[W1(psn)], writes=[("df_va", j, j + 1)])
        proj_tm(u, va_evac)
        tap("dk", kT[:, :, :], [("df_kT", 0, 4 * NT)])
        tap("dq", qT[:, :, :], [("df_qT", 0, 4 * NT)])
        ud = WS.get(2)
        for i in range(2):
            S.add("act", lambda e, i=i: e.activation(out=wod[:, i, :], in_=ud[i][0], func=AF.Copy), reads=[W1("wbf", ud[i][1])], writes=[W1("df_wod", i)])

        PS_S = Pool_([0, 1, 2])
        PS_ACC = Pool_([4, 5])
        PS_N = Pool_([6, 7])
        ONEF = cst[:, C_ONE:C_ONE + 128]
        npt = 0
        ng = 0
        for bi, (t0, w) in enumerate(BTS[:ntiles]):
            keys = list(range(NT)) if bi < 4 else [16, 17]
            nk = len(keys)
            S.add("sp", lambda e, t0=t0, w=w: e.dma_start(out=otl[:, :, 0:w], in_=d_ot[:, :, t0:t0 + w]), reads=[W1("otd")], writes=[W1("df_otl")], dma=True)
            S.add("pool", lambda e, t0=t0, w=w: e.tensor_copy(out=qx[0:64, :, 0, 0:w], in_=qT[0:64, :, t0:t0 + w]),
                  reads=[xsl("df_qT", h, t0, w) for h in range(4)] + [W1("df_qx")], writes=[W1("df_qx")])
            S.add("pool", lambda e, t0=t0, w=w: e.tensor_copy(out=qx[64:128, :, 1, 0:w], in_=qT[64:128, :, t0:t0 + w]),
                  reads=[xsl("df_qT", h, t0, w) for h in range(4)] + [W1("df_qx")], writes=[W1("df_qx")])
            LA = 2
            stream = []
            for h in range(4):
                for m in range(2):
                    g = ng % 2
                    ng += 1
                    acc, accn = PS_ACC.next()
                    for ki, kt in enumerate(keys):
                        stream.append((h, m, g, acc, accn, ki, kt))
            sps = {}

            def emit_S(idx, w=w):
                h, m, g, acc, accn, ki, kt = stream[idx]
                ps, psn = PS_S.next()
                sps[idx] = (ps, psn)
                S.add("pe", lambda e: e.matmul(ps[:, 0:w], lhsT=kT[:, h, 128 * kt:128 * kt + 128], rhs=qx[:, h, m, 0:w], start=True, stop=True),
                      reads=[xsl("df_kT", h, 128 * kt, 128), W1("df_qx")], writes=[W1(psn)])

            def epilogue(h, m, g, acc, accn, w=w):
                pn, pnn = PS_N.next()
                S.add("pe", lambda e: e.matmul(pn[:, 0:w], lhsT=ONEF, rhs=pacc[:, g, 0:w], start=True, stop=True),
                      reads=[W1("cst"), W1("df_pacc", g)], writes=[W1(pnn)])
                S.add("act", lambda e: e.activation(out=lns[:, g, 0:w], in_=pn[:, 0:w], func=AF.Ln), reads=[W1(pnn)], writes=[W1("df_lns", g)])
                S.add("act", lambda e: e.activation(out=rs[:, g, 0:w], in_=lns[:, g, 0:w], func=AF.Exp, scale=-1.0), reads=[W1("df_lns", g)], writes=[W1("df_rs", g)])
                hr = h % 2
                if m == 0:
                    S.add("dve", lambda e: e.tensor_tensor(out=tt[:, hr, 0:w], in0=acc[:, 0:w], in1=rs[:, g, 0:w], op=ALU.mult),
                          reads=[W1(accn), W1("df_rs", g)], writes=[W1("df_t", hr)])
                    return
                S.add("dve", lambda e: e.scalar_tensor_tensor(
                    out=oo[:, hr, 0:w], in0=rs[:, g, 0:w], scalar=lamv[:, 4:5], in1=acc[:, 0:w], op0=ALU.mult, op1=ALU.mult),
                    reads=[W1(accn), W1("df_rs", g), W1("df_lam")], writes=[W1("df_o", hr)])
                S.add("pool", lambda e: e.tensor_tensor(out=oo[:, hr, 0:w], in0=oo[:, hr, 0:w], in1=tt[:, hr, 0:w], op=ALU.add),
                      reads=[W1("df_o", hr), W1("df_t", hr)], writes=[W1("df_o", hr)])
                S.add("pool", lambda e: e.tensor_tensor(out=sqb[:, hr, 0:w], in0=oo[:, hr, 0:w], in1=oo[:, hr, 0:w], op=ALU.mult),
                      reads=[W1("df_o", hr)], writes=[W1("df_sqb", hr)])
                pn2, pnn2 = PS_N.next()
                S.add("pe", lambda e: e.matmul(pn2[:, 0:w], lhsT=ones_bf[:, :], rhs=sqb[:, hr, 0:w], start=True, stop=True),
                      reads=[W1("ones"), W1("df_sqb", hr)], writes=[W1(pnn2)])
                S.add("act", lambda e: e.activation(out=lns[:, g, 0:w], in_=pn2[:, 0:w], func=AF.Ln, scale=1.0 / 128, bias=EPS),
                      reads=[W1(pnn2), W1("df_lns", g)], writes=[W1("df_lns", g)])
                S.add("act", lambda e: e.activation(out=rs[:, g, 0:w], in_=lns[:, g, 0:w], func=AF.Exp, scale=-0.5),
                      reads=[W1("df_lns", g), W1("df_rs", g)], writes=[W1("df_rs", g)])
                S.add("dve", lambda e: e.scalar_tensor_tensor(
                    out=odT[:, h, 0:w], in0=oo[:, hr, 0:w], scalar=lamv[:, 5:6], in1=rs[:, g, 0:w], op0=ALU.mult, op1=ALU.mult),
                    reads=[W1("df_o", hr), W1("df_rs", g), W1("df_lam")], writes=[W1("df_odT", h)])

            pending = []
            for idx in range(min(LA, len(stream))):
                emit_S(idx)
            for idx, (h, m, g, acc, accn, ki, kt) in enumerate(stream):
                if idx + LA < len(stream):
                    emit_S(idx + LA)
                ps, psn = sps.pop(idx)
                pr = npt % 4
                npt += 1
                S.add("act", lambda e, ps=ps, pr=pr, w=w: e.activation(out=pt[:, pr, 0:w], in_=ps[:, 0:w], func=AF.Exp, scale=0.125),
                      reads=[W1(psn)], writes=[W1("df_pt", pr)])
                S.add("pe", lambda e, acc=acc, pr=pr, kt=kt, h=h, ki=ki, w=w, nk=nk: e.matmul(
                    acc[:, 0:w], lhsT=va[:, kt, 128 * h:128 * h + 128], rhs=pt[:, pr, 0:w], start=(ki == 0), stop=(ki == nk - 1)),
                    reads=[W1("df_pt", pr), ("df_va", kt, kt + 1)], writes=[W1(accn)])
                if ki == 0:
                    S.add("dve", lambda e, g=g, pr=pr, w=w: e.tensor_copy(out=pacc[:, g, 0:w], in_=pt[:, pr, 0:w]),
                          reads=[W1("df_pt", pr)], writes=[W1("df_pacc", g)])
                else:
                    S.add("dve", lambda e, g=g, pr=pr, w=w: e.tensor_tensor(out=pacc[:, g, 0:w], in0=pacc[:, g, 0:w], in1=pt[:, pr, 0:w], op=ALU.add),
                          reads=[W1("df_pt", pr), W1("df_pacc", g)], writes=[W1("df_pacc", g)])
                if pending and ki == min(3, nk - 1):
                    epilogue(*pending.pop(0))
                if ki == nk - 1:
                    pending.append((h, m, g, acc, accn))
            while pending:
                epilogue(*pending.pop(0))
            if "odT" in d_taps:
                S.add("sp", lambda e, t0=t0, w=w: e.dma_start(out=d_taps["odT"][:, :, t0:t0 + w], in_=odT[:, :, 0:w]), reads=[("df_odT", 0, 4)], writes=[W1("dram_out")], dma=True)
            tiles = [(dm, t0, w) for dm in range(8)]
            RP = ResidPass(l, 2, tiles, (xrd, "xr_d", 3))
            for dm in range(8):
                ps, psn = PS_S.next()

                def f(e, ps=ps, dm=dm, w=w):
                    for k in range(4):
                        off = 512 * k + 128 * (dm % 4)
                        e.matmul(ps[:, 0:w], lhsT=wrg[:, dm // 4, off:off + 128], rhs=otl[:, k, 0:w], start=(k == 0), stop=False)
                    for k in range(4):
                        off = 512 * k + 128 * (dm % 4)
                        ins = e.matmul(ps[:, 0:w], lhsT=wod[:, dm // 4, off:off + 128], rhs=odT[:, k, 0:w], start=False, stop=(k == 3))
                    return ins
                S.add("pe", f, reads=[("df_wod", dm // 4, dm // 4 + 1), ("df_wrg", dm // 4, dm // 4 + 1), ("df_odT", 0, 4), W1("df_otl")], writes=[W1(psn)])
                RP.evac(ps, psn)

    def ffn_phase(l, ntiles=5):
        TL = BTS[:ntiles]
        mk = A.mark()
        actT = A.alloc("ff_act", BF16, [12, T], nslots=12 * NT)
        wdn = A.alloc("ff_wdn", BF16, [6, UNIT], nslots=6)
        sg = A.alloc("ff_sg", F32, [2, 512], nslots=2)
        xrf = A.alloc("xr_f", F32, [8, 512], nslots=8)
        k = 0
        for (f0, nf) in FGROUPS:
            for fi in range(nf):
                (wap, ws), = WS.get(1)
                for (t0, w) in TL:
                    pg, pgn = PA.next()
                    pu, pun = PA.next()

                    def f(e, pg=pg, pu=pu, wap=wap, t0=t0, w=w):
                        for kc in range(8):
                            e.matmul(pg[:, 0:w], lhsT=wap[:, 256 * kc:256 * kc + 128], rhs=hT[:, kc, t0:t0 + w], start=(kc == 0), stop=(kc == 7))
                        for kc in range(8):
                            ins = e.matmul(pu[:, 0:w], lhsT=wap[:, 256 * kc + 128:256 * kc + 256], rhs=hT[:, kc, t0:t0 + w], start=(kc == 0), stop=(kc == 7))
                        return ins
                    S.add("pe", f, reads=[W1("wbf", ws)] + xall("hT", t0, w), writes=[W1(pgn), W1(pun)])
                    r = k % 2
                    k += 1
                    S.add("act", lambda e, pg=pg, r=r, w=w: e.activation(out=sg[:, r, 0:w], in_=pg[:, 0:w], func=AF.Silu), reads=[W1(pgn)], writes=[W1("ff_sg", r)])
                    S.add("dve", lambda e, pu=pu, r=r, fi=fi, t0=t0, w=w: e.tensor_tensor(out=actT[:, fi, t0:t0 + w], in0=pu[:, 0:w], in1=sg[:, r, 0:w], op=ALU.mult),
                          reads=[W1(pun), W1("ff_sg", r)], writes=[xsl("ff_act", fi, t0, w)])
            nu = nf // 2
            for i in range(nu):
                (wap, ws), = WS.get(1)
                S.add("act", lambda e, i=i, wap=wap: e.activation(out=wdn[:, i, :], in_=wap, func=AF.Copy), reads=[W1("wbf", ws)], writes=[W1("ff_wdn", i)])
            tiles = [(dm, t0, w) for (t0, w) in TL for dm in range(8)]
            RP = ResidPass(l, 5, tiles, (xrf, "xr_f", 8))
            for (dm, t0, w) in tiles:
                ps, psn = PA.next()

                def f(e, ps=ps, dm=dm, t0=t0, w=w, nf=nf):
                    for fi in range(nf):
                        off = 1024 * (fi % 2) + 128 * dm
                        ins = e.matmul(ps[:, 0:w], lhsT=wdn[:, fi // 2, off:off + 128], rhs=actT[:, fi, t0:t0 + w], start=(fi == 0), stop=(fi == nf - 1))
                    return ins
                S.add("pe", f, reads=[("ff_wdn", 0, nu)] + [xsl("ff_act", fi, t0, w) for fi in range(nf)], writes=[W1(psn)])
                RP.evac(ps, psn)
        A.release(mk)

    for l in range(depth):
        lam_init = 0.8 - 0.6 * math.exp(-0.3 * l)
        layer_params(l)
        rmsnorm_mod(l, 0)
        tap("h%d" % l, hT[:, :, :], [("hT", 0, 8 * NT)])
        mkL = A.mark()
        oT = A.alloc("oT", BF16, [4, T], nslots=4 * NT)
        linattn_phase(l, oT)
        S.add("sp", lambda e: e.dma_start(out=d_ot[:, :, :], in_=oT[:, :, :]), reads=[("oT", 0, 4 * NT)], writes=[W1("otd")], dma=True)
        A.release(mkL)
        mkD = A.mark()
        nt_ = 4 if l == DEPTH - 1 else 5
        diff_phase(l, lam_init, nt_)
        A.release(mkD)
        tap("xattn%d" % l, d_xd, [("xd", 0, 8 * NT)])
        rmsnorm_mod(l, 1, nt_)
        ffn_phase(l, nt_)
        tap("x%d" % l, d_xd, [("xd", 0, 8 * NT)])

    mk = A.mark()
    xt = A.alloc("f_xt", F32, [2, 8, 512], nslots=2)
    sq = A.alloc("f_sq", BF16, [8, 512])
    lnv = A.alloc("f_ln", F32, [512])
    rstd = A.alloc("f_rstd", F32, [512])
    out_v = d_out.rearrange("(c p) t -> p c t", p=128)
    for bi, (t0, w) in enumerate(BTS[:4]):
        r = bi % 2
        S.add("sp", lambda e, r=r, t0=t0, w=w: e.dma_start(out=xt[:, r, :, 0:w], in_=xd_v[:, :, t0:t0 + w]),
              reads=xall("xd", t0, w), writes=[W1("f_xt", r)], dma=True)
        norm_stats(xt[:, r, :, :], [W1("f_xt", r)], sq, "f_sq", lnv, "f_ln", rstd, "f_rstd", w)
        for c in range(8):
            S.add("dve", lambda e, c=c, r=r, w=w: e.scalar_tensor_tensor(out=xt[:, r, c, 0:w], in0=xt[:, r, c, 0:w], scalar=fnw[:, c:c + 1], in1=rstd[:, 0:w],
                                                                  op0=ALU.mult, op1=ALU.mult),
                  reads=[W1("f_xt", r), W1("fnw"), W1("f_rstd")], writes=[W1("f_xt", r)])
        S.add("sp", lambda e, r=r, t0=t0, w=w: e.dma_start(out=out_v[:, :, t0:t0 + w], in_=xt[:, r, :, 0:w]), reads=[W1("f_xt", r)], writes=[W1("dram_out")], dma=True)
    A.release(mk)
    S.add("sp", None, reads=[W1("dram_out")])
    S.emit(nc, es)
    es.close()
    return nc


_CACHE = {}


def _host_inputs(inp):
    wpk = _pack_weights(np.asarray(inp["w_in"], np.float32), np.asarray(inp["w_out"], np.float32),
                        np.asarray(inp["w_ffn_in"], np.float32), np.asarray(inp["w_ffn_out"], np.float32))
    npi = {k: np.asarray(v, np.float32) for k, v in inp.items()}
    lp = _layer_params(npi)
    cst = _const_pack()
    rope = _rope_tables()
    fnw = np.ascontiguousarray(npi["final_norm_w"].reshape(8, 128).T)
    wada = np.ascontiguousarray(npi["w_ada"])
    maps = []
    for b in range(8):
        xin = np.ascontiguousarray(np.concatenate([npi["x"][b].T, npi["ctx"][b].T], axis=1))
        cin = np.stack([npi["c"][b].reshape(8, 128).T, npi["c_ctx"].reshape(8, 128).T], axis=2)
        maps.append({"xin": xin, "cin": np.ascontiguousarray(cin), "wada": wada, "wpk": wpk, "lp": lp,
                     "fnw": fnw, "cst": cst, "rope": rope})
    return maps


def kernel(**inputs):
    maps = _host_inputs(inputs)
    if "nc" not in _CACHE:
        _CACHE["nc"] = build()
    res = run_bass_kernel_spmd(_CACHE["nc"], maps, core_ids=list(range(8)))
    out = np.stack([np.ascontiguousarray(r["out"].T) for r in res.results], axis=0)
    return out.astype(np.float32)
```
